# Optimizing a Trainium2 kernel written in Bass

```python
import jax, jax.numpy as jnp
from jax import lax
import numpy as np

D_MODEL = 1024
BATCH = 4
SEQ = 4096
DEPTH = 2
DEC_BATCH = 8
DEC_SEQ = 16
PAST_LEN = 2048

CHUNK = 64
N_A_LAYERS = DEPTH // 2
N_B_LAYERS = DEPTH - N_A_LAYERS
D_RNN = D_MODEL
N_RG_BLOCKS = 4
RG_BLOCK = D_RNN // N_RG_BLOCKS
CONV_WIDTH = 4
RG_C = 8.0
N_HEADS = 16
HEAD_DIM = 64
N_KV_HEADS = 2
GROUP = N_HEADS // N_KV_HEADS
WINDOW = 128
WIN_CHUNKS = WINDOW // CHUNK
EPS = 1e-6

kernel_name = "yoco_rglru_swa_sink_stream_step"


def rms_norm(x, g):
    xf = x.astype(jnp.float32)
    y = xf * lax.rsqrt(jnp.mean(xf * xf, axis=-1, keepdims=True) + EPS) * g.astype(jnp.float32)
    return y.astype(x.dtype)


def alibi_slopes():
    return 2.0 ** (-8.0 * jnp.arange(1, N_HEADS + 1, dtype=jnp.float32) / N_HEADS)


def causal_conv(x, hist, w, b):
    T = x.shape[1]
    xe = jnp.concatenate([hist.astype(x.dtype), x], axis=1)
    y = sum(xe[:, k:k + T] * w[k] for k in range(CONV_WIDTH)) + b
    return y, xe[:, -(CONV_WIDTH - 1):]


def rglru(x, h0, wa, ba, wx, bx, lam):
    B, T, C = x.shape
    xf = x.astype(jnp.float32)
    xb = xf.reshape(B, T, N_RG_BLOCKS, RG_BLOCK)
    gr = jnp.einsum('btnc,ncd->btnd', xb, wa.astype(jnp.float32)).reshape(B, T, C) + ba.astype(jnp.float32)
    gi = jnp.einsum('btnc,ncd->btnd', xb, wx.astype(jnp.float32)).reshape(B, T, C) + bx.astype(jnp.float32)
    r = jax.nn.sigmoid(gr)
    i = jax.nn.sigmoid(gi)
    log_a = RG_C * r * jax.nn.log_sigmoid(lam.astype(jnp.float32))
    a = jnp.exp(log_a)
    b = jnp.sqrt(-jnp.expm1(2.0 * log_a)) * (i * xf)
    b = b.at[:, 0].add(a[:, 0] * h0.astype(jnp.float32))

    def comb(lhs, rhs):
        a1, b1 = lhs
        a2, b2 = rhs
        return a1 * a2, a2 * b1 + b2

    _, h = lax.associative_scan(comb, (a, b), axis=1)
    return h.astype(x.dtype), h[:, -1].astype(x.dtype)


def a_layer(x, conv_hist, h0, norm_g, w_in, conv_w, conv_b, wa, ba, wx, bx, lam, w_out):
    u = rms_norm(x, norm_g) @ w_in
    xb, gate = u[..., :D_RNN], u[..., D_RNN:]
    xc, conv_new = causal_conv(xb, conv_hist, conv_w, conv_b)
    h, h_last = rglru(xc, h0, wa, ba, wx, bx, lam)
    y = (h * jax.nn.silu(gate)) @ w_out
    return x + y, conv_new, h_last


def shared_kv(x, kv_norm, w_kv, k_norm):
    B, T, _ = x.shape
    u = rms_norm(x, kv_norm) @ w_kv
    kw = N_KV_HEADS * HEAD_DIM
    k = rms_norm(u[..., :kw].reshape(B, T, N_KV_HEADS, HEAD_DIM), k_norm)
    v = u[..., kw:].reshape(B, T, N_KV_HEADS, HEAD_DIM)
    return k, v


def b_queries(x, norm_g, w_in, q_norm):
    B, T, _ = x.shape
    u = rms_norm(x, norm_g) @ w_in
    qw = N_HEADS * HEAD_DIM
    q = rms_norm(u[..., :qw].reshape(B, T, N_HEADS, HEAD_DIM), q_norm) * (HEAD_DIM ** -0.5)
    return q, u[..., qw:]


def sink_softmax(s, sink):
    m = jnp.maximum(jnp.max(s, axis=-1, keepdims=True), sink)
    p = jnp.exp(s - m)
    return p / (jnp.sum(p, axis=-1, keepdims=True) + jnp.exp(sink - m))


def swa_prompt(q, k, v, sinks):
    B, T, _, _ = q.shape
    NC = T // CHUNK
    KB = (WIN_CHUNKS + 1) * CHUNK
    qc = q.reshape(B, NC, CHUNK, N_KV_HEADS, GROUP, HEAD_DIM)
    pad = ((0, 0), (WIN_CHUNKS, 0), (0, 0), (0, 0), (0, 0))
    kp = jnp.pad(k.reshape(B, NC, CHUNK, N_KV_HEADS, HEAD_DIM), pad)
    vp = jnp.pad(v.reshape(B, NC, CHUNK, N_KV_HEADS, HEAD_DIM), pad)
    kband = jnp.concatenate([kp[:, j:j + NC] for j in range(WIN_CHUNKS + 1)], axis=2)
    vband = jnp.concatenate([vp[:, j:j + NC] for j in range(WIN_CHUNKS + 1)], axis=2)
    s = jnp.einsum('bnqkgd,bnskd->bnkgqs', qc, kband, preferred_element_type=jnp.float32)
    qi = jnp.arange(CHUNK)[:, None]
    sj = jnp.arange(KB)[None, :] - WIN_CHUNKS * CHUNK
    dist = jnp.abs(qi - sj).astype(jnp.float32)
    slopes = alibi_slopes().reshape(N_KV_HEADS, GROUP)[:, :, None, None]
    s = s - slopes * dist
    key_chunk = jnp.arange(NC)[:, None] - WIN_CHUNKS + jnp.arange(KB)[None, :] // CHUNK
    valid = (key_chunk >= 0)[None, :, None, None, None, :]
    s = jnp.where(valid, s, -jnp.inf)
    p = sink_softmax(s, sinks.astype(jnp.float32).reshape(N_KV_HEADS, GROUP)[:, :, None, None])
    o = jnp.einsum('bnkgqs,bnskd->bnqkgd', p, vband.astype(jnp.float32))
    return o.reshape(B, T, N_HEADS * HEAD_DIM).astype(q.dtype)


def swa_sample(q, k_all, v_all, past_len, sinks):
    B, T, _, _ = q.shape
    S = k_all.shape[1]
    qg = q.reshape(B, T, N_KV_HEADS, GROUP, HEAD_DIM)
    s = jnp.einsum('btkgd,bskd->bkgts', qg, k_all, preferred_element_type=jnp.float32)
    dist = jnp.abs(jnp.arange(T)[:, None] + past_len - jnp.arange(S)[None, :]).astype(jnp.float32)
    slopes = alibi_slopes().reshape(N_KV_HEADS, GROUP)[:, :, None, None]
    s = s - slopes * dist
    p = sink_softmax(s, sinks.astype(jnp.float32).reshape(N_KV_HEADS, GROUP)[:, :, None, None])
    o = jnp.einsum('bkgts,bskd->btkgd', p, v_all.astype(jnp.float32))
    return o.reshape(B, T, N_HEADS * HEAD_DIM).astype(q.dtype)


def run_group(x, conv_state, h_state, past_k, past_v, a_p, kv_p, b_p):
    conv_out, h_out = [], []
    k = v = k_buf = v_buf = None
    for layer in range(DEPTH):
        if layer < N_A_LAYERS:
            i = layer
            x, c_new, h_last = a_layer(x, conv_state[i], h_state[i], *[p[i] for p in a_p])
            conv_out.append(c_new)
            h_out.append(h_last)
            if layer == N_A_LAYERS - 1:
                k_new, v_new = shared_kv(x, *kv_p)
                if past_k is None:
                    k, v = k_new, v_new
                    k_buf, v_buf = k_new[:, -WINDOW:], v_new[:, -WINDOW:]
                else:
                    L = past_k.shape[1]
                    k = jnp.concatenate([past_k.astype(x.dtype), k_new], axis=1)
                    v = jnp.concatenate([past_v.astype(x.dtype), v_new], axis=1)
                    k_buf, v_buf = k[:, -L:], v[:, -L:]
        else:
            j = layer - N_A_LAYERS
            b_norm, b_w_in, q_norm, sinks, b_w_out = [p[j] for p in b_p]
            q, gate = b_queries(x, b_norm, b_w_in, q_norm)
            if past_k is None:
                o = swa_prompt(q, k, v, sinks)
            else:
                o = swa_sample(q, k, v, past_k.shape[1], sinks)
            x = x + (o * jax.nn.silu(gate)) @ b_w_out
    return x, k_buf, v_buf, jnp.stack(conv_out, axis=0), jnp.stack(h_out, axis=0)


def setup_inputs(seed: int = 0) -> dict:
    key = jax.random.key(seed)
    ks = iter(jax.random.split(key, 32))
    nrm = lambda shape, scale: jax.random.normal(next(ks), shape, jnp.float32) * scale
    f32 = jnp.float32
    kv_len = min(WINDOW, PAST_LEN)
    u = jax.random.uniform(next(ks), (N_A_LAYERS, D_RNN), f32, 0.9, 0.999)
    a_base = u ** (1.0 / RG_C)
    a_lambda = jnp.log(a_base) - jnp.log1p(-a_base)
    return {
        "x_prompt": nrm((BATCH, SEQ, D_MODEL), 1.0),
        "x_sample": nrm((DEC_BATCH, DEC_SEQ, D_MODEL), 1.0),
        "cache_k": nrm((DEC_BATCH, kv_len, N_KV_HEADS, HEAD_DIM), 1.0),
        "cache_v": nrm((DEC_BATCH, kv_len, N_KV_HEADS, HEAD_DIM), 1.0),
        "state_conv": nrm((N_A_LAYERS, DEC_BATCH, CONV_WIDTH - 1, D_RNN), 1.0),
        "state_rglru": nrm((N_A_LAYERS, DEC_BATCH, D_RNN), 0.5),
        "a_norm": 1.0 + nrm((N_A_LAYERS, D_MODEL), 0.02),
        "a_w_in": nrm((N_A_LAYERS, D_MODEL, 2 * D_RNN), D_MODEL ** -0.5),
        "a_conv_w": nrm((N_A_LAYERS, CONV_WIDTH, D_RNN), CONV_WIDTH ** -0.5),
        "a_conv_b": nrm((N_A_LAYERS, D_RNN), 0.02),
        "a_gate_a_w": nrm((N_A_LAYERS, N_RG_BLOCKS, RG_BLOCK, RG_BLOCK), RG_BLOCK ** -0.5),
        "a_gate_a_b": nrm((N_A_LAYERS, D_RNN), 0.02),
        "a_gate_x_w": nrm((N_A_LAYERS, N_RG_BLOCKS, RG_BLOCK, RG_BLOCK), RG_BLOCK ** -0.5),
        "a_gate_x_b": nrm((N_A_LAYERS, D_RNN), 0.02),
        "a_lambda": a_lambda,
        "a_w_out": nrm((N_A_LAYERS, D_RNN, D_MODEL), D_RNN ** -0.5),
        "kv_norm": 1.0 + nrm((D_MODEL,), 0.02),
        "w_kv": nrm((D_MODEL, 2 * N_KV_HEADS * HEAD_DIM), D_MODEL ** -0.5),
        "k_norm": 1.0 + nrm((HEAD_DIM,), 0.02),
        "b_norm": 1.0 + nrm((N_B_LAYERS, D_MODEL), 0.02),
        "b_w_in": nrm((N_B_LAYERS, D_MODEL, 2 * N_HEADS * HEAD_DIM), D_MODEL ** -0.5),
        "q_norm": 1.0 + nrm((N_B_LAYERS, HEAD_DIM), 0.02),
        "sinks": nrm((N_B_LAYERS, N_HEADS), 0.5),
        "b_w_out": nrm((N_B_LAYERS, N_HEADS * HEAD_DIM, D_MODEL), (N_HEADS * HEAD_DIM) ** -0.5),
    }


def reference(x_prompt, x_sample, cache_k, cache_v, state_conv, state_rglru,
              a_norm, a_w_in, a_conv_w, a_conv_b, a_gate_a_w, a_gate_a_b, a_gate_x_w, a_gate_x_b,
              a_lambda, a_w_out, kv_norm, w_kv, k_norm, b_norm, b_w_in, q_norm, sinks, b_w_out):
    a_p = (a_norm, a_w_in, a_conv_w, a_conv_b, a_gate_a_w, a_gate_a_b, a_gate_x_w, a_gate_x_b, a_lambda, a_w_out)
    kv_p = (kv_norm, w_kv, k_norm)
    b_p = (b_norm, b_w_in, q_norm, sinks, b_w_out)
    Bp = x_prompt.shape[0]
    p_conv0 = jnp.zeros((N_A_LAYERS, Bp, CONV_WIDTH - 1, D_RNN), x_prompt.dtype)
    p_h0 = jnp.zeros((N_A_LAYERS, Bp, D_RNN), x_prompt.dtype)
    y_prompt, p_k, p_v, p_conv, p_h = run_group(x_prompt, p_conv0, p_h0, None, None, a_p, kv_p, b_p)
    y_sample, s_k, s_v, s_conv, s_h = run_group(x_sample, state_conv, state_rglru, cache_k, cache_v, a_p, kv_p, b_p)
    return (y_prompt, y_sample, p_k, p_v, p_conv, p_h, s_k, s_v, s_conv, s_h)
```

```python
import numpy as np
from contextlib import ExitStack
import concourse.bass as bass
import concourse.mybir as mybir
from concourse.bass_utils import run_bass_kernel_spmd

F32 = mybir.dt.float32
BF16 = mybir.dt.bfloat16
AF = mybir.ActivationFunctionType
ALU = mybir.AluOpType
AX = mybir.AxisListType

D = 1024
NCH = 8
SEQ = 4096
NS = 16
EPS = 1e-6
N_HEADS = 16
HD = 64


_DBG_STOP = None
_NLITE = 3


class _Stop(Exception):
    pass


def _ck(tag):
    if _DBG_STOP is not None and tag == _DBG_STOP:
        raise _Stop()


class Buf:
    __slots__ = ("name", "w", "r", "parts")

    def __init__(self, name, parts=None):
        self.name = name
        self.w = None
        self.r = {}
        self.parts = parts


def _expand(bufs):
    out = []
    for b in bufs:
        if b.parts is not None:
            out.extend(b.parts)
        else:
            out.append(b)
    return out


class _Proxy:
    def __init__(self, eng):
        self._e = eng
        self.sz = 128
        self.name = None

    def __getattr__(self, name):
        f = getattr(self._e, name)

        def w(*a, **kw):
            out = kw.get("out", a[0] if a else None)
            try:
                m = 1
                for d in out.shape[1:]:
                    m *= d
                self.sz = m
            except Exception:
                pass
            self.name = name
            return f(*a, **kw)
        return w


class SyncMgr:
    NSLOT = 8

    def __init__(self, nc, es):
        self.nc = nc
        self.eng = {"pe": nc.tensor, "act": nc.scalar, "dve": nc.vector, "pool": nc.gpsimd, "sp": nc.sync}
        self.sem = {k: es.enter_context(nc.semaphore("s_" + k)) for k in self.eng}
        self.cnt = {k: 0 for k in self.eng}
        self.waited = {k: {} for k in self.eng}
        self.dsem = {q: [es.enter_context(nc.semaphore("d_%s_%d" % (q, i))) for i in range(self.NSLOT)]
                     for q in ("sp", "pool", "act")}
        self.dcnt = {"sp": 0, "pool": 0, "act": 0}
        self.ninst = 0
        self.efree = {k: 0.0 for k in self.eng}
        self.tfin = {}
        self.step_fin = 0.0

    def _est(self, e, reads, writes, dur):
        reads, writes = _expand(reads), _expand(writes)
        t0 = self.efree[e]
        for b in reads:
            if b.w is not None:
                t0 = max(t0, self.tfin.get(b.w[0:1] + (b.w[2],), 0.0))
        for b in writes:
            if b.w is not None:
                t0 = max(t0, self.tfin.get(b.w[0:1] + (b.w[2],), 0.0))
            for r in b.r.values():
                t0 = max(t0, self.tfin.get(r[0:1] + (r[2],), 0.0))
        fin = t0 + dur
        self.efree[e] = fin if e != "sp" else t0 + 100.0
        self.step_fin = max(self.step_fin, fin)
        return fin

    def _waits(self, e, reads, writes):
        reads, writes = _expand(reads), _expand(writes)
        need = {}
        deps = []
        for b in reads:
            if b.w is not None:
                deps.append(b.w)
        for b in writes:
            if b.w is not None:
                deps.append(b.w)
            deps.extend(b.r.values())
        for key, sem, val in deps:
            if key == ("e", "pe") and e == "pe":
                continue
            if key not in need or need[key][1] < val:
                need[key] = (sem, val)
        for key, (sem, val) in need.items():
            if self.waited[e].get(key, -1) >= val:
                continue
            self.eng[e].wait_ge(sem, val)
            self.waited[e][key] = val

    def _record(self, tok, reads, writes):
        reads, writes = _expand(reads), _expand(writes)
        key = tok[0]
        for b in writes:
            b.w = tok
            b.r = {}
        for b in reads:
            if b in writes:
                continue
            old = b.r.get(key)
            if old is None or old[2] < tok[2]:
                b.r[key] = tok

    def op(self, e, reads, writes, emit, inc=True, sz=None, k=None):
        self._waits(e, reads, writes)
        px = _Proxy(self.eng[e])
        ins = emit(px)
        self.ninst += 1
        if sz is None:
            sz = px.sz
        if k is None:
            k = {"reciprocal": 2.1, "tensor_tensor_scan": 2.0, "tensor_scalar": 0.6, "tensor_copy": 0.6}.get(px.name, 1.0)
        if e == "pe":
            dur = max(62.0, sz / 2.4 + 6.0)
        elif e == "act":
            dur = 220.0 + 0.83 * sz
        elif e == "dve":
            dur = 165.0 + 1.04 * sz * k
        else:
            dur = 150.0 + 2.4 * sz
        fin = self._est(e, reads, writes, dur)
        if inc:
            ins.then_inc(self.sem[e], 1)
            self.cnt[e] += 1
            tok = (("e", e), self.sem[e], self.cnt[e])
        else:
            assert e == "pe"
            tok = (("e", e), self.sem[e], self.cnt[e] + 1)
        self.tfin[(tok[0], tok[2])] = max(fin, self.tfin.get((tok[0], tok[2]), 0.0))
        self._record(tok, reads, writes)

    def dma(self, q, out, in_, reads, writes, **kw):
        fin = self._est(q, reads, writes, 2500.0)
        self._waits(q, reads, writes)
        slot = self.dcnt[q] % self.NSLOT
        use = self.dcnt[q] // self.NSLOT
        key = ("d", q, slot)
        sem = self.dsem[q][slot]
        if use > 0 and self.waited[q].get(key, -1) < 16 * use:
            self.eng[q].wait_ge(sem, 16 * use)
            self.waited[q][key] = 16 * use
        self.eng[q].dma_start(out=out, in_=in_, **kw).then_inc(sem, 16)
        self.ninst += 1
        self.dcnt[q] += 1
        tok = (key, sem, 16 * (use + 1))
        self.tfin[(tok[0], tok[2])] = fin
        self._record(tok, reads, writes)

    def finish(self):
        for q in ("sp", "pool", "act"):
            for slot in range(self.NSLOT):
                uses = (self.dcnt[q] - slot + self.NSLOT - 1) // self.NSLOT
                if uses > 0:
                    key = ("d", q, slot)
                    if self.waited[q].get(key, -1) < 16 * uses:
                        self.eng[q].wait_ge(self.dsem[q][slot], 16 * uses)
                        self.waited[q][key] = 16 * uses


def build(seq=SEQ):
    assert seq % 128 == 0
    assert seq % 256 == 0
    NTH = seq // 256
    NPRE = NTH - 1
    NPRE_A = max(NPRE, 1)
    nc = bass.Bass("TRN2", target_bir_lowering=False)
    es = ExitStack()

    def din(name, shape):
        return nc.dram_tensor(name, list(shape), F32, kind="ExternalInput").ap()

    def dout(name, shape):
        return nc.dram_tensor(name, list(shape), F32, kind="ExternalOutput").ap()

    xpre = din("xpre", [NPRE_A * 128, D]); xp = din("xp", [(NTH + 1) * 128, D]); xs = din("xs", [NS, D])
    flag = din("flag", [1])
    ck = din("ck", [128, 128]); cv = din("cv", [128, 128])
    pvec = din("pvec", [120, 128])
    w_in_a = din("w_in_a", [D, 2 * D]); wga = din("wga", [D, 256]); wgx = din("wgx", [D, 256])
    w_out_a = din("w_out_a", [D, D]); w_kv = din("w_kv", [D, 256])
    w_in_b = din("w_in_b", [D, 2 * D]); w_out_b = din("w_out_b", [D, D])
    knorm = din("knorm", [HD]); qnorm = din("qnorm", [HD]); sinks = din("sinks", [N_HEADS])
    ident = din("ident", [128, 128]); bones = din("bones", [128, 128])
    distA = din("distA", [128, 128]); distB = din("distB", [128, 128])
    maskA = din("maskA", [128, 128]); maskB = din("maskB", [128, 128])
    kmask = din("kmask", [128, 3])

    yp = dout("yp", [NTH * 128, D]); ys = dout("ys", [NS, D])
    pk = dout("pk", [128, 128]); pv = dout("pv", [128, 128])
    pconv = dout("pconv", [3, D]); ph = dout("ph", [D])
    sk = dout("sk", [128, 128]); sv = dout("sv", [128, 128])
    sconv = dout("sconv", [3, D]); sh = dout("sh", [D])

    S = SyncMgr(nc, es)

    def sb(name, shape, dt=F32):
        t = es.enter_context(nc.sbuf_tensor(name, list(shape), dt))
        return t, Buf(name)

    def ps(name, shape, dt=F32):
        t = es.enter_context(nc.psum_tensor(name, list(shape), dt))
        return t, Buf(name)

    WinA, bWinA = sb("WinA", [128, NCH, 2 * D], BF16)
    WGA, bWGA = sb("WGA", [128, NCH, 256], BF16)
    WGX, bWGX = sb("WGX", [128, NCH, 256], BF16)
    WoutA, bWoutA = sb("WoutA", [128, NCH, D], BF16)
    Wkv, bWkv = sb("Wkv", [128, NCH, 256], BF16)
    WinB, bWinB = sb("WinB", [128, NCH, 2 * D], BF16)
    WoutB, bWoutB = sb("WoutB", [128, NCH, D], BF16)
    IDF, bIDF = sb("IDF", [128, 128], F32)
    IDB, bIDB = sb("IDB", [128, 128], BF16)
    BON, bBON = sb("BON", [128, 128], BF16)
    EAB, bEAB = sb("EAB", [128, 2, N_HEADS, 128], BF16)
    bEA = bEB = bEAB
    PVT, bPVT = sb("PVT", [128, 120], F32)
    C8, bC8 = sb("C8", [128, NCH], F32)
    CH, bCH = sb("CH", [128, NCH], F32)
    BAH, bBAH = sb("BAH", [128, NCH], F32)
    BXH, bBXH = sb("BXH", [128, NCH], F32)
    GK, bGK = sb("GK", [128, HD], F32)
    GQ8, bGQ8 = sb("GQ8", [128, HD], F32)
    MNEG, bMNEG = sb("MNEG", [128, 1], F32)
    SINKE, bSINKE = sb("SINKE", [128, N_HEADS], F32)
    CONSTS, bCONSTS = sb("CONSTS", [128, 4], F32)
    SMALL, bSMALL = sb("SMALL", [128, 64], F32)
    SMA, bSMA = sb("SMA", [128, 8], F32)
    SMB, bSMB = sb("SMB", [128, 8], F32)
    SMK, bSMK = sb("SMK", [128, 8], F32)
    SMO = [sb("SMO%d" % i, [128, 8], F32) for i in range(2)]
    KM, bKM = sb("KM", [128, 3], F32)
    CST = [sb("CST%d" % i, [128, NCH, 4], F32) for i in range(2)]
    CTMP, bCTMP = sb("CTMP", [128, 3, 128], F32)

    EPS_T, bEPS = sb("EPS_T", [128, 1], F32)
    TA = [sb("TA%d" % i, [128, D], F32) for i in range(6)]
    TB = [sb("TB%d" % i, [128, D], F32) for i in range(4)]
    BA_ = [sb("BA%d" % i, [128, D], BF16) for i in range(4)]
    BB_ = [sb("BB%d" % i, [128, D], BF16) for i in range(6)]
    for t_ in (TA[2], TB[1]):
        t_[1].parts = [Buf(t_[1].name + "_lo"), Buf(t_[1].name + "_hi")]
    X1 = [sb("X1_%d" % i, [128, D], F32) for i in range(3)]
    XBS = [sb("XB%d" % i, [128, NCH, 3 + 128], F32) for i in range(2)]
    HC, bHC = sb("HC", [128, NCH], F32)
    K2, bK2 = sb("K2", [128, 2, 2, 2, 128], BF16)
    VA, bVA = sb("VA", [128, 2, 2, 65], BF16)
    K2S, bK2S = sb("K2S", [128, 2, 2, 2, 128], BF16)
    VAS, bVAS = sb("VAS", [128, 2, 2, 65], BF16)
    KOUT, bKOUT = sb("KOUT", [128, 128], F32)
    VOUT, bVOUT = sb("VOUT", [128, 128], F32)
    KD, bKD = sb("KD", [128, 2, 2, HD], BF16)
    bWinA2 = Buf("WinA_gate")
    bK2s = [Buf("K2_0"), Buf("K2_1")]
    bVAs = [Buf("VA_0"), Buf("VA_1")]
    bK2Ss = [Buf("K2S_0"), Buf("K2S_1")]
    bVASs = [Buf("VAS_0"), Buf("VAS_1")]

    PT0, bPT0 = ps("PT0", [128, D], BF16)
    PA, bPA = ps("PA", [128, D], F32)
    PB, bPB = ps("PB", [128, D], F32)
    PC, bPC = ps("PC", [128, D], F32)
    PD, bPD = ps("PD", [128, 512], F32)
    bPC0, bPC1 = Buf("PC0"), Buf("PC1")

    op = S.op
    T = TA

    def v3(t, n=128):
        return t[:, :].rearrange("p (c t) -> p c t", c=NCH)[:, :, 0:n]

    S.dma("sp", IDF[:, :], ident, [], [bIDF])
    S.dma("sp", T[0][0][:, 0:128], bones, [], [T[0][1]])
    S.dma("sp", T[1][0][:120, 0:128], pvec, [], [T[1][1]])
    S.dma("sp", GK[:, :], knorm.partition_broadcast(128), [], [bGK])
    S.dma("sp", GQ8[:, :], qnorm.partition_broadcast(128), [], [bGQ8])
    S.dma("sp", SINKE[:, :], sinks.partition_broadcast(128), [], [bSINKE])
    S.dma("sp", T[2][0][:, 0:128], distA, [], [T[2][1]])
    S.dma("sp", T[2][0][:, 128:256], distB, [], [T[2][1]])
    S.dma("sp", T[2][0][:, 256:384], maskA, [], [T[2][1]])
    S.dma("sp", T[2][0][:, 384:512], maskB, [], [T[2][1]])
    S.dma("sp", KM[:, :], kmask, [], [bKM])
    S.dma("sp", KM[:, 2:3], flag.partition_broadcast(128), [], [bKM])

    op("dve", [bIDF], [bIDB], lambda e: e.tensor_copy(IDB[:, :], IDF[:, :]))
    op("dve", [T[0][1]], [bBON], lambda e: e.tensor_copy(BON[:, :], T[0][0][:, 0:128]))
    op("dve", [], [bCONSTS], lambda e: e.memset(CONSTS[:, 0:1], 0.5))
    op("dve", [], [bCONSTS], lambda e: e.memset(CONSTS[:, 1:2], -0.5))
    op("dve", [], [bCONSTS], lambda e: e.memset(CONSTS[:, 2:3], 0.25))
    op("dve", [], [bEPS], lambda e: e.memset(EPS_T[:, 0:1], EPS))
    op("pe", [T[1][1], bIDF], [bPD],
       lambda e: e.transpose(PD[:, 0:120], T[1][0][:120, 0:128], IDF[:120, :120]))
    op("dve", [bPD], [bPVT], lambda e: e.tensor_copy(PVT[:, :], PD[:, 0:120]))
    op("dve", [bPVT], [bBAH], lambda e: e.tensor_scalar(BAH[:, :], PVT[:, 40:48], 0.5, None, ALU.mult))
    op("dve", [bPVT], [bBXH], lambda e: e.tensor_scalar(BXH[:, :], PVT[:, 48:56], 0.5, None, ALU.mult))
    sm = SMALL
    LAM = PVT[:, 56:64]
    op("act", [bPVT], [bSMALL], lambda e: e.activation(sm[:, 0:8], LAM, AF.Abs))
    op("act", [bSMALL], [bSMALL], lambda e: e.activation(sm[:, 8:16], sm[:, 0:8], AF.Exp, scale=-1.0))
    op("dve", [bSMALL], [bSMALL], lambda e: e.tensor_scalar(sm[:, 16:24], sm[:, 8:16], 2.0, None, ALU.add))
    op("dve", [bSMALL], [bSMALL], lambda e: e.reciprocal(sm[:, 16:24], sm[:, 16:24]))
    op("dve", [bSMALL], [bSMALL], lambda e: e.tensor_tensor(sm[:, 24:32], sm[:, 8:16], sm[:, 16:24], ALU.mult))
    op("dve", [bSMALL], [bSMALL], lambda e: e.tensor_tensor(sm[:, 32:40], sm[:, 24:32], sm[:, 24:32], ALU.mult))
    op("dve", [], [bSMALL], lambda e: e.memset(sm[:, 40:48], 1.0 / 19.0))
    for kk in (17, 15, 13, 11, 9, 7, 5, 3, 1):
        op("dve", [bSMALL], [bSMALL], lambda e: e.tensor_tensor(sm[:, 40:48], sm[:, 40:48], sm[:, 32:40], ALU.mult))
        op("dve", [bSMALL], [bSMALL],
           lambda e, kk=kk: e.tensor_scalar(sm[:, 40:48], sm[:, 40:48], 1.0 / kk, None, ALU.add))
    op("dve", [bSMALL], [bSMALL], lambda e: e.tensor_tensor(sm[:, 40:48], sm[:, 40:48], sm[:, 24:32], ALU.mult))
    op("dve", [bPVT, bSMALL], [bSMALL],
       lambda e: e.tensor_scalar(sm[:, 48:56], LAM, -1.0, 0.0, ALU.mult, ALU.max))
    op("dve", [bSMALL], [bSMALL],
       lambda e: e.scalar_tensor_tensor(sm[:, 48:56], sm[:, 40:48], 2.0, sm[:, 48:56], ALU.mult, ALU.add))
    op("dve", [bSMALL], [bC8], lambda e: e.tensor_scalar(C8[:, :], sm[:, 48:56], -8.0, None, ALU.mult))
    op("dve", [bSMALL], [bCH], lambda e: e.tensor_scalar(CH[:, :], sm[:, 48:56], -4.0, None, ALU.mult))

    op("dve", [bGK, bGQ8], [bSMALL], lambda e: e.tensor_tensor(sm[:, 0:64], GK[:, :], GQ8[:, :], ALU.mult))
    op("act", [bSMALL], [bSMALL], lambda e: e.activation(sm[:, 0:64], sm[:, 0:64], AF.Abs))
    op("dve", [bSMALL], [bMNEG], lambda e: e.tensor_reduce(MNEG[:, 0:1], sm[:, 0:64], AX.X, ALU.max))
    op("dve", [bMNEG], [bMNEG], lambda e: e.tensor_scalar(MNEG[:, 0:1], MNEG[:, 0:1], -8.0, None, ALU.mult))
    op("dve", [bGQ8], [bGQ8], lambda e: e.tensor_scalar(GQ8[:, :], GQ8[:, :], 0.125, None, ALU.mult))
    op("act", [bSINKE, bMNEG], [bSINKE],
       lambda e: e.activation(SINKE[:, :], SINKE[:, :], AF.Exp, bias=MNEG[:, 0:1]))

    for (E_, bE_, dcol, mcol, Tt) in ((EAB[:, 0], bEA, 0, 256, T[3]), (EAB[:, 1], bEB, 128, 384, T[4])):
        for h0 in range(0, N_HEADS, 8):
            for h in range(h0, h0 + 8):
                slope = float(2.0 ** (-8.0 * (h + 1) / N_HEADS))
                op("act", [T[2][1]], [Tt[1]],
                   lambda e, h=h, h0=h0, slope=slope, Tt=Tt, dcol=dcol: e.activation(
                       v3(Tt[0])[:, h - h0, :], T[2][0][:, dcol:dcol + 128], AF.Exp, scale=-slope))
            op("dve", [T[2][1], Tt[1]], [bE_],
               lambda e, E_=E_, h0=h0, Tt=Tt, mcol=mcol: e.tensor_tensor(
                   E_[:, h0:h0 + 8, :], v3(Tt[0]),
                   T[2][0][:, mcol:mcol + 128].unsqueeze(1).to_broadcast([128, 8, 128]), ALU.mult))

    stg_i = [0]
    STG = [TB[0], TB[1], TB[2], TB[3], X1[0], X1[1], X1[2]]
    QS = ["sp", "act"]

    def load_weight(dst, bdst, src_ap, ncols_total, gain_col=None):
        for c0 in range(0, ncols_total, 1024):
            ww = min(ncols_total, c0 + 1024) - c0
            tt = STG[stg_i[0] % len(STG)]
            q = QS[stg_i[0] % len(QS)]
            stg_i[0] += 1
            S.dma(q, tt[0][:, 0:ww], src_ap[:, c0:c0 + ww], [], [tt[1]])
            if gain_col is None:
                op("dve", [tt[1]], [bdst],
                   lambda e, tt=tt, ww=ww, c0=c0: e.tensor_copy(dst[:, c0:c0 + ww], tt[0][:, 0:ww]))
            else:
                op("dve", [tt[1], bPVT], [bdst],
                   lambda e, tt=tt, ww=ww, c0=c0: e.tensor_scalar(
                       dst[:, c0:c0 + ww], tt[0][:, 0:ww], PVT[:, gain_col:gain_col + 1], None, ALU.mult))

    for kc in range(NCH):
        load_weight(WinA[:, kc, 0:D], bWinA, w_in_a[kc * 128:(kc + 1) * 128, 0:D], D, gain_col=96 + kc)
    for kc in range(NCH):
        load_weight(WGA[:, kc, :], bWGA, wga[kc * 128:(kc + 1) * 128, :], 256)
        load_weight(WGX[:, kc, :], bWGX, wgx[kc * 128:(kc + 1) * 128, :], 256)

    def deferred_loads():
        stg, bstg = TA[1]
        jobs = []
        for kc in range(NCH):
            jobs.append((WinA[:, kc, D:2 * D], bWinA2, w_in_a[kc * 128:(kc + 1) * 128, D:2 * D], D, 96 + kc))
        for kc in range(NCH):
            jobs.append((WoutA[:, kc, :], bWoutA, w_out_a[kc * 128:(kc + 1) * 128, :], D, None))
        for kc in range(NCH):
            jobs.append((Wkv[:, kc, :], bWkv, w_kv[kc * 128:(kc + 1) * 128, :], 256, 104 + kc))
        for kc in range(NCH):
            pass
        for kc in range(NCH):
            jobs.append((WoutB[:, kc, :], bWoutB, w_out_b[kc * 128:(kc + 1) * 128, :], D, None))
        for (dst, bdst, src, w, gcol) in jobs:
            S.dma("sp", stg[:, 0:w], src, [], [bstg])
            if gcol is None:
                op("dve", [bstg], [bdst], lambda e, dst=dst, w=w: e.tensor_copy(dst, stg[:, 0:w]))
            else:
                op("dve", [bstg, bPVT], [bdst],
                   lambda e, dst=dst, w=w, gcol=gcol: e.tensor_scalar(dst, stg[:, 0:w], PVT[:, gcol:gcol + 1], None, ALU.mult))
            yield

    def deferred_winb():
        k_ = 0
        for kc in range(NCH):
            for c0 in (0, D):
                stg, bstg = TB[2 + k_ % 2]
                k_ += 1
                S.dma("sp", stg[:, 0:D], w_in_b[kc * 128:(kc + 1) * 128, c0:c0 + D], [], [bstg])
                op("dve", [bstg, bPVT], [bWinB],
                   lambda e, kc=kc, c0=c0, stg=stg: e.tensor_scalar(WinB[:, kc, c0:c0 + D], stg[:, 0:D],
                                                                    PVT[:, 112 + kc:113 + kc], None, ALU.mult))
                yield

    op("dve", [], [XBS[0][1]], lambda e: e.memset(XBS[0][0][:, :, 0:3], 0.0))
    op("dve", [], [XBS[1][1]], lambda e: e.memset(XBS[1][0][:, :, 0:3], 0.0))
    op("dve", [], [bHC], lambda e: e.memset(HC[:, :], 0.0))
    op("dve", [], [bK2s[0], bK2s[1]], lambda e: e.memset(K2[:, :, :, :, :].rearrange("p a b c d -> p (a b c d)"), 0.0))
    op("dve", [], [bK2Ss[0], bK2Ss[1]], lambda e: e.memset(K2S[:, :, :, :, :].rearrange("p a b c d -> p (a b c d)"), 0.0))
    op("dve", [], [bVAs[1]], lambda e: e.memset(VA[:, 1, :, :], 0.0))

    n = 128

    def rms_to_T(Xsrc, bXsrc, SM_, bSM, XNb, XTb, on_act=False):
        XN, bXN = XNb
        XT_, bXT = XTb
        op("act", [bXsrc], [bXN, bSM],
           lambda e: e.activation(XN[:, :], Xsrc[:, :], AF.Square, accum_out=SM_[:, 0:1]))
        op("dve", [bSM], [bSM],
           lambda e: e.tensor_scalar(SM_[:, 1:2], SM_[:, 0:1], 1.0 / D, EPS, ALU.mult, ALU.add))
        op("pool", [bSM, bCONSTS], [bSM],
           lambda e: e.tensor_tensor(SM_[:, 2:3], SM_[:, 1:2], CONSTS[:, 1:2], ALU.pow))
        if on_act:
            op("act", [bXsrc, bSM], [bXN],
               lambda e: e.activation(XN[:, :], Xsrc[:, :], AF.Identity, scale=SM_[:, 2:3]))
        else:
            op("dve", [bXsrc, bSM], [bXN],
               lambda e: e.tensor_scalar(XN[:, :], Xsrc[:, :], SM_[:, 2:3], None, ALU.mult))
        yield
        for kc in range(NCH):
            op("pe", [bXN, bIDB], [bPT0],
               lambda e, kc=kc: e.transpose(PT0[:, kc * 128:(kc + 1) * 128], XN[:, kc * 128:(kc + 1) * 128], IDB[:, :]),
               inc=(kc == NCH - 1))
        op("act", [bPT0], [bXT], lambda e: e.activation(XT_[:, :], PT0[:, :], AF.Copy))

    def k_to_k2(Ksrc, bKsrc, K2t, bK2slot, slot):
        op("dve", [bKsrc, bGQ8], [bKD],
           lambda e: e.tensor_tensor(
               KD[:, :, :, :],
               Ksrc[:, :].rearrange("p (k d) -> p k d", k=2).unsqueeze(2).to_broadcast([n, 2, 2, HD]),
               GQ8[:, :].unsqueeze(1).unsqueeze(1).to_broadcast([n, 2, 2, HD]), ALU.mult))
        for k in range(2):
            op("pe", [bKD, bIDB], [bPT0],
               lambda e, k=k: e.transpose(PT0[:, k * 128:(k + 1) * 128],
                                          KD[:, k, :, :].rearrange("p a d -> p (a d)"), IDB[:, :]),
               inc=(k == 1))
        for par in range(2):
            op("act", [bPT0], [bK2slot],
               lambda e, par=par: e.activation(
                   K2t[par * 64:(par + 1) * 64, slot, :, par, :],
                   PT0[par * 64:(par + 1) * 64, 0:256].rearrange("p (k t) -> p k t", k=2), AF.Copy))

    BS0 = dict(T0=TA[0], SG=TA[1], XC=TA[2], TI=TA[3], LA=TA[4], A2=TA[5], XN=BA_[0], XT=BA_[1], XCb=BA_[2], HG=BA_[3],
               SM=(SMA, bSMA), X=X1[2], P0=(PA, bPA), P1=(PB, bPB), XB=XBS[0], XBn=XBS[1])
    BS1 = dict(T0=TB[0], SG=None, XC=TB[1], TI=TB[2], LA=TB[3], A2=X1[0], XN=BB_[0], XT=BB_[1], XCb=BB_[2], HG=None,
               SM=(SMB, bSMB), X=X1[1], P0=(PC, bPC), P1=(PC, bPC), XB=XBS[1], XBn=XBS[0])
    R32 = WinB[:, :, :].rearrange("p a b -> p (a b)").bitcast(F32)
    def r32(i, name):
        return (R32[:, i * D:(i + 1) * D], Buf(name))
    R_ = [r32(i, "R32_%d" % i) for i in range(6)]
    XB3 = (R32[:, 6 * D:6 * D + NCH * 131].rearrange("p (c t) -> p c t", c=NCH), Buf("XB3"))
    R_[1][1].parts = [Buf("R32_1_lo"), Buf("R32_1_hi")]
    bWinB.parts = [b for (_, b) in R_[:1]] + R_[1][1].parts + [b for (_, b) in R_[2:]] + [XB3[1], Buf("WinB_rest")]
    BS2 = dict(T0=R_[0], SG=None, XC=R_[1], TI=R_[2], LA=R_[3], A2=R_[4], XN=BB_[3], XT=BB_[4], XCb=BB_[5], HG=None,
               SM=(SMK, bSMK), X=R_[5], P0=(PB, bPB), P1=(PB, bPB), XB=XB3, XBn=XBS[0])
    BS0P = dict(BS0, P1=(PA, bPA))
    XBS3 = [XBS[0], XBS[1], XB3]
    op("dve", [], [XB3[1]], lambda e: e.memset(XB3[0][:, :, 0:3], 0.0))

    EVENTS = {}

    def phaseA(tp):
        nv = tp["nv"]
        bs = tp["bs"]
        lite, mask = tp["lite"], tp["mask"]
        Xt, bXt = tp.get("Xbuf") or bs["X"]
        XBt, bXB = tp.get("XB") or bs["XB"]
        P0, bP0 = bs["P0"]
        P1, bP1 = bs["P1"]
        SM_, bSM = bs["SM"]
        def load_x(tq):
            Xq, bXq = tq.get("Xbuf") or tq["bs"]["X"]
            if tq["nv"] < n:
                op("dve", [], [bXq], lambda e: e.memset(Xq[:, :], 0.0))
            S.dma("sp", Xq[:tq["nv"], :], tq["x_src"], [], [bXq])
            tq["preloaded"] = True
        if not tp.get("preloaded"):
            load_x(tp)
        nxt = tp.get("prefetch")
        if nxt is not None and nxt.get("Xbuf") is not None:
            load_x(nxt)
        if tp["hist_init"] is not None:
            tp["hist_init"](XBt, bXB)
        yield
        XT_, bXT = bs["XT"]
        g_ = rms_to_T(Xt, bXt, SM_, bSM, bs["XN"], bs["XT"])
        next(g_)
        if nxt is not None and nxt.get("Xbuf") is None and lite:
            load_x(nxt)
        yield
        for _ in g_:
            yield
        XT3 = v3(XT_)
        yield
        for oc in range(NCH):
            for kc in range(NCH):
                op("pe", [bXT, bWinA], [bP0],
                   lambda e, oc=oc, kc=kc: e.matmul(v3(P0)[:, oc, :], WinA[:, kc, oc * 128:(oc + 1) * 128], XT3[:, kc, :],
                                                    start=(kc == 0), stop=(kc == NCH - 1)),
                   inc=(oc == NCH - 1 and kc == NCH - 1))
            if oc % 2 == 1 and oc != NCH - 1:
                yield
        op("act", [bP0], [bXB], lambda e: e.activation(XBt[:, :, 3:3 + n], v3(P0), AF.Copy))
        XBn, bXBn = tp.get("XBn") or bs["XBn"]
        while tp.get("dep2") is not None and not EVENTS.get(("conv", tp["dep2"])):
            yield "blocked"
        if XBn is not XBt:
            op("act", [bXB], [bXBn], lambda e: e.activation(XBn[:, :, 0:3], XBt[:, :, n:n + 3], AF.Copy))
            EVENTS[("hist", tp["idx"])] = True
        yield
        while tp["dep"] is not None and not EVENTS.get(("hist", tp["dep"])):
            yield "blocked"
        if not lite:
            for oc in range(NCH):
                for kc in range(NCH):
                    op("pe", [bXT, bWinA2], [bP1],
                       lambda e, oc=oc, kc=kc: e.matmul(v3(P1)[:, oc, :], WinA[:, kc, D + oc * 128:D + (oc + 1) * 128],
                                                        XT3[:, kc, :], start=(kc == 0), stop=(kc == NCH - 1)),
                       inc=(oc == NCH - 1 and kc == NCH - 1))
                if oc % 2 == 1 and oc != NCH - 1:
                    yield
            TG, bTG = bs["T0"]
            SG, bSG = bs["SG"]
            op("act", [bP1], [bTG], lambda e: e.activation(TG[:, :], P1[:, :], AF.Tanh, scale=0.5))
            op("dve", [bTG, bP1], [bSG],
               lambda e: e.scalar_tensor_tensor(SG[:, :], TG[:, :], 1.0, P1[:, :], ALU.add, ALU.mult))
            yield
        XC, bXC = bs["XC"]
        XC3 = v3(XC)
        NDC = 5
        bXCl, bXCh = bXC.parts
        for c in range(NDC):
            op("dve", [bXB, bPVT], [bXCl],
               lambda e, c=c: e.tensor_scalar(XC3[:, c, :], XBt[:, c, 0:n], PVT[:, c:c + 1], PVT[:, 32 + c:33 + c],
                                              ALU.mult, ALU.add))
        yield

        def wbc(col):
            return PVT[:, col + NDC:col + NCH].unsqueeze(2).to_broadcast([128, NCH - NDC, n])
        op("pool", [bXB, bPVT], [bXCh],
           lambda e: e.tensor_tensor(XC3[:, NDC:NCH, :], XBt[:, NDC:NCH, 0:n], wbc(0), ALU.mult))
        op("pool", [bPVT, bXCh], [bXCh],
           lambda e: e.tensor_tensor(XC3[:, NDC:NCH, :], XC3[:, NDC:NCH, :], wbc(32), ALU.add))
        yield
        for k in range(1, 4):
            for c in range(NDC):
                op("dve", [bXB, bPVT, bXCl], [bXCl],
                   lambda e, c=c, k=k: e.scalar_tensor_tensor(
                       XC3[:, c, :], XBt[:, c, k:k + n], PVT[:, k * 8 + c:k * 8 + c + 1], XC3[:, c, :], ALU.mult, ALU.add))
                if c == 2:
                    yield
            op("pool", [bXB, bPVT], [bCTMP],
               lambda e, k=k: e.tensor_tensor(CTMP[:, :, :], XBt[:, NDC:NCH, k:k + n], wbc(k * 8), ALU.mult))
            op("pool", [bCTMP, bXCh], [bXCh],
               lambda e: e.tensor_tensor(XC3[:, NDC:NCH, :], XC3[:, NDC:NCH, :], CTMP[:, :, :], ALU.add))
            yield
        EVENTS[("conv", tp["idx"])] = True
        so = tp["state_out"]
        if so is not None:
            CSt, bCS = so["stage"]
            op("act", [bXB], [bCS], lambda e: e.activation(CSt[:, :, 0:3], XBt[:, :, nv:nv + 3], AF.Copy))
            for j in range(3):
                S.dma("sp", so["conv"][j].rearrange("(c p) -> p c", p=128), CSt[:, :, j],
                      [bCS], [], allow_slow_non_contiguous=True)
        if XBn is XBt:
            op("act", [bXB], [bXB], lambda e: e.activation(XBt[:, :, 0:3], XBt[:, :, n:n + 3], AF.Copy))
        XCb, bXCb = bs["XCb"]
        op("act", [bXC], [bXCb], lambda e: e.activation(XCb[:, :], XC[:, :], AF.Copy))
        XCb3 = v3(XCb)
        yield
        TR, bTR = bs["T0"]
        TI, bTI = bs["TI"]
        LA, bLA = bs["LA"]
        A2, bA2 = bs["A2"]
        for (P_, bP, Wg, bWg, TT, bTT, BH, bBH) in ((P0, bP0, WGA, bWGA, TR, bTR, BAH, bBAH),
                                                   (P1, bP1, WGX, bWGX, TI, bTI, BXH, bBXH)):
            for blk in range(4):
                for oc in range(2):
                    for kc in range(2):
                        last = (blk == 3 and oc == 1 and kc == 1)
                        op("pe", [bXCb, bWg], [bP],
                           lambda e, P_=P_, Wg=Wg, blk=blk, oc=oc, kc=kc: e.matmul(
                               v3(P_)[:, blk * 2 + oc, :], Wg[:, blk * 2 + kc, oc * 128:(oc + 1) * 128],
                               XCb3[:, blk * 2 + kc, :], start=(kc == 0), stop=(kc == 1)),
                           inc=last)
                if blk == 1:
                    yield
            yield
            for c in range(NCH):
                op("act", [bP, bBH], [bTT],
                   lambda e, c=c, TT=TT, P_=P_, BH=BH: e.activation(v3(TT)[:, c, :], v3(P_)[:, c, :], AF.Tanh,
                                                                    bias=BH[:, c:c + 1], scale=0.5))
                if c == 3:
                    yield
            yield
        op("dve", [bTR, bCH], [bLA],
           lambda e: e.scalar_tensor_tensor(v3(LA), v3(TR), 1.0, CH[:, :].unsqueeze(2).to_broadcast([128, NCH, n]),
                                            ALU.add, ALU.mult))
        yield
        op("act", [bLA], [bA2], lambda e: e.activation(A2[:, :], LA[:, :], AF.Exp, scale=2.0))
        yield
        op("act", [bLA], [bLA], lambda e: e.activation(LA[:, :], LA[:, :], AF.Exp))
        yield
        op("act", [bA2, bCONSTS], [bA2],
           lambda e: e.activation(A2[:, :], A2[:, :], AF.Sqrt, bias=CONSTS[:, 2:3], scale=-0.25))
        yield
        op("dve", [bTI, bXC], [bTI],
           lambda e: e.scalar_tensor_tensor(TI[:, :], TI[:, :], 1.0, XC[:, :], ALU.add, ALU.mult))
        yield
        if mask:
            op("dve", [bA2, bTI, bKM], [bTI],
               lambda e: e.scalar_tensor_tensor(TI[:, :], A2[:, :], KM[:, 2:3], TI[:, :], ALU.mult, ALU.mult))
        else:
            op("pool", [bA2, bTI], [bTI], lambda e: e.tensor_tensor(TI[:, :], A2[:, :], TI[:, :], ALU.mult))
        yield
        while tp["dep"] is not None and not EVENTS.get(("carry", tp["dep"])):
            yield "blocked"
        Hp_fn, bHp = tp["Hprev"]
        Hv = v3(TI)
        for c in range(NCH):
            op("dve", [bLA, bTI, bHp], [bTI],
               lambda e, c=c: e.tensor_tensor_scan(Hv[:, c, :], v3(LA)[:, c, :], Hv[:, c, :],
                                                   Hp_fn(c), ALU.mult, ALU.add))
            if c % 2 == 1:
                yield
        if so is not None:
            CSt, bCS = so["stage"]
            op("act", [bTI], [bCS], lambda e: e.activation(CSt[:, :, 3:4], Hv[:, :, nv - 1:nv], AF.Copy))
            S.dma("sp", so["h"].rearrange("(c p) -> p c", p=128), CSt[:, :, 3],
                  [bCS], [], allow_slow_non_contiguous=True)
        if tp["carry"]:
            op("dve", [bTI], [bHC], lambda e: e.tensor_copy(HC[:, :], Hv[:, :, n - 1]))
        EVENTS[("carry", tp["idx"])] = True
        if lite:
            return
        X1c, bX1c = tp["X1"]
        HG, bHG = bs["HG"]
        SG, bSG = bs["SG"]
        op("dve", [bTI, bSG], [bHG],
           lambda e: e.scalar_tensor_tensor(v3(HG), Hv, 0.5, v3(SG), ALU.mult, ALU.mult))
        HG3 = v3(HG)
        yield
        for half in range(2):
            for kc in range(NCH):
                op("pe", [bHG, bWoutA], [bP0],
                   lambda e, half=half, kc=kc: e.matmul(P0[:, half * 512:(half + 1) * 512], HG3[:, kc, :],
                                                        WoutA[:, kc, half * 512:(half + 1) * 512],
                                                        start=(kc == 0), stop=(kc == NCH - 1)),
                   inc=(half == 1 and kc == NCH - 1))
            yield
        op("dve", [bXt, bP0], [bX1c], lambda e: e.tensor_tensor(X1c[:, :], Xt[:, :], P0[:, :], ALU.add))
        yield

    def phaseB(tp):
        nv = tp["nv"]
        X1c, bX1c = tp["X1"]
        K2t, bK2l, VAt, bVAl = tp["K2"], tp["bK2"], tp["VA"], tp["bVA"]
        cur, prev, kmcol = tp["cur"], tp["prev"], tp["kmcol"]
        so = tp["state_out"]
        yield from rms_to_T(X1c, bX1c, SMB, bSMB, BB_[0], BB_[1], on_act=True)
        X1T3 = v3(BB_[1][0])
        bX1T = BB_[1][1]
        yield
        for kc in range(NCH):
            op("pe", [bX1T, bWkv], [bPD],
               lambda e, kc=kc: e.matmul(PD[:, 0:256], X1T3[:, kc, :], Wkv[:, kc, :], start=(kc == 0), stop=(kc == NCH - 1)),
               inc=(kc == NCH - 1))
        Ksq, bKsq = TB[0]
        op("dve", [bPD], [bKsq], lambda e: e.tensor_copy(Ksq[:, 0:256], PD[:, 0:256]))
        op("dve", [bKsq], [bKsq], lambda e: e.tensor_tensor(Ksq[:, 256:384], Ksq[:, 0:128], Ksq[:, 0:128], ALU.mult))
        op("dve", [bKsq], [bSMK],
           lambda e: e.tensor_reduce(SMK[:, 0:2], Ksq[:, 256:384].rearrange("p (k d) -> p k d", k=2), AX.X, ALU.add))
        op("dve", [bSMK], [bSMK],
           lambda e: e.tensor_scalar(SMK[:, 0:2], SMK[:, 0:2], 1.0 / HD, EPS, ALU.mult, ALU.add))
        op("pool", [bSMK, bCONSTS], [bSMK],
           lambda e: e.tensor_tensor(SMK[:, 2:4], SMK[:, 0:2], CONSTS[:, 1:2].to_broadcast([n, 2]), ALU.pow))
        op("dve", [bKsq, bSMK], [bKsq],
           lambda e: e.tensor_tensor(Ksq[:, 384:512].rearrange("p (k d) -> p k d", k=2),
                                     Ksq[:, 0:128].rearrange("p (k d) -> p k d", k=2),
                                     SMK[:, 2:4].unsqueeze(2).to_broadcast([n, 2, HD]), ALU.mult))
        op("dve", [bKsq, bGK], [bKOUT],
           lambda e: e.tensor_tensor(KOUT[:, :].rearrange("p (k d) -> p k d", k=2),
                                     Ksq[:, 384:512].rearrange("p (k d) -> p k d", k=2),
                                     GK[:, :].unsqueeze(1).to_broadcast([n, 2, HD]), ALU.mult))
        yield

        def k_finish():
            k_to_k2(KOUT, bKOUT, K2t, bK2l[cur], cur)
            op("dve", [bKsq, bKM], [bVAl[cur]],
               lambda e: e.tensor_scalar(VAt[:, cur, :, 0:HD], Ksq[:, 128:256].rearrange("p (k d) -> p k d", k=2),
                                         KM[:, kmcol:kmcol + 1], None, ALU.mult))
            op("dve", [bKM], [bVAl[cur]],
               lambda e: e.tensor_copy(VAt[:, cur, :, HD:HD + 1], KM[:, kmcol:kmcol + 1].unsqueeze(1).to_broadcast([n, 2, 1])))
            if so is not None:
                op("dve", [bKsq], [bVOUT], lambda e: e.tensor_copy(VOUT[:, :], Ksq[:, 128:256]))
                S.dma("sp", so["k"], KOUT[:nv, :], [bKOUT], [])
                S.dma("sp", so["v"], VOUT[:nv, :], [bVOUT], [])
        if tp["kvonly"]:
            k_finish()
            yield
            return
        for oc in range(NCH):
            for kc in range(NCH):
                op("pe", [bX1T, bWinB], [bPC, bPC0, bPC1],
                   lambda e, oc=oc, kc=kc: e.matmul(v3(PC)[:, oc, :], WinB[:, kc, oc * 128:(oc + 1) * 128], X1T3[:, kc, :],
                                                    start=(kc == 0), stop=(kc == NCH - 1)),
                   inc=(oc == NCH - 1 and kc == NCH - 1))
            if oc % 2 == 1:
                yield
        SQ, bSQ = BB_[0]
        op("act", [bPC], [bSQ], lambda e: e.activation(SQ[:, :], PC[:, :], AF.Square))
        yield
        k_finish()
        yield
        RQ, bRQ = TB[0]
        for c0 in range(0, NCH, 4):
            op("pe", [bSQ, bBON], [bPD],
               lambda e, c0=c0: e.matmul(PD[:, :], BON[:, :], SQ[:, c0 * 128:(c0 + 4) * 128], start=True, stop=True))
            op("act", [bPD, bEPS], [bRQ],
               lambda e, c0=c0: e.activation(RQ[:, c0 * 128:(c0 + 4) * 128], PD[:, :], AF.Ln, bias=EPS_T[:, 0:1]))
        yield
        op("act", [bRQ], [bRQ], lambda e: e.activation(RQ[:, :], RQ[:, :], AF.Exp, scale=-0.5))
        yield
        QN, bQN = BB_[2]
        op("dve", [bPC, bRQ], [bQN], lambda e: e.tensor_tensor(QN[:, :], PC[:, :], RQ[:, :], ALU.mult))
        yield
        for half in range(2):
            for kc in range(NCH):
                op("pe", [bX1T, bWinB], [bPC, bPC0, bPC1],
                   lambda e, half=half, kc=kc: e.matmul(PC[:, half * 512:(half + 1) * 512], X1T3[:, kc, :],
                                                        WinB[:, kc, D + half * 512:D + (half + 1) * 512],
                                                        start=(kc == 0), stop=(kc == NCH - 1)),
                   inc=(half == 1 and kc == NCH - 1))
            yield
        SGB, bSGB = TB[1]
        op("act", [bPC], [bSGB], lambda e: e.activation(SGB[:, :], PC[:, :], AF.Tanh, scale=0.5))
        op("dve", [bSGB, bPC], [bSGB, bPC0, bPC1],
           lambda e: e.scalar_tensor_tensor(SGB[:, :], SGB[:, :], 1.0, PC[:, :], ALU.add, ALU.mult))
        yield
        OGb, bOGb = BB_[3]
        QN3 = v3(QN)
        for step in range(4):
            kvh, hf = step // 2, step % 2
            PeT, bPe = TB[2 + step % 2]
            PTt, bPTt = BB_[4 + step % 2]
            SMO_, bSMO = SMO[step % 2]
            blocks = ((prev, 0, bPC0), (cur, 1, bPC1))
            for (slot, bank, bBank) in blocks:
                for j in range(4):
                    hh = kvh * 8 + hf * 4 + j
                    oc, par = hh // 2, hh % 2
                    op("pe", [bK2l[slot], bQN], [bBank],
                       lambda e, slot=slot, bank=bank, j=j, oc=oc, par=par, kvh=kvh: e.matmul(
                           PC[:, bank * 512 + j * 128:bank * 512 + (j + 1) * 128], K2t[:, slot, kvh, par, :],
                           QN3[:, oc, :], start=True, stop=True),
                       inc=(j == 3))
            op("act", [bPC0, bPC1, bMNEG], [bPe],
               lambda e, PeT=PeT: e.activation(PeT[:, :], PC[:, :], AF.Exp, bias=MNEG[:, 0:1]))
            h0 = kvh * 8 + hf * 4
            op("pool", [bPe, bEAB], [bPTt],
               lambda e, PeT=PeT, PTt=PTt, h0=h0: e.tensor_tensor(
                   PTt[:, :].rearrange("p (b j q) -> p b j q", b=2, j=4),
                   PeT[:, :].rearrange("p (b j q) -> p b j q", b=2, j=4),
                   EAB[:, :, h0:h0 + 4, :], ALU.mult))
            yield
            for j in range(4):
                op("pe", [bPTt, bVAl[prev]], [bPD],
                   lambda e, j=j, PTt=PTt, kvh=kvh: e.matmul(PD[:, j * 65:(j + 1) * 65], PTt[:, j * 128:(j + 1) * 128],
                                                             VAt[:, prev, kvh, :], start=True, stop=False), inc=False)
                op("pe", [bPTt, bVAl[cur]], [bPD],
                   lambda e, j=j, PTt=PTt, kvh=kvh: e.matmul(PD[:, j * 65:(j + 1) * 65], PTt[:, 512 + j * 128:512 + (j + 1) * 128],
                                                             VAt[:, cur, kvh, :], start=False, stop=True), inc=(j == 3))
            yield
            O3 = PD[:, 0:260].rearrange("p (j e) -> p j e", e=65)
            h0 = kvh * 8 + hf * 4
            op("dve", [bPD, bSINKE], [bSMO],
               lambda e, O3=O3, SMO_=SMO_, h0=h0: e.tensor_tensor(
                   SMO_[:, 0:4].unsqueeze(2), O3[:, :, 64:65], SINKE[:, h0:h0 + 4].unsqueeze(2), ALU.add))
            op("dve", [bSMO], [bSMO], lambda e, SMO_=SMO_: e.reciprocal(SMO_[:, 4:8], SMO_[:, 0:4]))
            TMPO, bTMPO = TB[0]
            op("dve", [bPD, bSMO], [bTMPO],
               lambda e, O3=O3, SMO_=SMO_, TMPO=TMPO, step=step: e.tensor_tensor(
                   TMPO[:, step * 256:(step + 1) * 256].rearrange("p (j d) -> p j d", j=4), O3[:, :, 0:64],
                   SMO_[:, 4:8].unsqueeze(2).to_broadcast([n, 4, HD]), ALU.mult))
            op("dve", [bTMPO, bSGB], [bOGb],
               lambda e, TMPO=TMPO, step=step: e.scalar_tensor_tensor(
                   OGb[:, step * 256:(step + 1) * 256], TMPO[:, step * 256:(step + 1) * 256], 0.5,
                   SGB[:, step * 256:(step + 1) * 256], ALU.mult, ALU.mult))
            yield
        for kc in range(NCH):
            op("pe", [bOGb, bIDB], [bPT0],
               lambda e, kc=kc: e.transpose(PT0[:, kc * 128:(kc + 1) * 128], OGb[:, kc * 128:(kc + 1) * 128], IDB[:, :]),
               inc=(kc == NCH - 1))
        OGT, bOGT = BB_[1]
        op("act", [bPT0], [bOGT], lambda e: e.activation(OGT[:, :], PT0[:, :], AF.Copy))
        OGT3 = v3(OGT)
        yield
        for half in range(2):
            for kc in range(NCH):
                op("pe", [bOGT, bWoutB], [bPC, bPC0, bPC1],
                   lambda e, half=half, kc=kc: e.matmul(PC[:, half * 512:(half + 1) * 512], OGT3[:, kc, :],
                                                        WoutB[:, kc, half * 512:(half + 1) * 512],
                                                        start=(kc == 0), stop=(kc == NCH - 1)),
                   inc=(half == 1 and kc == NCH - 1))
            yield
        op("dve", [bX1c, bPC], [bX1c, bPC0, bPC1], lambda e: e.tensor_tensor(X1c[:, :], X1c[:, :], PC[:, :], ALU.add))
        S.dma("sp", tp["y_dst"], X1c[:nv, :], [bX1c], [])
        yield

    HPREV = ((lambda c: HC[:, c:c + 1]), bHC)
    pre_tiles = []
    for t in range(NPRE):
        pre_tiles.append(dict(
            nv=128, x_src=xpre[t * 128:(t + 1) * 128, :], y_dst=None, Hprev=HPREV, X1=None, carry=True,
            bs=(BS0P, BS1, BS2)[(t - NPRE) % 3], XB=XBS3[(t - NPRE) % 3], XBn=XBS3[(t + 1 - NPRE) % 3],
            idx=t, dep=(t - 1 if t > 0 else None), dep2=(t - 2 if t > 1 else None), Xbuf=None,
            hist_init=None, state_out=None, lite=True, mask=True, kvonly=True))
    tiles = []
    for j in range(NTH + 1):
        cur = j % 2
        prev = 1 - cur
        halo = (j == 0)
        tiles.append(dict(
            nv=128, x_src=xp[j * 128:(j + 1) * 128, :], y_dst=(None if halo else yp[(j - 1) * 128:j * 128, :]),
            Hprev=HPREV, X1=X1[j % 3], Xbuf=X1[j % 3], carry=True, bs=BS0, XBn=XBS[0], idx=NPRE + j, dep=None,
            K2=K2, bK2=bK2s, VA=VA, bVA=bVAs, cur=cur, prev=prev, kmcol=(2 if halo else 0), hist_init=None,
            state_out=({"conv": pconv, "h": ph, "k": pk, "v": pv, "stage": CST[0]} if j == NTH else None),
            lite=False, mask=halo, kvonly=halo))
    sc = (NTH + 1) % 2

    def sample_hist(XBt, bXB):
        op("dve", [bPVT], [bXB],
           lambda e: e.tensor_copy(XBt[:, :, 0:3], PVT[:, 64:88].rearrange("p (j c) -> p c j", j=3)))
    tiles.append(dict(
        nv=NS, x_src=xs[:, :], y_dst=ys[:, :],
        Hprev=((lambda c: PVT[:, 88 + c:89 + c]), bPVT), X1=X1[(NTH + 1) % 3], Xbuf=X1[(NTH + 1) % 3], carry=False,
        bs=BS0, XBn=XBS[0], idx=-1, dep=None,
        K2=K2S, bK2=bK2Ss, VA=VAS, bVA=bVASs, cur=1, prev=0, kmcol=1, hist_init=sample_hist,
        state_out={"conv": sconv, "h": sh, "k": sk[128 - NS:128, :], "v": sv[128 - NS:128, :], "stage": CST[1]},
        lite=False, mask=False, kvonly=False))

    for t in range(NPRE):
        if t + 3 < NPRE:
            pre_tiles[t]["prefetch"] = pre_tiles[t + 3]
    for j in range(len(tiles) - 1):
        tiles[j]["prefetch"] = tiles[j + 1]

    CKt, bCK = TB[3]
    S.dma("sp", CKt[:, 0:128], ck, [], [bCK])
    S.dma("sp", CKt[:, 128:256], cv, [], [bCK])
    op("dve", [bCK], [bKOUT], lambda e: e.tensor_copy(KOUT[:, :], CKt[:, 0:128]))
    k_to_k2(KOUT, bKOUT, K2S, bK2Ss[0], 0)
    op("dve", [bCK], [bVASs[0]],
       lambda e: e.tensor_copy(VAS[:, 0, :, 0:HD], CKt[:, 128:256].rearrange("p (k d) -> p k d", k=2)))
    op("dve", [], [bVASs[0]], lambda e: e.memset(VAS[:, 0, :, HD:HD + 1], 1.0))
    S.dma("sp", sk[0:128 - NS, :], ck[NS:128, :], [], [])
    S.dma("sp", sv[0:128 - NS, :], cv[NS:128, :], [], [])

    def advance(g):
        S.step_fin = 0.0
        try:
            r = next(g)
        except StopIteration:
            return None
        if r == "blocked":
            return -1.0
        return S.step_fin

    def run_streams(gens):
        gens = list(gens)
        times = [0.0 for _ in gens]
        while gens:
            j = min(range(len(gens)), key=lambda q: times[q])
            r = advance(gens[j])
            if r is None:
                gens.pop(j)
                times.pop(j)
            elif r > 0.0:
                times[j] = r

    pend = [phaseA(tp_) for tp_ in pre_tiles]
    active = [deferred_loads()]
    times = [0.0]
    nlite = 0
    NLITE = _NLITE
    while active:
        while nlite < NLITE and pend:
            active.append(pend.pop(0))
            times.append(min(times) if times else 0.0)
            nlite += 1
        j = min(range(len(active)), key=lambda q: times[q])
        r = advance(active[j])
        if r is None:
            if j != 0 or len(active) == 1 or True:
                pass
            g = active.pop(j)
            times.pop(j)
            if getattr(g, "gi_code", None) is not None and g.gi_code.co_name == "phaseA":
                nlite -= 1
        elif r < 0.0:
            times[j] = max(times) + 1.0
        elif r > 0.0:
            times[j] = r

    run_streams([phaseA(tiles[0]), deferred_winb()])
    for i in range(len(tiles)):
        gens = [phaseB(tiles[i])]
        if i + 1 < len(tiles):
            gens.append(phaseA(tiles[i + 1]))
        run_streams(gens)
    S.finish()
    es.close()
    return nc, S


_CACHE = {}


def _consts():
    ident = np.eye(128, dtype=np.float32)
    bones = np.zeros((128, 128), np.float32)
    bones[:64, :64] = 1.0 / 64
    bones[64:, 64:] = 1.0 / 64
    key = np.arange(128)[:, None]
    q = np.arange(128)[None, :]
    dA = (q + 128 - key).astype(np.float32)
    mA = ((key // 64 == 1) | (q // 64 == 0)).astype(np.float32)
    dB = np.abs(q - key).astype(np.float32)
    mB = ((key // 64 == 0) | (q // 64 == 1)).astype(np.float32)
    return ident, bones, dA * mA, dB * mB, mA, mB


def kernel(x_prompt, x_sample, cache_k, cache_v, state_conv, state_rglru,
           a_norm, a_w_in, a_conv_w, a_conv_b, a_gate_a_w, a_gate_a_b, a_gate_x_w, a_gate_x_b,
           a_lambda, a_w_out, kv_norm, w_kv, k_norm, b_norm, b_w_in, q_norm, sinks, b_w_out, _seq=None):
    f = lambda a: np.ascontiguousarray(np.asarray(a, dtype=np.float32))
    x_prompt = f(x_prompt)
    seq = x_prompt.shape[1]
    if seq not in _CACHE:
        _CACHE[seq] = build(seq)[0]
    nc = _CACHE[seq]
    ident, bones, dA, dB, mA, mB = _consts()
    km = np.ones((128, 3), np.float32)
    km[NS:, 1] = 0.0
    nth = seq // 256
    npre = nth - 1
    x_sample = f(x_sample); cache_k = f(cache_k); cache_v = f(cache_v)
    state_conv = f(state_conv); state_rglru = f(state_rglru)
    r8 = lambda v: f(v).reshape(-1, 8, 128).reshape(-1, 128)
    n_cores = 8
    BP = x_prompt.shape[0]
    in_maps = []
    for c in range(n_cores):
        b = c % BP
        half = c // BP
        zt = np.zeros((128, D), np.float32)
        if half == 0:
            xpre_c = np.zeros((max(npre, 1) * 128, D), np.float32)
            xmain_c = np.concatenate([zt, x_prompt[b, :nth * 128]], 0)
        else:
            xpre_c = x_prompt[b, :npre * 128] if npre > 0 else zt
            xmain_c = x_prompt[b, npre * 128:]
        xpre_c = np.ascontiguousarray(xpre_c)
        xmain_c = np.ascontiguousarray(xmain_c)
        pvec = np.concatenate([
            r8(f(a_conv_w)[0]),
            r8(f(a_conv_b)[0][None]), r8(f(a_gate_a_b)[0][None]), r8(f(a_gate_x_b)[0][None]), r8(f(a_lambda)[0][None]),
            r8(state_conv[0, c]),
            r8(state_rglru[0, c][None]),
            r8(f(a_norm)[0][None]), r8(f(kv_norm)[None]), r8(f(b_norm)[0][None]),
        ], axis=0)
        assert pvec.shape == (120, 128)
        in_maps.append({
            "xpre": xpre_c, "xp": xmain_c, "flag": np.array([float(half)], np.float32), "xs": x_sample[c],
            "ck": cache_k[c].reshape(128, 128), "cv": cache_v[c].reshape(128, 128),
            "pvec": np.ascontiguousarray(pvec),
            "w_in_a": f(a_w_in)[0], "wga": f(a_gate_a_w)[0].reshape(D, 256), "wgx": f(a_gate_x_w)[0].reshape(D, 256),
            "w_out_a": f(a_w_out)[0], "w_kv": f(w_kv), "w_in_b": f(b_w_in)[0], "w_out_b": f(b_w_out)[0],
            "knorm": f(k_norm), "qnorm": f(q_norm)[0], "sinks": f(sinks)[0],
            "ident": ident, "bones": bones, "distA": dA, "distB": dB, "maskA": mA, "maskB": mB, "kmask": km,
        })
    res = run_bass_kernel_spmd(nc, in_maps, core_ids=list(range(n_cores)))
    R = res.results
    g = lambda c, k: np.asarray(R[c][k], dtype=np.float32)
    y_prompt = np.stack([np.concatenate([g(b, "yp"), g(b + BP, "yp")], 0) for b in range(BP)], 0)
    y_sample = np.stack([g(c, "ys") for c in range(n_cores)], 0)
    p_k = np.stack([g(b + BP, "pk").reshape(128, 2, HD) for b in range(BP)], 0)
    p_v = np.stack([g(b + BP, "pv").reshape(128, 2, HD) for b in range(BP)], 0)
    p_conv = np.stack([g(b + BP, "pconv") for b in range(BP)], 0)[None]
    p_h = np.stack([g(b + BP, "ph") for b in range(BP)], 0)[None]
    s_k = np.stack([g(c, "sk").reshape(128, 2, HD) for c in range(n_cores)], 0)
    s_v = np.stack([g(c, "sv").reshape(128, 2, HD) for c in range(n_cores)], 0)
    s_conv = np.stack([g(c, "sconv") for c in range(n_cores)], 0)[None]
    s_h = np.stack([g(c, "sh") for c in range(n_cores)], 0)[None]
    return (y_prompt, y_sample, p_k, p_v, p_conv, p_h, s_k, s_v, s_conv, s_h)
```

```python
import numpy as np
from contextlib import ExitStack
import concourse.bass as bass
import concourse.mybir as mybir
from concourse.bass_utils import run_bass_kernel_spmd

F32 = mybir.dt.float32
BF16 = mybir.dt.bfloat16
AF = mybir.ActivationFunctionType
ALU = mybir.AluOpType
AX = mybir.AxisListType

D = 1024
NCH = 8
SEQ = 4096
NS = 16
EPS = 1e-6
N_HEADS = 16
HD = 64


_DBG_STOP = None
_NLITE = 3


class _Stop(Exception):
    pass


def _ck(tag):
    if _DBG_STOP is not None and tag == _DBG_STOP:
        raise _Stop()


class Buf:
    __slots__ = ("name", "w", "r", "parts")

    def __init__(self, name, parts=None):
        self.name = name
        self.w = None
        self.r = {}
        self.parts = parts


def _expand(bufs):
    out = []
    for b in bufs:
        if b.parts is not None:
            out.extend(b.parts)
        else:
            out.append(b)
    return out


class _Proxy:
    def __init__(self, eng):
        self._e = eng
        self.sz = 128
        self.name = None

    def __getattr__(self, name):
        f = getattr(self._e, name)

        def w(*a, **kw):
            out = kw.get("out", a[0] if a else None)
            try:
                m = 1
                for d in out.shape[1:]:
                    m *= d
                self.sz = m
            except Exception:
                pass
            self.name = name
            return f(*a, **kw)
        return w


class SyncMgr:
    NSLOT = 8

    def __init__(self, nc, es):
        self.nc = nc
        self.eng = {"pe": nc.tensor, "act": nc.scalar, "dve": nc.vector, "pool": nc.gpsimd, "sp": nc.sync}
        self.sem = {k: es.enter_context(nc.semaphore("s_" + k)) for k in self.eng}
        self.cnt = {k: 0 for k in self.eng}
        self.waited = {k: {} for k in self.eng}
        self.dsem = {q: [es.enter_context(nc.semaphore("d_%s_%d" % (q, i))) for i in range(self.NSLOT)]
                     for q in ("sp", "pool", "act")}
        self.dcnt = {"sp": 0, "pool": 0, "act": 0}
        self.ninst = 0
        self.efree = {k: 0.0 for k in self.eng}
        self.tfin = {}
        self.step_fin = 0.0

    def _est(self, e, reads, writes, dur):
        reads, writes = _expand(reads), _expand(writes)
        t0 = self.efree[e]
        for b in reads:
            if b.w is not None:
                t0 = max(t0, self.tfin.get(b.w[0:1] + (b.w[2],), 0.0))
        for b in writes:
            if b.w is not None:
                t0 = max(t0, self.tfin.get(b.w[0:1] + (b.w[2],), 0.0))
            for r in b.r.values():
                t0 = max(t0, self.tfin.get(r[0:1] + (r[2],), 0.0))
        fin = t0 + dur
        self.efree[e] = fin if e != "sp" else t0 + 100.0
        self.step_fin = max(self.step_fin, fin)
        return fin

    def _waits(self, e, reads, writes):
        reads, writes = _expand(reads), _expand(writes)
        need = {}
        deps = []
        for b in reads:
            if b.w is not None:
                deps.append(b.w)
        for b in writes:
            if b.w is not None:
                deps.append(b.w)
            deps.extend(b.r.values())
        for key, sem, val in deps:
            if key == ("e", "pe") and e == "pe":
                continue
            if key not in need or need[key][1] < val:
                need[key] = (sem, val)
        for key, (sem, val) in need.items():
            if self.waited[e].get(key, -1) >= val:
                continue
            self.eng[e].wait_ge(sem, val)
            self.waited[e][key] = val

    def _record(self, tok, reads, writes):
        reads, writes = _expand(reads), _expand(writes)
        key = tok[0]
        for b in writes:
            b.w = tok
            b.r = {}
        for b in reads:
            if b in writes:
                continue
            old = b.r.get(key)
            if old is None or old[2] < tok[2]:
                b.r[key] = tok

    def op(self, e, reads, writes, emit, inc=True, sz=None, k=None):
        self._waits(e, reads, writes)
        px = _Proxy(self.eng[e])
        ins = emit(px)
        self.ninst += 1
        if sz is None:
            sz = px.sz
        if k is None:
            k = {"reciprocal": 2.1, "tensor_tensor_scan": 2.0, "tensor_scalar": 0.6, "tensor_copy": 0.6}.get(px.name, 1.0)
        if e == "pe":
            dur = max(62.0, sz / 2.4 + 6.0)
        elif e == "act":
            dur = 220.0 + 0.83 * sz
        elif e == "dve":
            dur = 165.0 + 1.04 * sz * k
        else:
            dur = 150.0 + 2.4 * sz
        fin = self._est(e, reads, writes, dur)
        if inc:
            ins.then_inc(self.sem[e], 1)
            self.cnt[e] += 1
            tok = (("e", e), self.sem[e], self.cnt[e])
        else:
            assert e == "pe"
            tok = (("e", e), self.sem[e], self.cnt[e] + 1)
        self.tfin[(tok[0], tok[2])] = max(fin, self.tfin.get((tok[0], tok[2]), 0.0))
        self._record(tok, reads, writes)

    def dma(self, q, out, in_, reads, writes, **kw):
        fin = self._est(q, reads, writes, 2500.0)
        self._waits(q, reads, writes)
        slot = self.dcnt[q] % self.NSLOT
        use = self.dcnt[q] // self.NSLOT
        key = ("d", q, slot)
        sem = self.dsem[q][slot]
        if use > 0 and self.waited[q].get(key, -1) < 16 * use:
            self.eng[q].wait_ge(sem, 16 * use)
            self.waited[q][key] = 16 * use
        self.eng[q].dma_start(out=out, in_=in_, **kw).then_inc(sem, 16)
        self.ninst += 1
        self.dcnt[q] += 1
        tok = (key, sem, 16 * (use + 1))
        self.tfin[(tok[0], tok[2])] = fin
        self._record(tok, reads, writes)

    def finish(self):
        for q in ("sp", "pool", "act"):
            for slot in range(self.NSLOT):
                uses = (self.dcnt[q] - slot + self.NSLOT - 1) // self.NSLOT
                if uses > 0:
                    key = ("d", q, slot)
                    if self.waited[q].get(key, -1) < 16 * uses:
                        self.eng[q].wait_ge(self.dsem[q][slot], 16 * uses)
                        self.waited[q][key] = 16 * uses


def build(seq=SEQ):
    assert seq % 128 == 0
    assert seq % 256 == 0
    NTH = seq // 256
    NPRE = NTH - 1
    NPRE_A = max(NPRE, 1)
    nc = bass.Bass("TRN2", target_bir_lowering=False)
    es = ExitStack()

    def din(name, shape):
        return nc.dram_tensor(name, list(shape), F32, kind="ExternalInput").ap()

    def dout(name, shape):
        return nc.dram_tensor(name, list(shape), F32, kind="ExternalOutput").ap()

    xpre = din("xpre", [NPRE_A * 128, D]); xp = din("xp", [(NTH + 1) * 128, D]); xs = din("xs", [NS, D])
    flag = din("flag", [1])
    ck = din("ck", [128, 128]); cv = din("cv", [128, 128])
    pvec = din("pvec", [120, 128])
    w_in_a = din("w_in_a", [D, 2 * D]); wga = din("wga", [D, 256]); wgx = din("wgx", [D, 256])
    w_out_a = din("w_out_a", [D, D]); w_kv = din("w_kv", [D, 256])
    w_in_b = din("w_in_b", [D, 2 * D]); w_out_b = din("w_out_b", [D, D])
    knorm = din("knorm", [HD]); qnorm = din("qnorm", [HD]); sinks = din("sinks", [N_HEADS])
    ident = din("ident", [128, 128]); bones = din("bones", [128, 128])
    distA = din("distA", [128, 128]); distB = din("distB", [128, 128])
    maskA = din("maskA", [128, 128]); maskB = din("maskB", [128, 128])
    kmask = din("kmask", [128, 3])

    yp = dout("yp", [NTH * 128, D]); ys = dout("ys", [NS, D])
    pk = dout("pk", [128, 128]); pv = dout("pv", [128, 128])
    pconv = dout("pconv", [3, D]); ph = dout("ph", [D])
    sk = dout("sk", [128, 128]); sv = dout("sv", [128, 128])
    sconv = dout("sconv", [3, D]); sh = dout("sh", [D])

    S = SyncMgr(nc, es)

    def sb(name, shape, dt=F32):
        t = es.enter_context(nc.sbuf_tensor(name, list(shape), dt))
        return t, Buf(name)

    def ps(name, shape, dt=F32):
        t = es.enter_context(nc.psum_tensor(name, list(shape), dt))
        return t, Buf(name)

    WinA, bWinA = sb("WinA", [128, NCH, 2 * D], BF16)
    WGA, bWGA = sb("WGA", [128, NCH, 256], BF16)
    WGX, bWGX = sb("WGX", [128, NCH, 256], BF16)
    WoutA, bWoutA = sb("WoutA", [128, NCH, D], BF16)
    Wkv, bWkv = sb("Wkv", [128, NCH, 256], BF16)
    WinB, bWinB = sb("WinB", [128, NCH, 2 * D], BF16)
    WoutB, bWoutB = sb("WoutB", [128, NCH, D], BF16)
    IDF, bIDF = sb("IDF", [128, 128], F32)
    IDB, bIDB = sb("IDB", [128, 128], BF16)
    BON, bBON = sb("BON", [128, 128], BF16)
    EAB, bEAB = sb("EAB", [128, 2, N_HEADS, 128], BF16)
    bEA = bEB = bEAB
    PVT, bPVT = sb("PVT", [128, 120], F32)
    C8, bC8 = sb("C8", [128, NCH], F32)
    CH, bCH = sb("CH", [128, NCH], F32)
    BAH, bBAH = sb("BAH", [128, NCH], F32)
    BXH, bBXH = sb("BXH", [128, NCH], F32)
    GK, bGK = sb("GK", [128, HD], F32)
    GQ8, bGQ8 = sb("GQ8", [128, HD], F32)
    MNEG, bMNEG = sb("MNEG", [128, 1], F32)
    SINKE, bSINKE = sb("SINKE", [128, N_HEADS], F32)
    CONSTS, bCONSTS = sb("CONSTS", [128, 4], F32)
    SMALL, bSMALL = sb("SMALL", [128, 64], F32)
    SMA, bSMA = sb("SMA", [128, 8], F32)
    SMB, bSMB = sb("SMB", [128, 8], F32)
    SMK, bSMK = sb("SMK", [128, 8], F32)
    SMO = [sb("SMO%d" % i, [128, 8], F32) for i in range(2)]
    KM, bKM = sb("KM", [128, 3], F32)
    CST = [sb("CST%d" % i, [128, NCH, 4], F32) for i in range(2)]
    CTMP, bCTMP = sb("CTMP", [128, 3, 128], F32)

    EPS_T, bEPS = sb("EPS_T", [128, 1], F32)
    TA = [sb("TA%d" % i, [128, D], F32) for i in range(6)]
    TB = [sb("TB%d" % i, [128, D], F32) for i in range(4)]
    BA_ = [sb("BA%d" % i, [128, D], BF16) for i in range(4)]
    BB_ = [sb("BB%d" % i, [128, D], BF16) for i in range(6)]
    for t_ in (TA[2], TB[1]):
        t_[1].parts = [Buf(t_[1].name + "_lo"), Buf(t_[1].name + "_hi")]
    X1 = [sb("X1_%d" % i, [128, D], F32) for i in range(3)]
    XBS = [sb("XB%d" % i, [128, NCH, 3 + 128], F32) for i in range(2)]
    HC, bHC = sb("HC", [128, NCH], F32)
    K2, bK2 = sb("K2", [128, 2, 2, 2, 128], BF16)
    VA, bVA = sb("VA", [128, 2, 2, 65], BF16)
    K2S, bK2S = sb("K2S", [128, 2, 2, 2, 128], BF16)
    VAS, bVAS = sb("VAS", [128, 2, 2, 65], BF16)
    KOUT, bKOUT = sb("KOUT", [128, 128], F32)
    VOUT, bVOUT = sb("VOUT", [128, 128], F32)
    KD, bKD = sb("KD", [128, 2, 2, HD], BF16)
    bWinA2 = Buf("WinA_gate")
    bK2s = [Buf("K2_0"), Buf("K2_1")]
    bVAs = [Buf("VA_0"), Buf("VA_1")]
    bK2Ss = [Buf("K2S_0"), Buf("K2S_1")]
    bVASs = [Buf("VAS_0"), Buf("VAS_1")]

    PT0, bPT0 = ps("PT0", [128, D], BF16)
    PA, bPA = ps("PA", [128, D], F32)
    PB, bPB = ps("PB", [128, D], F32)
    PC, bPC = ps("PC", [128, D], F32)
    PD, bPD = ps("PD", [128, 512], F32)
    bPC0, bPC1 = Buf("PC0"), Buf("PC1")

    op = S.op
    T = TA

    def v3(t, n=128):
        return t[:, :].rearrange("p (c t) -> p c t", c=NCH)[:, :, 0:n]

    S.dma("sp", IDF[:, :], ident, [], [bIDF])
    S.dma("sp", T[0][0][:, 0:128], bones, [], [T[0][1]])
    S.dma("sp", T[1][0][:120, 0:128], pvec, [], [T[1][1]])
    S.dma("sp", GK[:, :], knorm.partition_broadcast(128), [], [bGK])
    S.dma("sp", GQ8[:, :], qnorm.partition_broadcast(128), [], [bGQ8])
    S.dma("sp", SINKE[:, :], sinks.partition_broadcast(128), [], [bSINKE])
    S.dma("sp", T[2][0][:, 0:128], distA, [], [T[2][1]])
    S.dma("sp", T[2][0][:, 128:256], distB, [], [T[2][1]])
    S.dma("sp", T[2][0][:, 256:384], maskA, [], [T[2][1]])
    S.dma("sp", T[2][0][:, 384:512], maskB, [], [T[2][1]])
    S.dma("sp", KM[:, :], kmask, [], [bKM])
    S.dma("sp", KM[:, 2:3], flag.partition_broadcast(128), [], [bKM])

    op("dve", [bIDF], [bIDB], lambda e: e.tensor_copy(IDB[:, :], IDF[:, :]))
    op("dve", [T[0][1]], [bBON], lambda e: e.tensor_copy(BON[:, :], T[0][0][:, 0:128]))
    op("dve", [], [bCONSTS], lambda e: e.memset(CONSTS[:, 0:1], 0.5))
    op("dve", [], [bCONSTS], lambda e: e.memset(CONSTS[:, 1:2], -0.5))
    op("dve", [], [bCONSTS], lambda e: e.memset(CONSTS[:, 2:3], 0.25))
    op("dve", [], [bEPS], lambda e: e.memset(EPS_T[:, 0:1], EPS))
    op("pe", [T[1][1], bIDF], [bPD],
       lambda e: e.transpose(PD[:, 0:120], T[1][0][:120, 0:128], IDF[:120, :120]))
    op("dve", [bPD], [bPVT], lambda e: e.tensor_copy(PVT[:, :], PD[:, 0:120]))
    op("dve", [bPVT], [bBAH], lambda e: e.tensor_scalar(BAH[:, :], PVT[:, 40:48], 0.5, None, ALU.mult))
    op("dve", [bPVT], [bBXH], lambda e: e.tensor_scalar(BXH[:, :], PVT[:, 48:56], 0.5, None, ALU.mult))
    sm = SMALL
    LAM = PVT[:, 56:64]
    op("act", [bPVT], [bSMALL], lambda e: e.activation(sm[:, 0:8], LAM, AF.Abs))
    op("act", [bSMALL], [bSMALL], lambda e: e.activation(sm[:, 8:16], sm[:, 0:8], AF.Exp, scale=-1.0))
    op("dve", [bSMALL], [bSMALL], lambda e: e.tensor_scalar(sm[:, 16:24], sm[:, 8:16], 2.0, None, ALU.add))
    op("dve", [bSMALL], [bSMALL], lambda e: e.reciprocal(sm[:, 16:24], sm[:, 16:24]))
    op("dve", [bSMALL], [bSMALL], lambda e: e.tensor_tensor(sm[:, 24:32], sm[:, 8:16], sm[:, 16:24], ALU.mult))
    op("dve", [bSMALL], [bSMALL], lambda e: e.tensor_tensor(sm[:, 32:40], sm[:, 24:32], sm[:, 24:32], ALU.mult))
    op("dve", [], [bSMALL], lambda e: e.memset(sm[:, 40:48], 1.0 / 19.0))
    for kk in (17, 15, 13, 11, 9, 7, 5, 3, 1):
        op("dve", [bSMALL], [bSMALL], lambda e: e.tensor_tensor(sm[:, 40:48], sm[:, 40:48], sm[:, 32:40], ALU.mult))
        op("dve", [bSMALL], [bSMALL],
           lambda e, kk=kk: e.tensor_scalar(sm[:, 40:48], sm[:, 40:48], 1.0 / kk, None, ALU.add))
    op("dve", [bSMALL], [bSMALL], lambda e: e.tensor_tensor(sm[:, 40:48], sm[:, 40:48], sm[:, 24:32], ALU.mult))
    op("dve", [bPVT, bSMALL], [bSMALL],
       lambda e: e.tensor_scalar(sm[:, 48:56], LAM, -1.0, 0.0, ALU.mult, ALU.max))
    op("dve", [bSMALL], [bSMALL],
       lambda e: e.scalar_tensor_tensor(sm[:, 48:56], sm[:, 40:48], 2.0, sm[:, 48:56], ALU.mult, ALU.add))
    op("dve", [bSMALL], [bC8], lambda e: e.tensor_scalar(C8[:, :], sm[:, 48:56], -8.0, None, ALU.mult))
    op("dve", [bSMALL], [bCH], lambda e: e.tensor_scalar(CH[:, :], sm[:, 48:56], -4.0, None, ALU.mult))

    op("dve", [bGK, bGQ8], [bSMALL], lambda e: e.tensor_tensor(sm[:, 0:64], GK[:, :], GQ8[:, :], ALU.mult))
    op("act", [bSMALL], [bSMALL], lambda e: e.activation(sm[:, 0:64], sm[:, 0:64], AF.Abs))
    op("dve", [bSMALL], [bMNEG], lambda e: e.tensor_reduce(MNEG[:, 0:1], sm[:, 0:64], AX.X, ALU.max))
    op("dve", [bMNEG], [bMNEG], lambda e: e.tensor_scalar(MNEG[:, 0:1], MNEG[:, 0:1], -8.0, None, ALU.mult))
    op("dve", [bGQ8], [bGQ8], lambda e: e.tensor_scalar(GQ8[:, :], GQ8[:, :], 0.125, None, ALU.mult))
    op("act", [bSINKE, bMNEG], [bSINKE],
       lambda e: e.activation(SINKE[:, :], SINKE[:, :], AF.Exp, bias=MNEG[:, 0:1]))

    for (E_, bE_, dcol, mcol, Tt) in ((EAB[:, 0], bEA, 0, 256, T[3]), (EAB[:, 1], bEB, 128, 384, T[4])):
        for h0 in range(0, N_HEADS, 8):
            for h in range(h0, h0 + 8):
                slope = float(2.0 ** (-8.0 * (h + 1) / N_HEADS))
                op("act", [T[2][1]], [Tt[1]],
                   lambda e, h=h, h0=h0, slope=slope, Tt=Tt, dcol=dcol: e.activation(
                       v3(Tt[0])[:, h - h0, :], T[2][0][:, dcol:dcol + 128], AF.Exp, scale=-slope))
            op("dve", [T[2][1], Tt[1]], [bE_],
               lambda e, E_=E_, h0=h0, Tt=Tt, mcol=mcol: e.tensor_tensor(
                   E_[:, h0:h0 + 8, :], v3(Tt[0]),
                   T[2][0][:, mcol:mcol + 128].unsqueeze(1).to_broadcast([128, 8, 128]), ALU.mult))

    stg_i = [0]
    STG = [TB[0], TB[1], TB[2], TB[3], X1[0], X1[1], X1[2]]
    QS = ["sp", "act"]

    def load_weight(dst, bdst, src_ap, ncols_total, gain_col=None):
        for c0 in range(0, ncols_total, 1024):
            ww = min(ncols_total, c0 + 1024) - c0
            tt = STG[stg_i[0] % len(STG)]
            q = QS[stg_i[0] % len(QS)]
            stg_i[0] += 1
            S.dma(q, tt[0][:, 0:ww], src_ap[:, c0:c0 + ww], [], [tt[1]])
            if gain_col is None:
                op("dve", [tt[1]], [bdst],
                   lambda e, tt=tt, ww=ww, c0=c0: e.tensor_copy(dst[:, c0:c0 + ww], tt[0][:, 0:ww]))
            else:
                op("dve", [tt[1], bPVT], [bdst],
                   lambda e, tt=tt, ww=ww, c0=c0: e.tensor_scalar(
                       dst[:, c0:c0 + ww], tt[0][:, 0:ww], PVT[:, gain_col:gain_col + 1], None, ALU.mult))

    for kc in range(NCH):
        load_weight(WinA[:, kc, 0:D], bWinA, w_in_a[kc * 128:(kc + 1) * 128, 0:D], D, gain_col=96 + kc)
    for kc in range(NCH):
        load_weight(WGA[:, kc, :], bWGA, wga[kc * 128:(kc + 1) * 128, :], 256)
        load_weight(WGX[:, kc, :], bWGX, wgx[kc * 128:(kc + 1) * 128, :], 256)

    def deferred_loads():
        stg, bstg = TA[1]
        jobs = []
        for kc in range(NCH):
            jobs.append((WinA[:, kc, D:2 * D], bWinA2, w_in_a[kc * 128:(kc + 1) * 128, D:2 * D], D, 96 + kc))
        for kc in range(NCH):
            jobs.append((WoutA[:, kc, :], bWoutA, w_out_a[kc * 128:(kc + 1) * 128, :], D, None))
        for kc in range(NCH):
            jobs.append((Wkv[:, kc, :], bWkv, w_kv[kc * 128:(kc + 1) * 128, :], 256, 104 + kc))
        for kc in range(NCH):
            pass
        for kc in range(NCH):
            jobs.append((WoutB[:, kc, :], bWoutB, w_out_b[kc * 128:(kc + 1) * 128, :], D, None))
        for (dst, bdst, src, w, gcol) in jobs:
            S.dma("sp", stg[:, 0:w], src, [], [bstg])
            if gcol is None:
                op("dve", [bstg], [bdst], lambda e, dst=dst, w=w: e.tensor_copy(dst, stg[:, 0:w]))
            else:
                op("dve", [bstg, bPVT], [bdst],
                   lambda e, dst=dst, w=w, gcol=gcol: e.tensor_scalar(dst, stg[:, 0:w], PVT[:, gcol:gcol + 1], None, ALU.mult))
            yield

    def deferred_winb():
        k_ = 0
        for kc in range(NCH):
            for c0 in (0, D):
                stg, bstg = TB[2 + k_ % 2]
                k_ += 1
                S.dma("sp", stg[:, 0:D], w_in_b[kc * 128:(kc + 1) * 128, c0:c0 + D], [], [bstg])
                op("dve", [bstg, bPVT], [bWinB],
                   lambda e, kc=kc, c0=c0, stg=stg: e.tensor_scalar(WinB[:, kc, c0:c0 + D], stg[:, 0:D],
                                                                    PVT[:, 112 + kc:113 + kc], None, ALU.mult))
                yield

    op("dve", [], [XBS[0][1]], lambda e: e.memset(XBS[0][0][:, :, 0:3], 0.0))
    op("dve", [], [XBS[1][1]], lambda e: e.memset(XBS[1][0][:, :, 0:3], 0.0))
    op("dve", [], [bHC], lambda e: e.memset(HC[:, :], 0.0))
    op("dve", [], [bK2s[0], bK2s[1]], lambda e: e.memset(K2[:, :, :, :, :].rearrange("p a b c d -> p (a b c d)"), 0.0))
    op("dve", [], [bK2Ss[0], bK2Ss[1]], lambda e: e.memset(K2S[:, :, :, :, :].rearrange("p a b c d -> p (a b c d)"), 0.0))
    op("dve", [], [bVAs[1]], lambda e: e.memset(VA[:, 1, :, :], 0.0))

    n = 128

    def rms_to_T(Xsrc, bXsrc, SM_, bSM, XNb, XTb, on_act=False):
        XN, bXN = XNb
        XT_, bXT = XTb
        op("act", [bXsrc], [bXN, bSM],
           lambda e: e.activation(XN[:, :], Xsrc[:, :], AF.Square, accum_out=SM_[:, 0:1]))
        op("dve", [bSM], [bSM],
           lambda e: e.tensor_scalar(SM_[:, 1:2], SM_[:, 0:1], 1.0 / D, EPS, ALU.mult, ALU.add))
        op("pool", [bSM, bCONSTS], [bSM],
           lambda e: e.tensor_tensor(SM_[:, 2:3], SM_[:, 1:2], CONSTS[:, 1:2], ALU.pow))
        if on_act:
            op("act", [bXsrc, bSM], [bXN],
               lambda e: e.activation(XN[:, :], Xsrc[:, :], AF.Identity, scale=SM_[:, 2:3]))
        else:
            op("dve", [bXsrc, bSM], [bXN],
               lambda e: e.tensor_scalar(XN[:, :], Xsrc[:, :], SM_[:, 2:3], None, ALU.mult))
        yield
        for kc in range(NCH):
            op("pe", [bXN, bIDB], [bPT0],
               lambda e, kc=kc: e.transpose(PT0[:, kc * 128:(kc + 1) * 128], XN[:, kc * 128:(kc + 1) * 128], IDB[:, :]),
               inc=(kc == NCH - 1))
        op("act", [bPT0], [bXT], lambda e: e.activation(XT_[:, :], PT0[:, :], AF.Copy))

    def k_to_k2(Ksrc, bKsrc, K2t, bK2slot, slot):
        op("dve", [bKsrc, bGQ8], [bKD],
           lambda e: e.tensor_tensor(
               KD[:, :, :, :],
               Ksrc[:, :].rearrange("p (k d) -> p k d", k=2).unsqueeze(2).to_broadcast([n, 2, 2, HD]),
               GQ8[:, :].unsqueeze(1).unsqueeze(1).to_broadcast([n, 2, 2, HD]), ALU.mult))
        for k in range(2):
            op("pe", [bKD, bIDB], [bPT0],
               lambda e, k=k: e.transpose(PT0[:, k * 128:(k + 1) * 128],
                                          KD[:, k, :, :].rearrange("p a d -> p (a d)"), IDB[:, :]),
               inc=(k == 1))
        for par in range(2):
            op("act", [bPT0], [bK2slot],
               lambda e, par=par: e.activation(
                   K2t[par * 64:(par + 1) * 64, slot, :, par, :],
                   PT0[par * 64:(par + 1) * 64, 0:256].rearrange("p (k t) -> p k t", k=2), AF.Copy))

    BS0 = dict(T0=TA[0], SG=TA[1], XC=TA[2], TI=TA[3], LA=TA[4], A2=TA[5], XN=BA_[0], XT=BA_[1], XCb=BA_[2], HG=BA_[3],
               SM=(SMA, bSMA), X=X1[2], P0=(PA, bPA), P1=(PB, bPB), XB=XBS[0], XBn=XBS[1])
    BS1 = dict(T0=TB[0], SG=None, XC=TB[1], TI=TB[2], LA=TB[3], A2=X1[0], XN=BB_[0], XT=BB_[1], XCb=BB_[2], HG=None,
               SM=(SMB, bSMB), X=X1[1], P0=(PC, bPC), P1=(PC, bPC), XB=XBS[1], XBn=XBS[0])
    R32 = WinB[:, :, :].rearrange("p a b -> p (a b)").bitcast(F32)
    def r32(i, name):
        return (R32[:, i * D:(i + 1) * D], Buf(name))
    R_ = [r32(i, "R32_%d" % i) for i in range(6)]
    XB3 = (R32[:, 6 * D:6 * D + NCH * 131].rearrange("p (c t) -> p c t", c=NCH), Buf("XB3"))
    R_[1][1].parts = [Buf("R32_1_lo"), Buf("R32_1_hi")]
    bWinB.parts = [b for (_, b) in R_[:1]] + R_[1][1].parts + [b for (_, b) in R_[2:]] + [XB3[1], Buf("WinB_rest")]
    BS2 = dict(T0=R_[0], SG=None, XC=R_[1], TI=R_[2], LA=R_[3], A2=R_[4], XN=BB_[3], XT=BB_[4], XCb=BB_[5], HG=None,
               SM=(SMK, bSMK), X=R_[5], P0=(PB, bPB), P1=(PB, bPB), XB=XB3, XBn=XBS[0])
    BS0P = dict(BS0, P1=(PA, bPA))
    XBS3 = [XBS[0], XBS[1], XB3]
    op("dve", [], [XB3[1]], lambda e: e.memset(XB3[0][:, :, 0:3], 0.0))

    EVENTS = {}

    def phaseA(tp):
        nv = tp["nv"]
        bs = tp["bs"]
        lite, mask = tp["lite"], tp["mask"]
        Xt, bXt = tp.get("Xbuf") or bs["X"]
        XBt, bXB = tp.get("XB") or bs["XB"]
        P0, bP0 = bs["P0"]
        P1, bP1 = bs["P1"]
        SM_, bSM = bs["SM"]
        def load_x(tq):
            Xq, bXq = tq.get("Xbuf") or tq["bs"]["X"]
            if tq["nv"] < n:
                op("dve", [], [bXq], lambda e: e.memset(Xq[:, :], 0.0))
            S.dma("sp", Xq[:tq["nv"], :], tq["x_src"], [], [bXq])
            tq["preloaded"] = True
        if not tp.get("preloaded"):
            load_x(tp)
        nxt = tp.get("prefetch")
        if nxt is not None and nxt.get("Xbuf") is not None:
            load_x(nxt)
        if tp["hist_init"] is not None:
            tp["hist_init"](XBt, bXB)
        yield
        XT_, bXT = bs["XT"]
        g_ = rms_to_T(Xt, bXt, SM_, bSM, bs["XN"], bs["XT"], on_act=(not lite))
        next(g_)
        if nxt is not None and nxt.get("Xbuf") is None and lite:
            load_x(nxt)
        yield
        for _ in g_:
            yield
        XT3 = v3(XT_)
        yield
        for oc in range(NCH):
            for kc in range(NCH):
                op("pe", [bXT, bWinA], [bP0],
                   lambda e, oc=oc, kc=kc: e.matmul(v3(P0)[:, oc, :], WinA[:, kc, oc * 128:(oc + 1) * 128], XT3[:, kc, :],
                                                    start=(kc == 0), stop=(kc == NCH - 1)),
                   inc=(oc == NCH - 1 and kc == NCH - 1))
            if oc % 2 == 1 and oc != NCH - 1:
                yield
        op("act", [bP0], [bXB], lambda e: e.activation(XBt[:, :, 3:3 + n], v3(P0), AF.Copy))
        XBn, bXBn = tp.get("XBn") or bs["XBn"]
        while tp.get("dep2") is not None and not EVENTS.get(("conv", tp["dep2"])):
            yield "blocked"
        if XBn is not XBt:
            op("act", [bXB], [bXBn], lambda e: e.activation(XBn[:, :, 0:3], XBt[:, :, n:n + 3], AF.Copy))
            EVENTS[("hist", tp["idx"])] = True
        yield
        while tp["dep"] is not None and not EVENTS.get(("hist", tp["dep"])):
            yield "blocked"
        if not lite:
            for oc in range(NCH):
                for kc in range(NCH):
                    op("pe", [bXT, bWinA2], [bP1],
                       lambda e, oc=oc, kc=kc: e.matmul(v3(P1)[:, oc, :], WinA[:, kc, D + oc * 128:D + (oc + 1) * 128],
                                                        XT3[:, kc, :], start=(kc == 0), stop=(kc == NCH - 1)),
                       inc=(oc == NCH - 1 and kc == NCH - 1))
                if oc % 2 == 1 and oc != NCH - 1:
                    yield
            TG, bTG = bs["T0"]
            SG, bSG = bs["SG"]
            op("act", [bP1], [bTG], lambda e: e.activation(TG[:, :], P1[:, :], AF.Tanh, scale=0.5))
            op("dve", [bTG, bP1], [bSG],
               lambda e: e.scalar_tensor_tensor(SG[:, :], TG[:, :], 1.0, P1[:, :], ALU.add, ALU.mult))
            yield
        XC, bXC = bs["XC"]
        XC3 = v3(XC)
        NDC = 5
        bXCl, bXCh = bXC.parts
        for c in range(NDC):
            op("dve", [bXB, bPVT], [bXCl],
               lambda e, c=c: e.tensor_scalar(XC3[:, c, :], XBt[:, c, 0:n], PVT[:, c:c + 1], PVT[:, 32 + c:33 + c],
                                              ALU.mult, ALU.add))
        yield

        def wbc(col):
            return PVT[:, col + NDC:col + NCH].unsqueeze(2).to_broadcast([128, NCH - NDC, n])
        op("pool", [bXB, bPVT], [bXCh],
           lambda e: e.tensor_tensor(XC3[:, NDC:NCH, :], XBt[:, NDC:NCH, 0:n], wbc(0), ALU.mult))
        op("pool", [bPVT, bXCh], [bXCh],
           lambda e: e.tensor_tensor(XC3[:, NDC:NCH, :], XC3[:, NDC:NCH, :], wbc(32), ALU.add))
        yield
        for k in range(1, 4):
            for c in range(NDC):
                op("dve", [bXB, bPVT, bXCl], [bXCl],
                   lambda e, c=c, k=k: e.scalar_tensor_tensor(
                       XC3[:, c, :], XBt[:, c, k:k + n], PVT[:, k * 8 + c:k * 8 + c + 1], XC3[:, c, :], ALU.mult, ALU.add))
                if c == 2:
                    yield
            op("pool", [bXB, bPVT], [bCTMP],
               lambda e, k=k: e.tensor_tensor(CTMP[:, :, :], XBt[:, NDC:NCH, k:k + n], wbc(k * 8), ALU.mult))
            op("pool", [bCTMP, bXCh], [bXCh],
               lambda e: e.tensor_tensor(XC3[:, NDC:NCH, :], XC3[:, NDC:NCH, :], CTMP[:, :, :], ALU.add))
            yield
        EVENTS[("conv", tp["idx"])] = True
        so = tp["state_out"]
        if so is not None:
            CSt, bCS = so["stage"]
            op("act", [bXB], [bCS], lambda e: e.activation(CSt[:, :, 0:3], XBt[:, :, nv:nv + 3], AF.Copy))
            for j in range(3):
                S.dma("sp", so["conv"][j].rearrange("(c p) -> p c", p=128), CSt[:, :, j],
                      [bCS], [], allow_slow_non_contiguous=True)
        if XBn is XBt:
            op("act", [bXB], [bXB], lambda e: e.activation(XBt[:, :, 0:3], XBt[:, :, n:n + 3], AF.Copy))
        XCb, bXCb = bs["XCb"]
        op("act", [bXC], [bXCb], lambda e: e.activation(XCb[:, :], XC[:, :], AF.Copy))
        XCb3 = v3(XCb)
        yield
        TR, bTR = bs["T0"]
        TI, bTI = bs["TI"]
        LA, bLA = bs["LA"]
        A2, bA2 = bs["A2"]
        for (P_, bP, Wg, bWg, TT, bTT, BH, bBH) in ((P0, bP0, WGA, bWGA, TR, bTR, BAH, bBAH),
                                                   (P1, bP1, WGX, bWGX, TI, bTI, BXH, bBXH)):
            for blk in range(4):
                for oc in range(2):
                    for kc in range(2):
                        last = (blk == 3 and oc == 1 and kc == 1)
                        op("pe", [bXCb, bWg], [bP],
                           lambda e, P_=P_, Wg=Wg, blk=blk, oc=oc, kc=kc: e.matmul(
                               v3(P_)[:, blk * 2 + oc, :], Wg[:, blk * 2 + kc, oc * 128:(oc + 1) * 128],
                               XCb3[:, blk * 2 + kc, :], start=(kc == 0), stop=(kc == 1)),
                           inc=last)
                if blk == 1:
                    yield
            yield
            for c in range(NCH):
                op("act", [bP, bBH], [bTT],
                   lambda e, c=c, TT=TT, P_=P_, BH=BH: e.activation(v3(TT)[:, c, :], v3(P_)[:, c, :], AF.Tanh,
                                                                    bias=BH[:, c:c + 1], scale=0.5))
                if c == 3:
                    yield
            yield
        op("dve", [bTR, bCH], [bLA],
           lambda e: e.scalar_tensor_tensor(v3(LA), v3(TR), 1.0, CH[:, :].unsqueeze(2).to_broadcast([128, NCH, n]),
                                            ALU.add, ALU.mult))
        yield
        op("act", [bLA], [bA2], lambda e: e.activation(A2[:, :], LA[:, :], AF.Exp, scale=2.0))
        yield
        op("act", [bLA], [bLA], lambda e: e.activation(LA[:, :], LA[:, :], AF.Exp))
        yield
        op("act", [bA2, bCONSTS], [bA2],
           lambda e: e.activation(A2[:, :], A2[:, :], AF.Sqrt, bias=CONSTS[:, 2:3], scale=-0.25))
        yield
        op("dve", [bTI, bXC], [bTI],
           lambda e: e.scalar_tensor_tensor(TI[:, :], TI[:, :], 1.0, XC[:, :], ALU.add, ALU.mult))
        yield
        if mask:
            op("dve", [bA2, bTI, bKM], [bTI],
               lambda e: e.scalar_tensor_tensor(TI[:, :], A2[:, :], KM[:, 2:3], TI[:, :], ALU.mult, ALU.mult))
        else:
            op("pool", [bA2, bTI], [bTI], lambda e: e.tensor_tensor(TI[:, :], A2[:, :], TI[:, :], ALU.mult))
        yield
        while tp["dep"] is not None and not EVENTS.get(("carry", tp["dep"])):
            yield "blocked"
        Hp_fn, bHp = tp["Hprev"]
        Hv = v3(TI)
        for c in range(NCH):
            op("dve", [bLA, bTI, bHp], [bTI],
               lambda e, c=c: e.tensor_tensor_scan(Hv[:, c, :], v3(LA)[:, c, :], Hv[:, c, :],
                                                   Hp_fn(c), ALU.mult, ALU.add))
            if c % 2 == 1:
                yield
        if so is not None:
            CSt, bCS = so["stage"]
            op("act", [bTI], [bCS], lambda e: e.activation(CSt[:, :, 3:4], Hv[:, :, nv - 1:nv], AF.Copy))
            S.dma("sp", so["h"].rearrange("(c p) -> p c", p=128), CSt[:, :, 3],
                  [bCS], [], allow_slow_non_contiguous=True)
        if tp["carry"]:
            op("dve", [bTI], [bHC], lambda e: e.tensor_copy(HC[:, :], Hv[:, :, n - 1]))
        EVENTS[("carry", tp["idx"])] = True
        if lite:
            return
        X1c, bX1c = tp["X1"]
        HG, bHG = bs["HG"]
        SG, bSG = bs["SG"]
        op("dve", [bTI, bSG], [bHG],
           lambda e: e.scalar_tensor_tensor(v3(HG), Hv, 0.5, v3(SG), ALU.mult, ALU.mult))
        HG3 = v3(HG)
        yield
        for half in range(2):
            for kc in range(NCH):
                op("pe", [bHG, bWoutA], [bP0],
                   lambda e, half=half, kc=kc: e.matmul(P0[:, half * 512:(half + 1) * 512], HG3[:, kc, :],
                                                        WoutA[:, kc, half * 512:(half + 1) * 512],
                                                        start=(kc == 0), stop=(kc == NCH - 1)),
                   inc=(half == 1 and kc == NCH - 1))
            yield
        op("dve", [bXt, bP0], [bX1c], lambda e: e.tensor_tensor(X1c[:, :], Xt[:, :], P0[:, :], ALU.add))
        yield

    def phaseB(tp):
        nv = tp["nv"]
        X1c, bX1c = tp["X1"]
        K2t, bK2l, VAt, bVAl = tp["K2"], tp["bK2"], tp["VA"], tp["bVA"]
        cur, prev, kmcol = tp["cur"], tp["prev"], tp["kmcol"]
        so = tp["state_out"]
        yield from rms_to_T(X1c, bX1c, SMB, bSMB, BB_[0], BB_[1])
        X1T3 = v3(BB_[1][0])
        bX1T = BB_[1][1]
        yield
        for kc in range(NCH):
            op("pe", [bX1T, bWkv], [bPD],
               lambda e, kc=kc: e.matmul(PD[:, 0:256], X1T3[:, kc, :], Wkv[:, kc, :], start=(kc == 0), stop=(kc == NCH - 1)),
               inc=(kc == NCH - 1))
        Ksq, bKsq = TB[0]
        op("dve", [bPD], [bKsq], lambda e: e.tensor_copy(Ksq[:, 0:256], PD[:, 0:256]))
        op("dve", [bKsq], [bKsq], lambda e: e.tensor_tensor(Ksq[:, 256:384], Ksq[:, 0:128], Ksq[:, 0:128], ALU.mult))
        op("dve", [bKsq], [bSMK],
           lambda e: e.tensor_reduce(SMK[:, 0:2], Ksq[:, 256:384].rearrange("p (k d) -> p k d", k=2), AX.X, ALU.add))
        op("dve", [bSMK], [bSMK],
           lambda e: e.tensor_scalar(SMK[:, 0:2], SMK[:, 0:2], 1.0 / HD, EPS, ALU.mult, ALU.add))
        op("pool", [bSMK, bCONSTS], [bSMK],
           lambda e: e.tensor_tensor(SMK[:, 2:4], SMK[:, 0:2], CONSTS[:, 1:2].to_broadcast([n, 2]), ALU.pow))
        op("dve", [bKsq, bSMK], [bKsq],
           lambda e: e.tensor_tensor(Ksq[:, 384:512].rearrange("p (k d) -> p k d", k=2),
                                     Ksq[:, 0:128].rearrange("p (k d) -> p k d", k=2),
                                     SMK[:, 2:4].unsqueeze(2).to_broadcast([n, 2, HD]), ALU.mult))
        op("dve", [bKsq, bGK], [bKOUT],
           lambda e: e.tensor_tensor(KOUT[:, :].rearrange("p (k d) -> p k d", k=2),
                                     Ksq[:, 384:512].rearrange("p (k d) -> p k d", k=2),
                                     GK[:, :].unsqueeze(1).to_broadcast([n, 2, HD]), ALU.mult))
        yield

        def k_finish():
            k_to_k2(KOUT, bKOUT, K2t, bK2l[cur], cur)
            op("dve", [bKsq, bKM], [bVAl[cur]],
               lambda e: e.tensor_scalar(VAt[:, cur, :, 0:HD], Ksq[:, 128:256].rearrange("p (k d) -> p k d", k=2),
                                         KM[:, kmcol:kmcol + 1], None, ALU.mult))
            op("dve", [bKM], [bVAl[cur]],
               lambda e: e.tensor_copy(VAt[:, cur, :, HD:HD + 1], KM[:, kmcol:kmcol + 1].unsqueeze(1).to_broadcast([n, 2, 1])))
            if so is not None:
                op("dve", [bKsq], [bVOUT], lambda e: e.tensor_copy(VOUT[:, :], Ksq[:, 128:256]))
                S.dma("sp", so["k"], KOUT[:nv, :], [bKOUT], [])
                S.dma("sp", so["v"], VOUT[:nv, :], [bVOUT], [])
        if tp["kvonly"]:
            k_finish()
            yield
            return
        for oc in range(NCH):
            for kc in range(NCH):
                op("pe", [bX1T, bWinB], [bPC, bPC0, bPC1],
                   lambda e, oc=oc, kc=kc: e.matmul(v3(PC)[:, oc, :], WinB[:, kc, oc * 128:(oc + 1) * 128], X1T3[:, kc, :],
                                                    start=(kc == 0), stop=(kc == NCH - 1)),
                   inc=(oc == NCH - 1 and kc == NCH - 1))
            if oc % 2 == 1:
                yield
        SQ, bSQ = BB_[0]
        op("act", [bPC], [bSQ], lambda e: e.activation(SQ[:, :], PC[:, :], AF.Square))
        yield
        k_finish()
        yield
        RQ, bRQ = TB[0]
        for c0 in range(0, NCH, 4):
            op("pe", [bSQ, bBON], [bPD],
               lambda e, c0=c0: e.matmul(PD[:, :], BON[:, :], SQ[:, c0 * 128:(c0 + 4) * 128], start=True, stop=True))
            op("act", [bPD, bEPS], [bRQ],
               lambda e, c0=c0: e.activation(RQ[:, c0 * 128:(c0 + 4) * 128], PD[:, :], AF.Ln, bias=EPS_T[:, 0:1]))
        yield
        op("act", [bRQ], [bRQ], lambda e: e.activation(RQ[:, :], RQ[:, :], AF.Exp, scale=-0.5))
        yield
        QN, bQN = BB_[2]
        op("dve", [bPC, bRQ], [bQN], lambda e: e.tensor_tensor(QN[:, :], PC[:, :], RQ[:, :], ALU.mult))
        yield
        for half in range(2):
            for kc in range(NCH):
                op("pe", [bX1T, bWinB], [bPC, bPC0, bPC1],
                   lambda e, half=half, kc=kc: e.matmul(PC[:, half * 512:(half + 1) * 512], X1T3[:, kc, :],
                                                        WinB[:, kc, D + half * 512:D + (half + 1) * 512],
                                                        start=(kc == 0), stop=(kc == NCH - 1)),
                   inc=(half == 1 and kc == NCH - 1))
            yield
        SGB, bSGB = TB[1]
        op("act", [bPC], [bSGB], lambda e: e.activation(SGB[:, :], PC[:, :], AF.Tanh, scale=0.5))
        op("dve", [bSGB, bPC], [bSGB, bPC0, bPC1],
           lambda e: e.scalar_tensor_tensor(SGB[:, :], SGB[:, :], 1.0, PC[:, :], ALU.add, ALU.mult))
        yield
        OGb, bOGb = BB_[3]
        QN3 = v3(QN)
        for step in range(4):
            kvh, hf = step // 2, step % 2
            PeT, bPe = TB[2 + step % 2]
            PTt, bPTt = BB_[4 + step % 2]
            SMO_, bSMO = SMO[step % 2]
            blocks = ((prev, 0, bPC0), (cur, 1, bPC1))
            for (slot, bank, bBank) in blocks:
                for j in range(4):
                    hh = kvh * 8 + hf * 4 + j
                    oc, par = hh // 2, hh % 2
                    op("pe", [bK2l[slot], bQN], [bBank],
                       lambda e, slot=slot, bank=bank, j=j, oc=oc, par=par, kvh=kvh: e.matmul(
                           PC[:, bank * 512 + j * 128:bank * 512 + (j + 1) * 128], K2t[:, slot, kvh, par, :],
                           QN3[:, oc, :], start=True, stop=True),
                       inc=(j == 3))
            op("act", [bPC0, bPC1, bMNEG], [bPe],
               lambda e, PeT=PeT: e.activation(PeT[:, :], PC[:, :], AF.Exp, bias=MNEG[:, 0:1]))
            h0 = kvh * 8 + hf * 4
            op("pool", [bPe, bEAB], [bPTt],
               lambda e, PeT=PeT, PTt=PTt, h0=h0: e.tensor_tensor(
                   PTt[:, :].rearrange("p (b j q) -> p b j q", b=2, j=4),
                   PeT[:, :].rearrange("p (b j q) -> p b j q", b=2, j=4),
                   EAB[:, :, h0:h0 + 4, :], ALU.mult))
            yield
            for j in range(4):
                op("pe", [bPTt, bVAl[prev]], [bPD],
                   lambda e, j=j, PTt=PTt, kvh=kvh: e.matmul(PD[:, j * 65:(j + 1) * 65], PTt[:, j * 128:(j + 1) * 128],
                                                             VAt[:, prev, kvh, :], start=True, stop=False), inc=False)
                op("pe", [bPTt, bVAl[cur]], [bPD],
                   lambda e, j=j, PTt=PTt, kvh=kvh: e.matmul(PD[:, j * 65:(j + 1) * 65], PTt[:, 512 + j * 128:512 + (j + 1) * 128],
                                                             VAt[:, cur, kvh, :], start=False, stop=True), inc=(j == 3))
            yield
            O3 = PD[:, 0:260].rearrange("p (j e) -> p j e", e=65)
            h0 = kvh * 8 + hf * 4
            op("dve", [bPD, bSINKE], [bSMO],
               lambda e, O3=O3, SMO_=SMO_, h0=h0: e.tensor_tensor(
                   SMO_[:, 0:4].unsqueeze(2), O3[:, :, 64:65], SINKE[:, h0:h0 + 4].unsqueeze(2), ALU.add))
            op("dve", [bSMO], [bSMO], lambda e, SMO_=SMO_: e.reciprocal(SMO_[:, 4:8], SMO_[:, 0:4]))
            TMPO, bTMPO = TB[0]
            op("dve", [bPD, bSMO], [bTMPO],
               lambda e, O3=O3, SMO_=SMO_, TMPO=TMPO, step=step: e.tensor_tensor(
                   TMPO[:, step * 256:(step + 1) * 256].rearrange("p (j d) -> p j d", j=4), O3[:, :, 0:64],
                   SMO_[:, 4:8].unsqueeze(2).to_broadcast([n, 4, HD]), ALU.mult))
            op("dve", [bTMPO, bSGB], [bOGb],
               lambda e, TMPO=TMPO, step=step: e.scalar_tensor_tensor(
                   OGb[:, step * 256:(step + 1) * 256], TMPO[:, step * 256:(step + 1) * 256], 0.5,
                   SGB[:, step * 256:(step + 1) * 256], ALU.mult, ALU.mult))
            yield
        for kc in range(NCH):
            op("pe", [bOGb, bIDB], [bPT0],
               lambda e, kc=kc: e.transpose(PT0[:, kc * 128:(kc + 1) * 128], OGb[:, kc * 128:(kc + 1) * 128], IDB[:, :]),
               inc=(kc == NCH - 1))
        OGT, bOGT = BB_[1]
        op("act", [bPT0], [bOGT], lambda e: e.activation(OGT[:, :], PT0[:, :], AF.Copy))
        OGT3 = v3(OGT)
        yield
        for half in range(2):
            for kc in range(NCH):
                op("pe", [bOGT, bWoutB], [bPC, bPC0, bPC1],
                   lambda e, half=half, kc=kc: e.matmul(PC[:, half * 512:(half + 1) * 512], OGT3[:, kc, :],
                                                        WoutB[:, kc, half * 512:(half + 1) * 512],
                                                        start=(kc == 0), stop=(kc == NCH - 1)),
                   inc=(half == 1 and kc == NCH - 1))
            yield
        op("dve", [bX1c, bPC], [bX1c, bPC0, bPC1], lambda e: e.tensor_tensor(X1c[:, :], X1c[:, :], PC[:, :], ALU.add))
        S.dma("sp", tp["y_dst"], X1c[:nv, :], [bX1c], [])
        yield

    HPREV = ((lambda c: HC[:, c:c + 1]), bHC)
    pre_tiles = []
    for t in range(NPRE):
        pre_tiles.append(dict(
            nv=128, x_src=xpre[t * 128:(t + 1) * 128, :], y_dst=None, Hprev=HPREV, X1=None, carry=True,
            bs=(BS0P, BS1, BS2)[(t - NPRE) % 3], XB=XBS3[(t - NPRE) % 3], XBn=XBS3[(t + 1 - NPRE) % 3],
            idx=t, dep=(t - 1 if t > 0 else None), dep2=(t - 2 if t > 1 else None), Xbuf=None,
            hist_init=None, state_out=None, lite=True, mask=True, kvonly=True))
    tiles = []
    for j in range(NTH + 1):
        cur = j % 2
        prev = 1 - cur
        halo = (j == 0)
        tiles.append(dict(
            nv=128, x_src=xp[j * 128:(j + 1) * 128, :], y_dst=(None if halo else yp[(j - 1) * 128:j * 128, :]),
            Hprev=HPREV, X1=X1[j % 3], Xbuf=X1[j % 3], carry=True, bs=BS0, XBn=XBS[0], idx=NPRE + j, dep=None,
            K2=K2, bK2=bK2s, VA=VA, bVA=bVAs, cur=cur, prev=prev, kmcol=(2 if halo else 0), hist_init=None,
            state_out=({"conv": pconv, "h": ph, "k": pk, "v": pv, "stage": CST[0]} if j == NTH else None),
            lite=False, mask=halo, kvonly=halo))
    sc = (NTH + 1) % 2

    def sample_hist(XBt, bXB):
        op("dve", [bPVT], [bXB],
           lambda e: e.tensor_copy(XBt[:, :, 0:3], PVT[:, 64:88].rearrange("p (j c) -> p c j", j=3)))
    tiles.append(dict(
        nv=NS, x_src=xs[:, :], y_dst=ys[:, :],
        Hprev=((lambda c: PVT[:, 88 + c:89 + c]), bPVT), X1=X1[(NTH + 1) % 3], Xbuf=X1[(NTH + 1) % 3], carry=False,
        bs=BS0, XBn=XBS[0], idx=-1, dep=None,
        K2=K2S, bK2=bK2Ss, VA=VAS, bVA=bVASs, cur=1, prev=0, kmcol=1, hist_init=sample_hist,
        state_out={"conv": sconv, "h": sh, "k": sk[128 - NS:128, :], "v": sv[128 - NS:128, :], "stage": CST[1]},
        lite=False, mask=False, kvonly=False))

    for t in range(NPRE):
        if t + 3 < NPRE:
            pre_tiles[t]["prefetch"] = pre_tiles[t + 3]
    for j in range(len(tiles) - 1):
        tiles[j]["prefetch"] = tiles[j + 1]

    CKt, bCK = TB[3]
    S.dma("sp", CKt[:, 0:128], ck, [], [bCK])
    S.dma("sp", CKt[:, 128:256], cv, [], [bCK])
    op("dve", [bCK], [bKOUT], lambda e: e.tensor_copy(KOUT[:, :], CKt[:, 0:128]))
    k_to_k2(KOUT, bKOUT, K2S, bK2Ss[0], 0)
    op("dve", [bCK], [bVASs[0]],
       lambda e: e.tensor_copy(VAS[:, 0, :, 0:HD], CKt[:, 128:256].rearrange("p (k d) -> p k d", k=2)))
    op("dve", [], [bVASs[0]], lambda e: e.memset(VAS[:, 0, :, HD:HD + 1], 1.0))
    S.dma("sp", sk[0:128 - NS, :], ck[NS:128, :], [], [])
    S.dma("sp", sv[0:128 - NS, :], cv[NS:128, :], [], [])

    def advance(g):
        S.step_fin = 0.0
        try:
            r = next(g)
        except StopIteration:
            return None
        if r == "blocked":
            return -1.0
        return S.step_fin

    def run_streams(gens):
        gens = list(gens)
        times = [0.0 for _ in gens]
        while gens:
            j = min(range(len(gens)), key=lambda q: times[q])
            r = advance(gens[j])
            if r is None:
                gens.pop(j)
                times.pop(j)
            elif r > 0.0:
                times[j] = r

    pend = [phaseA(tp_) for tp_ in pre_tiles]
    active = [deferred_loads()]
    times = [0.0]
    nlite = 0
    NLITE = _NLITE
    while active:
        while nlite < NLITE and pend:
            active.append(pend.pop(0))
            times.append(min(times) if times else 0.0)
            nlite += 1
        j = min(range(len(active)), key=lambda q: times[q])
        r = advance(active[j])
        if r is None:
            if j != 0 or len(active) == 1 or True:
                pass
            g = active.pop(j)
            times.pop(j)
            if getattr(g, "gi_code", None) is not None and g.gi_code.co_name == "phaseA":
                nlite -= 1
        elif r < 0.0:
            times[j] = max(times) + 1.0
        elif r > 0.0:
            times[j] = r

    run_streams([phaseA(tiles[0]), deferred_winb()])
    for i in range(len(tiles)):
        gens = [phaseB(tiles[i])]
        if i + 1 < len(tiles):
            gens.append(phaseA(tiles[i + 1]))
        run_streams(gens)
    S.finish()
    es.close()
    return nc, S


_CACHE = {}


def _consts():
    ident = np.eye(128, dtype=np.float32)
    bones = np.zeros((128, 128), np.float32)
    bones[:64, :64] = 1.0 / 64
    bones[64:, 64:] = 1.0 / 64
    key = np.arange(128)[:, None]
    q = np.arange(128)[None, :]
    dA = (q + 128 - key).astype(np.float32)
    mA = ((key // 64 == 1) | (q // 64 == 0)).astype(np.float32)
    dB = np.abs(q - key).astype(np.float32)
    mB = ((key // 64 == 0) | (q // 64 == 1)).astype(np.float32)
    return ident, bones, dA * mA, dB * mB, mA, mB


def kernel(x_prompt, x_sample, cache_k, cache_v, state_conv, state_rglru,
           a_norm, a_w_in, a_conv_w, a_conv_b, a_gate_a_w, a_gate_a_b, a_gate_x_w, a_gate_x_b,
           a_lambda, a_w_out, kv_norm, w_kv, k_norm, b_norm, b_w_in, q_norm, sinks, b_w_out, _seq=None):
    f = lambda a: np.ascontiguousarray(np.asarray(a, dtype=np.float32))
    x_prompt = f(x_prompt)
    seq = x_prompt.shape[1]
    if seq not in _CACHE:
        _CACHE[seq] = build(seq)[0]
    nc = _CACHE[seq]
    ident, bones, dA, dB, mA, mB = _consts()
    km = np.ones((128, 3), np.float32)
    km[NS:, 1] = 0.0
    nth = seq // 256
    npre = nth - 1
    x_sample = f(x_sample); cache_k = f(cache_k); cache_v = f(cache_v)
    state_conv = f(state_conv); state_rglru = f(state_rglru)
    r8 = lambda v: f(v).reshape(-1, 8, 128).reshape(-1, 128)
    n_cores = 8
    BP = x_prompt.shape[0]
    in_maps = []
    for c in range(n_cores):
        b = c % BP
        half = c // BP
        zt = np.zeros((128, D), np.float32)
        if half == 0:
            xpre_c = np.zeros((max(npre, 1) * 128, D), np.float32)
            xmain_c = np.concatenate([zt, x_prompt[b, :nth * 128]], 0)
        else:
            xpre_c = x_prompt[b, :npre * 128] if npre > 0 else zt
            xmain_c = x_prompt[b, npre * 128:]
        xpre_c = np.ascontiguousarray(xpre_c)
        xmain_c = np.ascontiguousarray(xmain_c)
        pvec = np.concatenate([
            r8(f(a_conv_w)[0]),
            r8(f(a_conv_b)[0][None]), r8(f(a_gate_a_b)[0][None]), r8(f(a_gate_x_b)[0][None]), r8(f(a_lambda)[0][None]),
            r8(state_conv[0, c]),
            r8(state_rglru[0, c][None]),
            r8(f(a_norm)[0][None]), r8(f(kv_norm)[None]), r8(f(b_norm)[0][None]),
        ], axis=0)
        assert pvec.shape == (120, 128)
        in_maps.append({
            "xpre": xpre_c, "xp": xmain_c, "flag": np.array([float(half)], np.float32), "xs": x_sample[c],
            "ck": cache_k[c].reshape(128, 128), "cv": cache_v[c].reshape(128, 128),
            "pvec": np.ascontiguousarray(pvec),
            "w_in_a": f(a_w_in)[0], "wga": f(a_gate_a_w)[0].reshape(D, 256), "wgx": f(a_gate_x_w)[0].reshape(D, 256),
            "w_out_a": f(a_w_out)[0], "w_kv": f(w_kv), "w_in_b": f(b_w_in)[0], "w_out_b": f(b_w_out)[0],
            "knorm": f(k_norm), "qnorm": f(q_norm)[0], "sinks": f(sinks)[0],
            "ident": ident, "bones": bones, "distA": dA, "distB": dB, "maskA": mA, "maskB": mB, "kmask": km,
        })
    res = run_bass_kernel_spmd(nc, in_maps, core_ids=list(range(n_cores)))
    R = res.results
    g = lambda c, k: np.asarray(R[c][k], dtype=np.float32)
    y_prompt = np.stack([np.concatenate([g(b, "yp"), g(b + BP, "yp")], 0) for b in range(BP)], 0)
    y_sample = np.stack([g(c, "ys") for c in range(n_cores)], 0)
    p_k = np.stack([g(b + BP, "pk").reshape(128, 2, HD) for b in range(BP)], 0)
    p_v = np.stack([g(b + BP, "pv").reshape(128, 2, HD) for b in range(BP)], 0)
    p_conv = np.stack([g(b + BP, "pconv") for b in range(BP)], 0)[None]
    p_h = np.stack([g(b + BP, "ph") for b in range(BP)], 0)[None]
    s_k = np.stack([g(c, "sk").reshape(128, 2, HD) for c in range(n_cores)], 0)
    s_v = np.stack([g(c, "sv").reshape(128, 2, HD) for c in range(n_cores)], 0)
    s_conv = np.stack([g(c, "sconv") for c in range(n_cores)], 0)[None]
    s_h = np.stack([g(c, "sh") for c in range(n_cores)], 0)[None]
    return (y_prompt, y_sample, p_k, p_v, p_conv, p_h, s_k, s_v, s_conv, s_h)
```

```python
import numpy as np
from contextlib import ExitStack
import concourse.bass as bass
import concourse.mybir as mybir
from concourse.bass_utils import run_bass_kernel_spmd

F32 = mybir.dt.float32
BF16 = mybir.dt.bfloat16
AF = mybir.ActivationFunctionType
ALU = mybir.AluOpType
AX = mybir.AxisListType

D = 1024
NCH = 8
SEQ = 4096
NS = 16
EPS = 1e-6
N_HEADS = 16
HD = 64


_DBG_STOP = None
_NLITE = 3


class _Stop(Exception):
    pass


def _ck(tag):
    if _DBG_STOP is not None and tag == _DBG_STOP:
        raise _Stop()


class Buf:
    __slots__ = ("name", "w", "r", "parts")

    def __init__(self, name, parts=None):
        self.name = name
        self.w = None
        self.r = {}
        self.parts = parts


def _expand(bufs):
    out = []
    for b in bufs:
        if b.parts is not None:
            out.extend(b.parts)
        else:
            out.append(b)
    return out


class _Proxy:
    def __init__(self, eng):
        self._e = eng
        self.sz = 128
        self.name = None

    def __getattr__(self, name):
        f = getattr(self._e, name)

        def w(*a, **kw):
            out = kw.get("out", a[0] if a else None)
            try:
                m = 1
                for d in out.shape[1:]:
                    m *= d
                self.sz = m
            except Exception:
                pass
            self.name = name
            return f(*a, **kw)
        return w


class SyncMgr:
    NSLOT = 8

    def __init__(self, nc, es):
        self.nc = nc
        self.eng = {"pe": nc.tensor, "act": nc.scalar, "dve": nc.vector, "pool": nc.gpsimd, "sp": nc.sync}
        self.sem = {k: es.enter_context(nc.semaphore("s_" + k)) for k in self.eng}
        self.cnt = {k: 0 for k in self.eng}
        self.waited = {k: {} for k in self.eng}
        self.dsem = {q: [es.enter_context(nc.semaphore("d_%s_%d" % (q, i))) for i in range(self.NSLOT)]
                     for q in ("sp", "pool", "act")}
        self.dcnt = {"sp": 0, "pool": 0, "act": 0}
        self.ninst = 0
        self.efree = {k: 0.0 for k in self.eng}
        self.tfin = {}
        self.step_fin = 0.0

    def _est(self, e, reads, writes, dur):
        reads, writes = _expand(reads), _expand(writes)
        t0 = self.efree[e]
        for b in reads:
            if b.w is not None:
                t0 = max(t0, self.tfin.get(b.w[0:1] + (b.w[2],), 0.0))
        for b in writes:
            if b.w is not None:
                t0 = max(t0, self.tfin.get(b.w[0:1] + (b.w[2],), 0.0))
            for r in b.r.values():
                t0 = max(t0, self.tfin.get(r[0:1] + (r[2],), 0.0))
        fin = t0 + dur
        self.efree[e] = fin if e != "sp" else t0 + 100.0
        self.step_fin = max(self.step_fin, fin)
        return fin

    def _waits(self, e, reads, writes):
        reads, writes = _expand(reads), _expand(writes)
        need = {}
        deps = []
        for b in reads:
            if b.w is not None:
                deps.append(b.w)
        for b in writes:
            if b.w is not None:
                deps.append(b.w)
            deps.extend(b.r.values())
        for key, sem, val in deps:
            if key == ("e", "pe") and e == "pe":
                continue
            if key not in need or need[key][1] < val:
                need[key] = (sem, val)
        for key, (sem, val) in need.items():
            if self.waited[e].get(key, -1) >= val:
                continue
            self.eng[e].wait_ge(sem, val)
            self.waited[e][key] = val

    def _record(self, tok, reads, writes):
        reads, writes = _expand(reads), _expand(writes)
        key = tok[0]
        for b in writes:
            b.w = tok
            b.r = {}
        for b in reads:
            if b in writes:
                continue
            old = b.r.get(key)
            if old is None or old[2] < tok[2]:
                b.r[key] = tok

    def op(self, e, reads, writes, emit, inc=True, sz=None, k=None):
        self._waits(e, reads, writes)
        px = _Proxy(self.eng[e])
        ins = emit(px)
        self.ninst += 1
        if sz is None:
            sz = px.sz
        if k is None:
            k = {"reciprocal": 2.1, "tensor_tensor_scan": 2.0, "tensor_scalar": 0.6, "tensor_copy": 0.6}.get(px.name, 1.0)
        if e == "pe":
            dur = max(62.0, sz / 2.4 + 6.0)
        elif e == "act":
            dur = 220.0 + 0.83 * sz
        elif e == "dve":
            dur = 165.0 + 1.04 * sz * k
        else:
            dur = 150.0 + 2.4 * sz
        fin = self._est(e, reads, writes, dur)
        if inc:
            ins.then_inc(self.sem[e], 1)
            self.cnt[e] += 1
            tok = (("e", e), self.sem[e], self.cnt[e])
        else:
            assert e == "pe"
            tok = (("e", e), self.sem[e], self.cnt[e] + 1)
        self.tfin[(tok[0], tok[2])] = max(fin, self.tfin.get((tok[0], tok[2]), 0.0))
        self._record(tok, reads, writes)

    def dma(self, q, out, in_, reads, writes, **kw):
        fin = self._est(q, reads, writes, 2500.0)
        self._waits(q, reads, writes)
        slot = self.dcnt[q] % self.NSLOT
        use = self.dcnt[q] // self.NSLOT
        key = ("d", q, slot)
        sem = self.dsem[q][slot]
        if use > 0 and self.waited[q].get(key, -1) < 16 * use:
            self.eng[q].wait_ge(sem, 16 * use)
            self.waited[q][key] = 16 * use
        self.eng[q].dma_start(out=out, in_=in_, **kw).then_inc(sem, 16)
        self.ninst += 1
        self.dcnt[q] += 1
        tok = (key, sem, 16 * (use + 1))
        self.tfin[(tok[0], tok[2])] = fin
        self._record(tok, reads, writes)

    def finish(self):
        for q in ("sp", "pool", "act"):
            for slot in range(self.NSLOT):
                uses = (self.dcnt[q] - slot + self.NSLOT - 1) // self.NSLOT
                if uses > 0:
                    key = ("d", q, slot)
                    if self.waited[q].get(key, -1) < 16 * uses:
                        self.eng[q].wait_ge(self.dsem[q][slot], 16 * uses)
                        self.waited[q][key] = 16 * uses


def build(seq=SEQ):
    assert seq % 128 == 0
    assert seq % 256 == 0
    NTH = seq // 256
    NPRE = NTH - 1
    NPRE_A = max(NPRE, 1)
    nc = bass.Bass("TRN2", target_bir_lowering=False)
    es = ExitStack()

    def din(name, shape):
        return nc.dram_tensor(name, list(shape), F32, kind="ExternalInput").ap()

    def dout(name, shape):
        return nc.dram_tensor(name, list(shape), F32, kind="ExternalOutput").ap()

    xpre = din("xpre", [NPRE_A * 128, D]); xp = din("xp", [(NTH + 1) * 128, D]); xs = din("xs", [NS, D])
    flag = din("flag", [1])
    ck = din("ck", [128, 128]); cv = din("cv", [128, 128])
    pvec = din("pvec", [120, 128])
    w_in_a = din("w_in_a", [D, 2 * D]); wga = din("wga", [D, 256]); wgx = din("wgx", [D, 256])
    w_out_a = din("w_out_a", [D, D]); w_kv = din("w_kv", [D, 256])
    w_in_b = din("w_in_b", [D, 2 * D]); w_out_b = din("w_out_b", [D, D])
    knorm = din("knorm", [HD]); qnorm = din("qnorm", [HD]); sinks = din("sinks", [N_HEADS])
    ident = din("ident", [128, 128]); bones = din("bones", [128, 128])
    distA = din("distA", [128, 128]); distB = din("distB", [128, 128])
    maskA = din("maskA", [128, 128]); maskB = din("maskB", [128, 128])
    kmask = din("kmask", [128, 3])

    yp = dout("yp", [NTH * 128, D]); ys = dout("ys", [NS, D])
    pk = dout("pk", [128, 128]); pv = dout("pv", [128, 128])
    pconv = dout("pconv", [3, D]); ph = dout("ph", [D])
    sk = dout("sk", [128, 128]); sv = dout("sv", [128, 128])
    sconv = dout("sconv", [3, D]); sh = dout("sh", [D])

    S = SyncMgr(nc, es)

    def sb(name, shape, dt=F32):
        t = es.enter_context(nc.sbuf_tensor(name, list(shape), dt))
        return t, Buf(name)

    def ps(name, shape, dt=F32):
        t = es.enter_context(nc.psum_tensor(name, list(shape), dt))
        return t, Buf(name)

    WinA, bWinA = sb("WinA", [128, NCH, 2 * D], BF16)
    WGA, bWGA = sb("WGA", [128, NCH, 256], BF16)
    WGX, bWGX = sb("WGX", [128, NCH, 256], BF16)
    WoutA, bWoutA = sb("WoutA", [128, NCH, D], BF16)
    Wkv, bWkv = sb("Wkv", [128, NCH, 256], BF16)
    WinB, bWinB = sb("WinB", [128, NCH, 2 * D], BF16)
    WoutB, bWoutB = sb("WoutB", [128, NCH, D], BF16)
    IDF, bIDF = sb("IDF", [128, 128], F32)
    IDB, bIDB = sb("IDB", [128, 128], BF16)
    BON, bBON = sb("BON", [128, 128], BF16)
    EAB, bEAB = sb("EAB", [128, 2, N_HEADS, 128], BF16)
    bEA = bEB = bEAB
    PVT, bPVT = sb("PVT", [128, 120], F32)
    C8, bC8 = sb("C8", [128, NCH], F32)
    CH, bCH = sb("CH", [128, NCH], F32)
    BAH, bBAH = sb("BAH", [128, NCH], F32)
    BXH, bBXH = sb("BXH", [128, NCH], F32)
    GK, bGK = sb("GK", [128, HD], F32)
    GQ8, bGQ8 = sb("GQ8", [128, HD], F32)
    MNEG, bMNEG = sb("MNEG", [128, 1], F32)
    SINKE, bSINKE = sb("SINKE", [128, N_HEADS], F32)
    CONSTS, bCONSTS = sb("CONSTS", [128, 4], F32)
    SMALL, bSMALL = sb("SMALL", [128, 64], F32)
    SMA, bSMA = sb("SMA", [128, 8], F32)
    SMB, bSMB = sb("SMB", [128, 8], F32)
    SMK, bSMK = sb("SMK", [128, 8], F32)
    SMO = [sb("SMO%d" % i, [128, 8], F32) for i in range(2)]
    KM, bKM = sb("KM", [128, 3], F32)
    CST = [sb("CST%d" % i, [128, NCH, 4], F32) for i in range(2)]
    CTMP, bCTMP = sb("CTMP", [128, 3, 128], F32)

    EPS_T, bEPS = sb("EPS_T", [128, 1], F32)
    TA = [sb("TA%d" % i, [128, D], F32) for i in range(6)]
    TB = [sb("TB%d" % i, [128, D], F32) for i in range(4)]
    BA_ = [sb("BA%d" % i, [128, D], BF16) for i in range(4)]
    BB_ = [sb("BB%d" % i, [128, D], BF16) for i in range(6)]
    for t_ in (TA[2], TB[1]):
        t_[1].parts = [Buf(t_[1].name + "_lo"), Buf(t_[1].name + "_hi")]
    X1 = [sb("X1_%d" % i, [128, D], F32) for i in range(3)]
    XBS = [sb("XB%d" % i, [128, NCH, 3 + 128], F32) for i in range(2)]
    HC, bHC = sb("HC", [128, NCH], F32)
    K2, bK2 = sb("K2", [128, 2, 2, 2, 128], BF16)
    VA, bVA = sb("VA", [128, 2, 2, 65], BF16)
    K2S, bK2S = sb("K2S", [128, 2, 2, 2, 128], BF16)
    VAS, bVAS = sb("VAS", [128, 2, 2, 65], BF16)
    KOUT, bKOUT = sb("KOUT", [128, 128], F32)
    VOUT, bVOUT = sb("VOUT", [128, 128], F32)
    KD, bKD = sb("KD", [128, 2, 2, HD], BF16)
    bWinA2 = Buf("WinA_gate")
    bK2s = [Buf("K2_0"), Buf("K2_1")]
    bVAs = [Buf("VA_0"), Buf("VA_1")]
    bK2Ss = [Buf("K2S_0"), Buf("K2S_1")]
    bVASs = [Buf("VAS_0"), Buf("VAS_1")]

    PT0, bPT0 = ps("PT0", [128, D], BF16)
    PA, bPA = ps("PA", [128, D], F32)
    PB, bPB = ps("PB", [128, D], F32)
    PC, bPC = ps("PC", [128, D], F32)
    PD, bPD = ps("PD", [128, 512], F32)
    bPC0, bPC1 = Buf("PC0"), Buf("PC1")

    op = S.op
    T = TA

    def v3(t, n=128):
        return t[:, :].rearrange("p (c t) -> p c t", c=NCH)[:, :, 0:n]

    S.dma("sp", IDF[:, :], ident, [], [bIDF])
    S.dma("sp", T[0][0][:, 0:128], bones, [], [T[0][1]])
    S.dma("sp", T[1][0][:120, 0:128], pvec, [], [T[1][1]])
    S.dma("sp", GK[:, :], knorm.partition_broadcast(128), [], [bGK])
    S.dma("sp", GQ8[:, :], qnorm.partition_broadcast(128), [], [bGQ8])
    S.dma("sp", SINKE[:, :], sinks.partition_broadcast(128), [], [bSINKE])
    S.dma("sp", T[2][0][:, 0:128], distA, [], [T[2][1]])
    S.dma("sp", T[2][0][:, 128:256], distB, [], [T[2][1]])
    S.dma("sp", T[2][0][:, 256:384], maskA, [], [T[2][1]])
    S.dma("sp", T[2][0][:, 384:512], maskB, [], [T[2][1]])
    S.dma("sp", KM[:, :], kmask, [], [bKM])
    S.dma("sp", KM[:, 2:3], flag.partition_broadcast(128), [], [bKM])

    op("dve", [bIDF], [bIDB], lambda e: e.tensor_copy(IDB[:, :], IDF[:, :]))
    op("dve", [T[0][1]], [bBON], lambda e: e.tensor_copy(BON[:, :], T[0][0][:, 0:128]))
    op("dve", [], [bCONSTS], lambda e: e.memset(CONSTS[:, 0:1], 0.5))
    op("dve", [], [bCONSTS], lambda e: e.memset(CONSTS[:, 1:2], -0.5))
    op("dve", [], [bCONSTS], lambda e: e.memset(CONSTS[:, 2:3], 0.25))
    op("dve", [], [bEPS], lambda e: e.memset(EPS_T[:, 0:1], EPS))
    op("pe", [T[1][1], bIDF], [bPD],
       lambda e: e.transpose(PD[:, 0:120], T[1][0][:120, 0:128], IDF[:120, :120]))
    op("dve", [bPD], [bPVT], lambda e: e.tensor_copy(PVT[:, :], PD[:, 0:120]))
    op("dve", [bPVT], [bBAH], lambda e: e.tensor_scalar(BAH[:, :], PVT[:, 40:48], 0.5, None, ALU.mult))
    op("dve", [bPVT], [bBXH], lambda e: e.tensor_scalar(BXH[:, :], PVT[:, 48:56], 0.5, None, ALU.mult))
    sm = SMALL
    LAM = PVT[:, 56:64]
    op("act", [bPVT], [bSMALL], lambda e: e.activation(sm[:, 0:8], LAM, AF.Abs))
    op("act", [bSMALL], [bSMALL], lambda e: e.activation(sm[:, 8:16], sm[:, 0:8], AF.Exp, scale=-1.0))
    op("dve", [bSMALL], [bSMALL], lambda e: e.tensor_scalar(sm[:, 16:24], sm[:, 8:16], 2.0, None, ALU.add))
    op("dve", [bSMALL], [bSMALL], lambda e: e.reciprocal(sm[:, 16:24], sm[:, 16:24]))
    op("dve", [bSMALL], [bSMALL], lambda e: e.tensor_tensor(sm[:, 24:32], sm[:, 8:16], sm[:, 16:24], ALU.mult))
    op("dve", [bSMALL], [bSMALL], lambda e: e.tensor_tensor(sm[:, 32:40], sm[:, 24:32], sm[:, 24:32], ALU.mult))
    op("dve", [], [bSMALL], lambda e: e.memset(sm[:, 40:48], 1.0 / 19.0))
    for kk in (17, 15, 13, 11, 9, 7, 5, 3, 1):
        op("dve", [bSMALL], [bSMALL], lambda e: e.tensor_tensor(sm[:, 40:48], sm[:, 40:48], sm[:, 32:40], ALU.mult))
        op("dve", [bSMALL], [bSMALL],
           lambda e, kk=kk: e.tensor_scalar(sm[:, 40:48], sm[:, 40:48], 1.0 / kk, None, ALU.add))
    op("dve", [bSMALL], [bSMALL], lambda e: e.tensor_tensor(sm[:, 40:48], sm[:, 40:48], sm[:, 24:32], ALU.mult))
    op("dve", [bPVT, bSMALL], [bSMALL],
       lambda e: e.tensor_scalar(sm[:, 48:56], LAM, -1.0, 0.0, ALU.mult, ALU.max))
    op("dve", [bSMALL], [bSMALL],
       lambda e: e.scalar_tensor_tensor(sm[:, 48:56], sm[:, 40:48], 2.0, sm[:, 48:56], ALU.mult, ALU.add))
    op("dve", [bSMALL], [bC8], lambda e: e.tensor_scalar(C8[:, :], sm[:, 48:56], -8.0, None, ALU.mult))
    op("dve", [bSMALL], [bCH], lambda e: e.tensor_scalar(CH[:, :], sm[:, 48:56], -4.0, None, ALU.mult))

    op("dve", [bGK, bGQ8], [bSMALL], lambda e: e.tensor_tensor(sm[:, 0:64], GK[:, :], GQ8[:, :], ALU.mult))
    op("act", [bSMALL], [bSMALL], lambda e: e.activation(sm[:, 0:64], sm[:, 0:64], AF.Abs))
    op("dve", [bSMALL], [bMNEG], lambda e: e.tensor_reduce(MNEG[:, 0:1], sm[:, 0:64], AX.X, ALU.max))
    op("dve", [bMNEG], [bMNEG], lambda e: e.tensor_scalar(MNEG[:, 0:1], MNEG[:, 0:1], -8.0, None, ALU.mult))
    op("dve", [bGQ8], [bGQ8], lambda e: e.tensor_scalar(GQ8[:, :], GQ8[:, :], 0.125, None, ALU.mult))
    op("act", [bSINKE, bMNEG], [bSINKE],
       lambda e: e.activation(SINKE[:, :], SINKE[:, :], AF.Exp, bias=MNEG[:, 0:1]))

    for (E_, bE_, dcol, mcol, Tt) in ((EAB[:, 0], bEA, 0, 256, T[3]), (EAB[:, 1], bEB, 128, 384, T[4])):
        for h0 in range(0, N_HEADS, 8):
            for h in range(h0, h0 + 8):
                slope = float(2.0 ** (-8.0 * (h + 1) / N_HEADS))
                op("act", [T[2][1]], [Tt[1]],
                   lambda e, h=h, h0=h0, slope=slope, Tt=Tt, dcol=dcol: e.activation(
                       v3(Tt[0])[:, h - h0, :], T[2][0][:, dcol:dcol + 128], AF.Exp, scale=-slope))
            op("dve", [T[2][1], Tt[1]], [bE_],
               lambda e, E_=E_, h0=h0, Tt=Tt, mcol=mcol: e.tensor_tensor(
                   E_[:, h0:h0 + 8, :], v3(Tt[0]),
                   T[2][0][:, mcol:mcol + 128].unsqueeze(1).to_broadcast([128, 8, 128]), ALU.mult))

    stg_i = [0]
    STG = [TB[0], TB[1], TB[2], TB[3], X1[0], X1[1], X1[2]]
    QS = ["sp", "act"]

    def load_weight(dst, bdst, src_ap, ncols_total, gain_col=None):
        for c0 in range(0, ncols_total, 1024):
            ww = min(ncols_total, c0 + 1024) - c0
            tt = STG[stg_i[0] % len(STG)]
            q = QS[stg_i[0] % len(QS)]
            stg_i[0] += 1
            S.dma(q, tt[0][:, 0:ww], src_ap[:, c0:c0 + ww], [], [tt[1]])
            if gain_col is None:
                op("dve", [tt[1]], [bdst],
                   lambda e, tt=tt, ww=ww, c0=c0: e.tensor_copy(dst[:, c0:c0 + ww], tt[0][:, 0:ww]))
            else:
                op("dve", [tt[1], bPVT], [bdst],
                   lambda e, tt=tt, ww=ww, c0=c0: e.tensor_scalar(
                       dst[:, c0:c0 + ww], tt[0][:, 0:ww], PVT[:, gain_col:gain_col + 1], None, ALU.mult))

    for kc in range(NCH):
        load_weight(WinA[:, kc, 0:D], bWinA, w_in_a[kc * 128:(kc + 1) * 128, 0:D], D, gain_col=96 + kc)
    for kc in range(NCH):
        load_weight(WGA[:, kc, :], bWGA, wga[kc * 128:(kc + 1) * 128, :], 256)
        load_weight(WGX[:, kc, :], bWGX, wgx[kc * 128:(kc + 1) * 128, :], 256)

    def deferred_loads():
        stg, bstg = TA[1]
        jobs = []
        for kc in range(NCH):
            jobs.append((WinA[:, kc, D:2 * D], bWinA2, w_in_a[kc * 128:(kc + 1) * 128, D:2 * D], D, 96 + kc))
        for kc in range(NCH):
            jobs.append((WoutA[:, kc, :], bWoutA, w_out_a[kc * 128:(kc + 1) * 128, :], D, None))
        for kc in range(NCH):
            jobs.append((Wkv[:, kc, :], bWkv, w_kv[kc * 128:(kc + 1) * 128, :], 256, 104 + kc))
        for kc in range(NCH):
            pass
        for kc in range(NCH):
            jobs.append((WoutB[:, kc, :], bWoutB, w_out_b[kc * 128:(kc + 1) * 128, :], D, None))
        for (dst, bdst, src, w, gcol) in jobs:
            S.dma("sp", stg[:, 0:w], src, [], [bstg])
            if gcol is None:
                op("dve", [bstg], [bdst], lambda e, dst=dst, w=w: e.tensor_copy(dst, stg[:, 0:w]))
            else:
                op("dve", [bstg, bPVT], [bdst],
                   lambda e, dst=dst, w=w, gcol=gcol: e.tensor_scalar(dst, stg[:, 0:w], PVT[:, gcol:gcol + 1], None, ALU.mult))
            yield

    def deferred_winb():
        k_ = 0
        for kc in range(NCH):
            for c0 in (0, D):
                stg, bstg = TB[2 + k_ % 2]
                k_ += 1
                S.dma("sp", stg[:, 0:D], w_in_b[kc * 128:(kc + 1) * 128, c0:c0 + D], [], [bstg])
                op("dve", [bstg, bPVT], [bWinB],
                   lambda e, kc=kc, c0=c0, stg=stg: e.tensor_scalar(WinB[:, kc, c0:c0 + D], stg[:, 0:D],
                                                                    PVT[:, 112 + kc:113 + kc], None, ALU.mult))
                yield

    op("dve", [], [XBS[0][1]], lambda e: e.memset(XBS[0][0][:, :, 0:3], 0.0))
    op("dve", [], [XBS[1][1]], lambda e: e.memset(XBS[1][0][:, :, 0:3], 0.0))
    op("dve", [], [bHC], lambda e: e.memset(HC[:, :], 0.0))
    op("dve", [], [bK2s[0], bK2s[1]], lambda e: e.memset(K2[:, :, :, :, :].rearrange("p a b c d -> p (a b c d)"), 0.0))
    op("dve", [], [bK2Ss[0], bK2Ss[1]], lambda e: e.memset(K2S[:, :, :, :, :].rearrange("p a b c d -> p (a b c d)"), 0.0))
    op("dve", [], [bVAs[1]], lambda e: e.memset(VA[:, 1, :, :], 0.0))

    n = 128

    def rms_to_T(Xsrc, bXsrc, SM_, bSM, XNb, XTb, evac_dve=False):
        XN, bXN = XNb
        XT_, bXT = XTb
        op("act", [bXsrc], [bXN, bSM],
           lambda e: e.activation(XN[:, :], Xsrc[:, :], AF.Square, accum_out=SM_[:, 0:1]))
        op("dve", [bSM], [bSM],
           lambda e: e.tensor_scalar(SM_[:, 1:2], SM_[:, 0:1], 1.0 / D, EPS, ALU.mult, ALU.add))
        op("pool", [bSM, bCONSTS], [bSM],
           lambda e: e.tensor_tensor(SM_[:, 2:3], SM_[:, 1:2], CONSTS[:, 1:2], ALU.pow))
        op("dve", [bXsrc, bSM], [bXN],
           lambda e: e.tensor_scalar(XN[:, :], Xsrc[:, :], SM_[:, 2:3], None, ALU.mult))
        yield
        for kc in range(NCH):
            op("pe", [bXN, bIDB], [bPT0],
               lambda e, kc=kc: e.transpose(PT0[:, kc * 128:(kc + 1) * 128], XN[:, kc * 128:(kc + 1) * 128], IDB[:, :]),
               inc=(kc == NCH - 1))
        if evac_dve:
            op("dve", [bPT0], [bXT], lambda e: e.tensor_copy(XT_[:, :], PT0[:, :]))
        else:
            op("act", [bPT0], [bXT], lambda e: e.activation(XT_[:, :], PT0[:, :], AF.Copy))

    def k_to_k2(Ksrc, bKsrc, K2t, bK2slot, slot):
        op("dve", [bKsrc, bGQ8], [bKD],
           lambda e: e.tensor_tensor(
               KD[:, :, :, :],
               Ksrc[:, :].rearrange("p (k d) -> p k d", k=2).unsqueeze(2).to_broadcast([n, 2, 2, HD]),
               GQ8[:, :].unsqueeze(1).unsqueeze(1).to_broadcast([n, 2, 2, HD]), ALU.mult))
        for k in range(2):
            op("pe", [bKD, bIDB], [bPT0],
               lambda e, k=k: e.transpose(PT0[:, k * 128:(k + 1) * 128],
                                          KD[:, k, :, :].rearrange("p a d -> p (a d)"), IDB[:, :]),
               inc=(k == 1))
        for par in range(2):
            op("act", [bPT0], [bK2slot],
               lambda e, par=par: e.activation(
                   K2t[par * 64:(par + 1) * 64, slot, :, par, :],
                   PT0[par * 64:(par + 1) * 64, 0:256].rearrange("p (k t) -> p k t", k=2), AF.Copy))

    BS0 = dict(T0=TA[0], SG=TA[1], XC=TA[2], TI=TA[3], LA=TA[4], A2=TA[5], XN=BA_[0], XT=BA_[1], XCb=BA_[2], HG=BA_[3],
               SM=(SMA, bSMA), X=X1[2], P0=(PA, bPA), P1=(PB, bPB), XB=XBS[0], XBn=XBS[1])
    BS1 = dict(T0=TB[0], SG=None, XC=TB[1], TI=TB[2], LA=TB[3], A2=X1[0], XN=BB_[0], XT=BB_[1], XCb=BB_[2], HG=None,
               SM=(SMB, bSMB), X=X1[1], P0=(PC, bPC), P1=(PC, bPC), XB=XBS[1], XBn=XBS[0])
    R32 = WinB[:, :, :].rearrange("p a b -> p (a b)").bitcast(F32)
    def r32(i, name):
        return (R32[:, i * D:(i + 1) * D], Buf(name))
    R_ = [r32(i, "R32_%d" % i) for i in range(6)]
    XB3 = (R32[:, 6 * D:6 * D + NCH * 131].rearrange("p (c t) -> p c t", c=NCH), Buf("XB3"))
    R_[1][1].parts = [Buf("R32_1_lo"), Buf("R32_1_hi")]
    bWinB.parts = [b for (_, b) in R_[:1]] + R_[1][1].parts + [b for (_, b) in R_[2:]] + [XB3[1], Buf("WinB_rest")]
    BS2 = dict(T0=R_[0], SG=None, XC=R_[1], TI=R_[2], LA=R_[3], A2=R_[4], XN=BB_[3], XT=BB_[4], XCb=BB_[5], HG=None,
               SM=(SMK, bSMK), X=R_[5], P0=(PB, bPB), P1=(PB, bPB), XB=XB3, XBn=XBS[0])
    BS0P = dict(BS0, P1=(PA, bPA))
    XBS3 = [XBS[0], XBS[1], XB3]
    op("dve", [], [XB3[1]], lambda e: e.memset(XB3[0][:, :, 0:3], 0.0))

    EVENTS = {}

    def phaseA(tp):
        nv = tp["nv"]
        bs = tp["bs"]
        lite, mask = tp["lite"], tp["mask"]
        Xt, bXt = tp.get("Xbuf") or bs["X"]
        XBt, bXB = tp.get("XB") or bs["XB"]
        P0, bP0 = bs["P0"]
        P1, bP1 = bs["P1"]
        SM_, bSM = bs["SM"]
        def load_x(tq):
            Xq, bXq = tq.get("Xbuf") or tq["bs"]["X"]
            if tq["nv"] < n:
                op("dve", [], [bXq], lambda e: e.memset(Xq[:, :], 0.0))
            S.dma("sp", Xq[:tq["nv"], :], tq["x_src"], [], [bXq])
            tq["preloaded"] = True
        if not tp.get("preloaded"):
            load_x(tp)
        nxt = tp.get("prefetch")
        if nxt is not None and nxt.get("Xbuf") is not None:
            load_x(nxt)
        if tp["hist_init"] is not None:
            tp["hist_init"](XBt, bXB)
        yield
        XT_, bXT = bs["XT"]
        g_ = rms_to_T(Xt, bXt, SM_, bSM, bs["XN"], bs["XT"])
        next(g_)
        if nxt is not None and nxt.get("Xbuf") is None and lite:
            load_x(nxt)
        yield
        for _ in g_:
            yield
        XT3 = v3(XT_)
        yield
        for oc in range(NCH):
            for kc in range(NCH):
                op("pe", [bXT, bWinA], [bP0],
                   lambda e, oc=oc, kc=kc: e.matmul(v3(P0)[:, oc, :], WinA[:, kc, oc * 128:(oc + 1) * 128], XT3[:, kc, :],
                                                    start=(kc == 0), stop=(kc == NCH - 1)),
                   inc=(oc == NCH - 1 and kc == NCH - 1))
            if oc % 2 == 1 and oc != NCH - 1:
                yield
        op("act", [bP0], [bXB], lambda e: e.activation(XBt[:, :, 3:3 + n], v3(P0), AF.Copy))
        XBn, bXBn = tp.get("XBn") or bs["XBn"]
        while tp.get("dep2") is not None and not EVENTS.get(("conv", tp["dep2"])):
            yield "blocked"
        if XBn is not XBt:
            op("act", [bXB], [bXBn], lambda e: e.activation(XBn[:, :, 0:3], XBt[:, :, n:n + 3], AF.Copy))
            EVENTS[("hist", tp["idx"])] = True
        yield
        while tp["dep"] is not None and not EVENTS.get(("hist", tp["dep"])):
            yield "blocked"
        if not lite:
            for oc in range(NCH):
                for kc in range(NCH):
                    op("pe", [bXT, bWinA2], [bP1],
                       lambda e, oc=oc, kc=kc: e.matmul(v3(P1)[:, oc, :], WinA[:, kc, D + oc * 128:D + (oc + 1) * 128],
                                                        XT3[:, kc, :], start=(kc == 0), stop=(kc == NCH - 1)),
                       inc=(oc == NCH - 1 and kc == NCH - 1))
                if oc % 2 == 1 and oc != NCH - 1:
                    yield
            TG, bTG = bs["T0"]
            SG, bSG = bs["SG"]
            op("act", [bP1], [bTG], lambda e: e.activation(TG[:, :], P1[:, :], AF.Tanh, scale=0.5))
            op("dve", [bTG, bP1], [bSG],
               lambda e: e.scalar_tensor_tensor(SG[:, :], TG[:, :], 1.0, P1[:, :], ALU.add, ALU.mult))
            yield
        XC, bXC = bs["XC"]
        XC3 = v3(XC)
        NDC = 5
        bXCl, bXCh = bXC.parts
        for c in range(NDC):
            op("dve", [bXB, bPVT], [bXCl],
               lambda e, c=c: e.tensor_scalar(XC3[:, c, :], XBt[:, c, 0:n], PVT[:, c:c + 1], PVT[:, 32 + c:33 + c],
                                              ALU.mult, ALU.add))
        yield

        def wbc(col):
            return PVT[:, col + NDC:col + NCH].unsqueeze(2).to_broadcast([128, NCH - NDC, n])
        op("pool", [bXB, bPVT], [bXCh],
           lambda e: e.tensor_tensor(XC3[:, NDC:NCH, :], XBt[:, NDC:NCH, 0:n], wbc(0), ALU.mult))
        op("pool", [bPVT, bXCh], [bXCh],
           lambda e: e.tensor_tensor(XC3[:, NDC:NCH, :], XC3[:, NDC:NCH, :], wbc(32), ALU.add))
        yield
        for k in range(1, 4):
            for c in range(NDC):
                op("dve", [bXB, bPVT, bXCl], [bXCl],
                   lambda e, c=c, k=k: e.scalar_tensor_tensor(
                       XC3[:, c, :], XBt[:, c, k:k + n], PVT[:, k * 8 + c:k * 8 + c + 1], XC3[:, c, :], ALU.mult, ALU.add))
                if c == 2:
                    yield
            op("pool", [bXB, bPVT], [bCTMP],
               lambda e, k=k: e.tensor_tensor(CTMP[:, :, :], XBt[:, NDC:NCH, k:k + n], wbc(k * 8), ALU.mult))
            op("pool", [bCTMP, bXCh], [bXCh],
               lambda e: e.tensor_tensor(XC3[:, NDC:NCH, :], XC3[:, NDC:NCH, :], CTMP[:, :, :], ALU.add))
            yield
        EVENTS[("conv", tp["idx"])] = True
        so = tp["state_out"]
        if so is not None:
            CSt, bCS = so["stage"]
            op("act", [bXB], [bCS], lambda e: e.activation(CSt[:, :, 0:3], XBt[:, :, nv:nv + 3], AF.Copy))
            for j in range(3):
                S.dma("sp", so["conv"][j].rearrange("(c p) -> p c", p=128), CSt[:, :, j],
                      [bCS], [], allow_slow_non_contiguous=True)
        if XBn is XBt:
            op("act", [bXB], [bXB], lambda e: e.activation(XBt[:, :, 0:3], XBt[:, :, n:n + 3], AF.Copy))
        XCb, bXCb = bs["XCb"]
        op("act", [bXC], [bXCb], lambda e: e.activation(XCb[:, :], XC[:, :], AF.Copy))
        XCb3 = v3(XCb)
        yield
        TR, bTR = bs["T0"]
        TI, bTI = bs["TI"]
        LA, bLA = bs["LA"]
        A2, bA2 = bs["A2"]
        for (P_, bP, Wg, bWg, TT, bTT, BH, bBH) in ((P0, bP0, WGA, bWGA, TR, bTR, BAH, bBAH),
                                                   (P1, bP1, WGX, bWGX, TI, bTI, BXH, bBXH)):
            for blk in range(4):
                for oc in range(2):
                    for kc in range(2):
                        last = (blk == 3 and oc == 1 and kc == 1)
                        op("pe", [bXCb, bWg], [bP],
                           lambda e, P_=P_, Wg=Wg, blk=blk, oc=oc, kc=kc: e.matmul(
                               v3(P_)[:, blk * 2 + oc, :], Wg[:, blk * 2 + kc, oc * 128:(oc + 1) * 128],
                               XCb3[:, blk * 2 + kc, :], start=(kc == 0), stop=(kc == 1)),
                           inc=last)
                if blk == 1:
                    yield
            yield
            for c in range(NCH):
                op("act", [bP, bBH], [bTT],
                   lambda e, c=c, TT=TT, P_=P_, BH=BH: e.activation(v3(TT)[:, c, :], v3(P_)[:, c, :], AF.Tanh,
                                                                    bias=BH[:, c:c + 1], scale=0.5))
                if c == 3:
                    yield
            yield
        op("dve", [bTR, bCH], [bLA],
           lambda e: e.scalar_tensor_tensor(v3(LA), v3(TR), 1.0, CH[:, :].unsqueeze(2).to_broadcast([128, NCH, n]),
                                            ALU.add, ALU.mult))
        yield
        op("act", [bLA], [bA2], lambda e: e.activation(A2[:, :], LA[:, :], AF.Exp, scale=2.0))
        yield
        op("act", [bLA], [bLA], lambda e: e.activation(LA[:, :], LA[:, :], AF.Exp))
        yield
        op("act", [bA2, bCONSTS], [bA2],
           lambda e: e.activation(A2[:, :], A2[:, :], AF.Sqrt, bias=CONSTS[:, 2:3], scale=-0.25))
        yield
        op("dve", [bTI, bXC], [bTI],
           lambda e: e.scalar_tensor_tensor(TI[:, :], TI[:, :], 1.0, XC[:, :], ALU.add, ALU.mult))
        yield
        if mask:
            op("dve", [bA2, bTI, bKM], [bTI],
               lambda e: e.scalar_tensor_tensor(TI[:, :], A2[:, :], KM[:, 2:3], TI[:, :], ALU.mult, ALU.mult))
        else:
            op("pool", [bA2, bTI], [bTI], lambda e: e.tensor_tensor(TI[:, :], A2[:, :], TI[:, :], ALU.mult))
        yield
        while tp["dep"] is not None and not EVENTS.get(("carry", tp["dep"])):
            yield "blocked"
        Hp_fn, bHp = tp["Hprev"]
        Hv = v3(TI)
        for c in range(NCH):
            op("dve", [bLA, bTI, bHp], [bTI],
               lambda e, c=c: e.tensor_tensor_scan(Hv[:, c, :], v3(LA)[:, c, :], Hv[:, c, :],
                                                   Hp_fn(c), ALU.mult, ALU.add))
            if c % 2 == 1:
                yield
        if so is not None:
            CSt, bCS = so["stage"]
            op("act", [bTI], [bCS], lambda e: e.activation(CSt[:, :, 3:4], Hv[:, :, nv - 1:nv], AF.Copy))
            S.dma("sp", so["h"].rearrange("(c p) -> p c", p=128), CSt[:, :, 3],
                  [bCS], [], allow_slow_non_contiguous=True)
        if tp["carry"]:
            op("dve", [bTI], [bHC], lambda e: e.tensor_copy(HC[:, :], Hv[:, :, n - 1]))
        EVENTS[("carry", tp["idx"])] = True
        if lite:
            return
        X1c, bX1c = tp["X1"]
        HG, bHG = bs["HG"]
        SG, bSG = bs["SG"]
        op("dve", [bTI, bSG], [bHG],
           lambda e: e.scalar_tensor_tensor(v3(HG), Hv, 0.5, v3(SG), ALU.mult, ALU.mult))
        HG3 = v3(HG)
        yield
        for half in range(2):
            for kc in range(NCH):
                op("pe", [bHG, bWoutA], [bP0],
                   lambda e, half=half, kc=kc: e.matmul(P0[:, half * 512:(half + 1) * 512], HG3[:, kc, :],
                                                        WoutA[:, kc, half * 512:(half + 1) * 512],
                                                        start=(kc == 0), stop=(kc == NCH - 1)),
                   inc=(half == 1 and kc == NCH - 1))
            yield
        op("dve", [bXt, bP0], [bX1c], lambda e: e.tensor_tensor(X1c[:, :], Xt[:, :], P0[:, :], ALU.add))
        yield

    def phaseB(tp):
        nv = tp["nv"]
        X1c, bX1c = tp["X1"]
        K2t, bK2l, VAt, bVAl = tp["K2"], tp["bK2"], tp["VA"], tp["bVA"]
        cur, prev, kmcol = tp["cur"], tp["prev"], tp["kmcol"]
        so = tp["state_out"]
        yield from rms_to_T(X1c, bX1c, SMB, bSMB, BB_[0], BB_[1], evac_dve=True)
        X1T3 = v3(BB_[1][0])
        bX1T = BB_[1][1]
        yield
        for kc in range(NCH):
            op("pe", [bX1T, bWkv], [bPD],
               lambda e, kc=kc: e.matmul(PD[:, 0:256], X1T3[:, kc, :], Wkv[:, kc, :], start=(kc == 0), stop=(kc == NCH - 1)),
               inc=(kc == NCH - 1))
        Ksq, bKsq = TB[0]
        op("dve", [bPD], [bKsq], lambda e: e.tensor_copy(Ksq[:, 0:256], PD[:, 0:256]))
        op("dve", [bKsq], [bKsq], lambda e: e.tensor_tensor(Ksq[:, 256:384], Ksq[:, 0:128], Ksq[:, 0:128], ALU.mult))
        op("dve", [bKsq], [bSMK],
           lambda e: e.tensor_reduce(SMK[:, 0:2], Ksq[:, 256:384].rearrange("p (k d) -> p k d", k=2), AX.X, ALU.add))
        op("dve", [bSMK], [bSMK],
           lambda e: e.tensor_scalar(SMK[:, 0:2], SMK[:, 0:2], 1.0 / HD, EPS, ALU.mult, ALU.add))
        op("pool", [bSMK, bCONSTS], [bSMK],
           lambda e: e.tensor_tensor(SMK[:, 2:4], SMK[:, 0:2], CONSTS[:, 1:2].to_broadcast([n, 2]), ALU.pow))
        op("dve", [bKsq, bSMK], [bKsq],
           lambda e: e.tensor_tensor(Ksq[:, 384:512].rearrange("p (k d) -> p k d", k=2),
                                     Ksq[:, 0:128].rearrange("p (k d) -> p k d", k=2),
                                     SMK[:, 2:4].unsqueeze(2).to_broadcast([n, 2, HD]), ALU.mult))
        op("dve", [bKsq, bGK], [bKOUT],
           lambda e: e.tensor_tensor(KOUT[:, :].rearrange("p (k d) -> p k d", k=2),
                                     Ksq[:, 384:512].rearrange("p (k d) -> p k d", k=2),
                                     GK[:, :].unsqueeze(1).to_broadcast([n, 2, HD]), ALU.mult))
        yield

        def k_finish():
            k_to_k2(KOUT, bKOUT, K2t, bK2l[cur], cur)
            op("dve", [bKsq, bKM], [bVAl[cur]],
               lambda e: e.tensor_scalar(VAt[:, cur, :, 0:HD], Ksq[:, 128:256].rearrange("p (k d) -> p k d", k=2),
                                         KM[:, kmcol:kmcol + 1], None, ALU.mult))
            op("dve", [bKM], [bVAl[cur]],
               lambda e: e.tensor_copy(VAt[:, cur, :, HD:HD + 1], KM[:, kmcol:kmcol + 1].unsqueeze(1).to_broadcast([n, 2, 1])))
            if so is not None:
                op("dve", [bKsq], [bVOUT], lambda e: e.tensor_copy(VOUT[:, :], Ksq[:, 128:256]))
                S.dma("sp", so["k"], KOUT[:nv, :], [bKOUT], [])
                S.dma("sp", so["v"], VOUT[:nv, :], [bVOUT], [])
        if tp["kvonly"]:
            k_finish()
            yield
            return
        for oc in range(NCH):
            for kc in range(NCH):
                op("pe", [bX1T, bWinB], [bPC, bPC0, bPC1],
                   lambda e, oc=oc, kc=kc: e.matmul(v3(PC)[:, oc, :], WinB[:, kc, oc * 128:(oc + 1) * 128], X1T3[:, kc, :],
                                                    start=(kc == 0), stop=(kc == NCH - 1)),
                   inc=(oc == NCH - 1 and kc == NCH - 1))
            if oc % 2 == 1:
                yield
        SQ, bSQ = BB_[0]
        op("act", [bPC], [bSQ], lambda e: e.activation(SQ[:, :], PC[:, :], AF.Square))
        yield
        k_finish()
        yield
        RQ, bRQ = TB[0]
        for c0 in range(0, NCH, 4):
            op("pe", [bSQ, bBON], [bPD],
               lambda e, c0=c0: e.matmul(PD[:, :], BON[:, :], SQ[:, c0 * 128:(c0 + 4) * 128], start=True, stop=True))
            op("act", [bPD, bEPS], [bRQ],
               lambda e, c0=c0: e.activation(RQ[:, c0 * 128:(c0 + 4) * 128], PD[:, :], AF.Ln, bias=EPS_T[:, 0:1]))
        yield
        op("act", [bRQ], [bRQ], lambda e: e.activation(RQ[:, :], RQ[:, :], AF.Exp, scale=-0.5))
        yield
        QN, bQN = BB_[2]
        op("dve", [bPC, bRQ], [bQN], lambda e: e.tensor_tensor(QN[:, :], PC[:, :], RQ[:, :], ALU.mult))
        yield
        for half in range(2):
            for kc in range(NCH):
                op("pe", [bX1T, bWinB], [bPC, bPC0, bPC1],
                   lambda e, half=half, kc=kc: e.matmul(PC[:, half * 512:(half + 1) * 512], X1T3[:, kc, :],
                                                        WinB[:, kc, D + half * 512:D + (half + 1) * 512],
                                                        start=(kc == 0), stop=(kc == NCH - 1)),
                   inc=(half == 1 and kc == NCH - 1))
            yield
        SGB, bSGB = TB[1]
        op("act", [bPC], [bSGB], lambda e: e.activation(SGB[:, :], PC[:, :], AF.Tanh, scale=0.5))
        op("dve", [bSGB, bPC], [bSGB, bPC0, bPC1],
           lambda e: e.scalar_tensor_tensor(SGB[:, :], SGB[:, :], 1.0, PC[:, :], ALU.add, ALU.mult))
        yield
        OGb, bOGb = BB_[3]
        QN3 = v3(QN)
        for step in range(4):
            kvh, hf = step // 2, step % 2
            PeT, bPe = TB[2 + step % 2]
            PTt, bPTt = BB_[4 + step % 2]
            SMO_, bSMO = SMO[step % 2]
            blocks = ((prev, 0, bPC0), (cur, 1, bPC1))
            for (slot, bank, bBank) in blocks:
                for j in range(4):
                    hh = kvh * 8 + hf * 4 + j
                    oc, par = hh // 2, hh % 2
                    op("pe", [bK2l[slot], bQN], [bBank],
                       lambda e, slot=slot, bank=bank, j=j, oc=oc, par=par, kvh=kvh: e.matmul(
                           PC[:, bank * 512 + j * 128:bank * 512 + (j + 1) * 128], K2t[:, slot, kvh, par, :],
                           QN3[:, oc, :], start=True, stop=True),
                       inc=(j == 3))
            op("act", [bPC0, bPC1, bMNEG], [bPe],
               lambda e, PeT=PeT: e.activation(PeT[:, :], PC[:, :], AF.Exp, bias=MNEG[:, 0:1]))
            h0 = kvh * 8 + hf * 4
            op("pool", [bPe, bEAB], [bPTt],
               lambda e, PeT=PeT, PTt=PTt, h0=h0: e.tensor_tensor(
                   PTt[:, :].rearrange("p (b j q) -> p b j q", b=2, j=4),
                   PeT[:, :].rearrange("p (b j q) -> p b j q", b=2, j=4),
                   EAB[:, :, h0:h0 + 4, :], ALU.mult))
            yield
            for j in range(4):
                op("pe", [bPTt, bVAl[prev]], [bPD],
                   lambda e, j=j, PTt=PTt, kvh=kvh: e.matmul(PD[:, j * 65:(j + 1) * 65], PTt[:, j * 128:(j + 1) * 128],
                                                             VAt[:, prev, kvh, :], start=True, stop=False), inc=False)
                op("pe", [bPTt, bVAl[cur]], [bPD],
                   lambda e, j=j, PTt=PTt, kvh=kvh: e.matmul(PD[:, j * 65:(j + 1) * 65], PTt[:, 512 + j * 128:512 + (j + 1) * 128],
                                                             VAt[:, cur, kvh, :], start=False, stop=True), inc=(j == 3))
            yield
            O3 = PD[:, 0:260].rearrange("p (j e) -> p j e", e=65)
            h0 = kvh * 8 + hf * 4
            op("dve", [bPD, bSINKE], [bSMO],
               lambda e, O3=O3, SMO_=SMO_, h0=h0: e.tensor_tensor(
                   SMO_[:, 0:4].unsqueeze(2), O3[:, :, 64:65], SINKE[:, h0:h0 + 4].unsqueeze(2), ALU.add))
            op("dve", [bSMO], [bSMO], lambda e, SMO_=SMO_: e.reciprocal(SMO_[:, 4:8], SMO_[:, 0:4]))
            TMPO, bTMPO = TB[0]
            op("dve", [bPD, bSMO], [bTMPO],
               lambda e, O3=O3, SMO_=SMO_, TMPO=TMPO, step=step: e.tensor_tensor(
                   TMPO[:, step * 256:(step + 1) * 256].rearrange("p (j d) -> p j d", j=4), O3[:, :, 0:64],
                   SMO_[:, 4:8].unsqueeze(2).to_broadcast([n, 4, HD]), ALU.mult))
            op("dve", [bTMPO, bSGB], [bOGb],
               lambda e, TMPO=TMPO, step=step: e.scalar_tensor_tensor(
                   OGb[:, step * 256:(step + 1) * 256], TMPO[:, step * 256:(step + 1) * 256], 0.5,
                   SGB[:, step * 256:(step + 1) * 256], ALU.mult, ALU.mult))
            yield
        for kc in range(NCH):
            op("pe", [bOGb, bIDB], [bPT0],
               lambda e, kc=kc: e.transpose(PT0[:, kc * 128:(kc + 1) * 128], OGb[:, kc * 128:(kc + 1) * 128], IDB[:, :]),
               inc=(kc == NCH - 1))
        OGT, bOGT = BB_[1]
        op("act", [bPT0], [bOGT], lambda e: e.activation(OGT[:, :], PT0[:, :], AF.Copy))
        OGT3 = v3(OGT)
        yield
        for half in range(2):
            for kc in range(NCH):
                op("pe", [bOGT, bWoutB], [bPC, bPC0, bPC1],
                   lambda e, half=half, kc=kc: e.matmul(PC[:, half * 512:(half + 1) * 512], OGT3[:, kc, :],
                                                        WoutB[:, kc, half * 512:(half + 1) * 512],
                                                        start=(kc == 0), stop=(kc == NCH - 1)),
                   inc=(half == 1 and kc == NCH - 1))
            yield
        op("dve", [bX1c, bPC], [bX1c, bPC0, bPC1], lambda e: e.tensor_tensor(X1c[:, :], X1c[:, :], PC[:, :], ALU.add))
        S.dma("sp", tp["y_dst"], X1c[:nv, :], [bX1c], [])
        yield

    HPREV = ((lambda c: HC[:, c:c + 1]), bHC)
    pre_tiles = []
    for t in range(NPRE):
        pre_tiles.append(dict(
            nv=128, x_src=xpre[t * 128:(t + 1) * 128, :], y_dst=None, Hprev=HPREV, X1=None, carry=True,
            bs=(BS0P, BS1, BS2)[(t - NPRE) % 3], XB=XBS3[(t - NPRE) % 3], XBn=XBS3[(t + 1 - NPRE) % 3],
            idx=t, dep=(t - 1 if t > 0 else None), dep2=(t - 2 if t > 1 else None), Xbuf=None,
            hist_init=None, state_out=None, lite=True, mask=True, kvonly=True))
    tiles = []
    for j in range(NTH + 1):
        cur = j % 2
        prev = 1 - cur
        halo = (j == 0)
        tiles.append(dict(
            nv=128, x_src=xp[j * 128:(j + 1) * 128, :], y_dst=(None if halo else yp[(j - 1) * 128:j * 128, :]),
            Hprev=HPREV, X1=X1[j % 3], Xbuf=X1[j % 3], carry=True, bs=BS0, XBn=XBS[0], idx=NPRE + j, dep=None,
            K2=K2, bK2=bK2s, VA=VA, bVA=bVAs, cur=cur, prev=prev, kmcol=(2 if halo else 0), hist_init=None,
            state_out=({"conv": pconv, "h": ph, "k": pk, "v": pv, "stage": CST[0]} if j == NTH else None),
            lite=False, mask=halo, kvonly=halo))
    sc = (NTH + 1) % 2

    def sample_hist(XBt, bXB):
        op("dve", [bPVT], [bXB],
           lambda e: e.tensor_copy(XBt[:, :, 0:3], PVT[:, 64:88].rearrange("p (j c) -> p c j", j=3)))
    tiles.append(dict(
        nv=NS, x_src=xs[:, :], y_dst=ys[:, :],
        Hprev=((lambda c: PVT[:, 88 + c:89 + c]), bPVT), X1=X1[(NTH + 1) % 3], Xbuf=X1[(NTH + 1) % 3], carry=False,
        bs=BS0, XBn=XBS[0], idx=-1, dep=None,
        K2=K2S, bK2=bK2Ss, VA=VAS, bVA=bVASs, cur=1, prev=0, kmcol=1, hist_init=sample_hist,
        state_out={"conv": sconv, "h": sh, "k": sk[128 - NS:128, :], "v": sv[128 - NS:128, :], "stage": CST[1]},
        lite=False, mask=False, kvonly=False))

    for t in range(NPRE):
        if t + 3 < NPRE:
            pre_tiles[t]["prefetch"] = pre_tiles[t + 3]
    for j in range(len(tiles) - 1):
        tiles[j]["prefetch"] = tiles[j + 1]

    CKt, bCK = TB[3]
    S.dma("sp", CKt[:, 0:128], ck, [], [bCK])
    S.dma("sp", CKt[:, 128:256], cv, [], [bCK])
    op("dve", [bCK], [bKOUT], lambda e: e.tensor_copy(KOUT[:, :], CKt[:, 0:128]))
    k_to_k2(KOUT, bKOUT, K2S, bK2Ss[0], 0)
    op("dve", [bCK], [bVASs[0]],
       lambda e: e.tensor_copy(VAS[:, 0, :, 0:HD], CKt[:, 128:256].rearrange("p (k d) -> p k d", k=2)))
    op("dve", [], [bVASs[0]], lambda e: e.memset(VAS[:, 0, :, HD:HD + 1], 1.0))
    S.dma("sp", sk[0:128 - NS, :], ck[NS:128, :], [], [])
    S.dma("sp", sv[0:128 - NS, :], cv[NS:128, :], [], [])

    def advance(g):
        S.step_fin = 0.0
        try:
            r = next(g)
        except StopIteration:
            return None
        if r == "blocked":
            return -1.0
        return S.step_fin

    def run_streams(gens):
        gens = list(gens)
        times = [0.0 for _ in gens]
        while gens:
            j = min(range(len(gens)), key=lambda q: times[q])
            r = advance(gens[j])
            if r is None:
                gens.pop(j)
                times.pop(j)
            elif r > 0.0:
                times[j] = r

    pend = [phaseA(tp_) for tp_ in pre_tiles]
    active = [deferred_loads()]
    times = [0.0]
    nlite = 0
    NLITE = _NLITE
    while active:
        while nlite < NLITE and pend:
            active.append(pend.pop(0))
            times.append(min(times) if times else 0.0)
            nlite += 1
        j = min(range(len(active)), key=lambda q: times[q])
        r = advance(active[j])
        if r is None:
            if j != 0 or len(active) == 1 or True:
                pass
            g = active.pop(j)
            times.pop(j)
            if getattr(g, "gi_code", None) is not None and g.gi_code.co_name == "phaseA":
                nlite -= 1
        elif r < 0.0:
            times[j] = max(times) + 1.0
        elif r > 0.0:
            times[j] = r

    run_streams([phaseA(tiles[0]), deferred_winb()])
    for i in range(len(tiles)):
        gens = [phaseB(tiles[i])]
        if i + 1 < len(tiles):
            gens.append(phaseA(tiles[i + 1]))
        run_streams(gens)
    S.finish()
    es.close()
    return nc, S


_CACHE = {}


def _consts():
    ident = np.eye(128, dtype=np.float32)
    bones = np.zeros((128, 128), np.float32)
    bones[:64, :64] = 1.0 / 64
    bones[64:, 64:] = 1.0 / 64
    key = np.arange(128)[:, None]
    q = np.arange(128)[None, :]
    dA = (q + 128 - key).astype(np.float32)
    mA = ((key // 64 == 1) | (q // 64 == 0)).astype(np.float32)
    dB = np.abs(q - key).astype(np.float32)
    mB = ((key // 64 == 0) | (q // 64 == 1)).astype(np.float32)
    return ident, bones, dA * mA, dB * mB, mA, mB


def kernel(x_prompt, x_sample, cache_k, cache_v, state_conv, state_rglru,
           a_norm, a_w_in, a_conv_w, a_conv_b, a_gate_a_w, a_gate_a_b, a_gate_x_w, a_gate_x_b,
           a_lambda, a_w_out, kv_norm, w_kv, k_norm, b_norm, b_w_in, q_norm, sinks, b_w_out, _seq=None):
    f = lambda a: np.ascontiguousarray(np.asarray(a, dtype=np.float32))
    x_prompt = f(x_prompt)
    seq = x_prompt.shape[1]
    if seq not in _CACHE:
        _CACHE[seq] = build(seq)[0]
    nc = _CACHE[seq]
    ident, bones, dA, dB, mA, mB = _consts()
    km = np.ones((128, 3), np.float32)
    km[NS:, 1] = 0.0
    nth = seq // 256
    npre = nth - 1
    x_sample = f(x_sample); cache_k = f(cache_k); cache_v = f(cache_v)
    state_conv = f(state_conv); state_rglru = f(state_rglru)
    r8 = lambda v: f(v).reshape(-1, 8, 128).reshape(-1, 128)
    n_cores = 8
    BP = x_prompt.shape[0]
    in_maps = []
    for c in range(n_cores):
        b = c % BP
        half = c // BP
        zt = np.zeros((128, D), np.float32)
        if half == 0:
            xpre_c = np.zeros((max(npre, 1) * 128, D), np.float32)
            xmain_c = np.concatenate([zt, x_prompt[b, :nth * 128]], 0)
        else:
            xpre_c = x_prompt[b, :npre * 128] if npre > 0 else zt
            xmain_c = x_prompt[b, npre * 128:]
        xpre_c = np.ascontiguousarray(xpre_c)
        xmain_c = np.ascontiguousarray(xmain_c)
        pvec = np.concatenate([
            r8(f(a_conv_w)[0]),
            r8(f(a_conv_b)[0][None]), r8(f(a_gate_a_b)[0][None]), r8(f(a_gate_x_b)[0][None]), r8(f(a_lambda)[0][None]),
            r8(state_conv[0, c]),
            r8(state_rglru[0, c][None]),
            r8(f(a_norm)[0][None]), r8(f(kv_norm)[None]), r8(f(b_norm)[0][None]),
        ], axis=0)
        assert pvec.shape == (120, 128)
        in_maps.append({
            "xpre": xpre_c, "xp": xmain_c, "flag": np.array([float(half)], np.float32), "xs": x_sample[c],
            "ck": cache_k[c].reshape(128, 128), "cv": cache_v[c].reshape(128, 128),
            "pvec": np.ascontiguousarray(pvec),
            "w_in_a": f(a_w_in)[0], "wga": f(a_gate_a_w)[0].reshape(D, 256), "wgx": f(a_gate_x_w)[0].reshape(D, 256),
            "w_out_a": f(a_w_out)[0], "w_kv": f(w_kv), "w_in_b": f(b_w_in)[0], "w_out_b": f(b_w_out)[0],
            "knorm": f(k_norm), "qnorm": f(q_norm)[0], "sinks": f(sinks)[0],
            "ident": ident, "bones": bones, "distA": dA, "distB": dB, "maskA": mA, "maskB": mB, "kmask": km,
        })
    res = run_bass_kernel_spmd(nc, in_maps, core_ids=list(range(n_cores)))
    R = res.results
    g = lambda c, k: np.asarray(R[c][k], dtype=np.float32)
    y_prompt = np.stack([np.concatenate([g(b, "yp"), g(b + BP, "yp")], 0) for b in range(BP)], 0)
    y_sample = np.stack([g(c, "ys") for c in range(n_cores)], 0)
    p_k = np.stack([g(b + BP, "pk").reshape(128, 2, HD) for b in range(BP)], 0)
    p_v = np.stack([g(b + BP, "pv").reshape(128, 2, HD) for b in range(BP)], 0)
    p_conv = np.stack([g(b + BP, "pconv") for b in range(BP)], 0)[None]
    p_h = np.stack([g(b + BP, "ph") for b in range(BP)], 0)[None]
    s_k = np.stack([g(c, "sk").reshape(128, 2, HD) for c in range(n_cores)], 0)
    s_v = np.stack([g(c, "sv").reshape(128, 2, HD) for c in range(n_cores)], 0)
    s_conv = np.stack([g(c, "sconv") for c in range(n_cores)], 0)[None]
    s_h = np.stack([g(c, "sh") for c in range(n_cores)], 0)[None]
    return (y_prompt, y_sample, p_k, p_v, p_conv, p_h, s_k, s_v, s_conv, s_h)
```

```python
import numpy as np
from contextlib import ExitStack
import concourse.bass as bass
import concourse.mybir as mybir
from concourse.bass_utils import run_bass_kernel_spmd

F32 = mybir.dt.float32
BF16 = mybir.dt.bfloat16
AF = mybir.ActivationFunctionType
ALU = mybir.AluOpType
AX = mybir.AxisListType

D = 1024
NCH = 8
SEQ = 4096
NS = 16
EPS = 1e-6
N_HEADS = 16
HD = 64


_DBG_STOP = None
_NLITE = 3


class _Stop(Exception):
    pass


def _ck(tag):
    if _DBG_STOP is not None and tag == _DBG_STOP:
        raise _Stop()


class Buf:
    __slots__ = ("name", "w", "r", "parts")

    def __init__(self, name, parts=None):
        self.name = name
        self.w = None
        self.r = {}
        self.parts = parts


def _expand(bufs):
    out = []
    for b in bufs:
        if b.parts is not None:
            out.extend(b.parts)
        else:
            out.append(b)
    return out


class _Proxy:
    def __init__(self, eng):
        self._e = eng
        self.sz = 128
        self.name = None

    def __getattr__(self, name):
        f = getattr(self._e, name)

        def w(*a, **kw):
            out = kw.get("out", a[0] if a else None)
            try:
                m = 1
                for d in out.shape[1:]:
                    m *= d
                self.sz = m
            except Exception:
                pass
            self.name = name
            return f(*a, **kw)
        return w


class SyncMgr:
    NSLOT = 8

    def __init__(self, nc, es):
        self.nc = nc
        self.eng = {"pe": nc.tensor, "act": nc.scalar, "dve": nc.vector, "pool": nc.gpsimd, "sp": nc.sync}
        self.sem = {k: es.enter_context(nc.semaphore("s_" + k)) for k in self.eng}
        self.cnt = {k: 0 for k in self.eng}
        self.waited = {k: {} for k in self.eng}
        self.dsem = {q: [es.enter_context(nc.semaphore("d_%s_%d" % (q, i))) for i in range(self.NSLOT)]
                     for q in ("sp", "pool", "act")}
        self.dcnt = {"sp": 0, "pool": 0, "act": 0}
        self.ninst = 0
        self.efree = {k: 0.0 for k in self.eng}
        self.tfin = {}
        self.step_fin = 0.0

    def _est(self, e, reads, writes, dur):
        reads, writes = _expand(reads), _expand(writes)
        t0 = self.efree[e]
        for b in reads:
            if b.w is not None:
                t0 = max(t0, self.tfin.get(b.w[0:1] + (b.w[2],), 0.0))
        for b in writes:
            if b.w is not None:
                t0 = max(t0, self.tfin.get(b.w[0:1] + (b.w[2],), 0.0))
            for r in b.r.values():
                t0 = max(t0, self.tfin.get(r[0:1] + (r[2],), 0.0))
        fin = t0 + dur
        self.efree[e] = fin if e != "sp" else t0 + 100.0
        self.step_fin = max(self.step_fin, fin)
        return fin

    def _waits(self, e, reads, writes):
        reads, writes = _expand(reads), _expand(writes)
        need = {}
        deps = []
        for b in reads:
            if b.w is not None:
                deps.append(b.w)
        for b in writes:
            if b.w is not None:
                deps.append(b.w)
            deps.extend(b.r.values())
        for key, sem, val in deps:
            if key == ("e", "pe") and e == "pe":
                continue
            if key not in need or need[key][1] < val:
                need[key] = (sem, val)
        for key, (sem, val) in need.items():
            if self.waited[e].get(key, -1) >= val:
                continue
            self.eng[e].wait_ge(sem, val)
            self.waited[e][key] = val

    def _record(self, tok, reads, writes):
        reads, writes = _expand(reads), _expand(writes)
        key = tok[0]
        for b in writes:
            b.w = tok
            b.r = {}
        for b in reads:
            if b in writes:
                continue
            old = b.r.get(key)
            if old is None or old[2] < tok[2]:
                b.r[key] = tok

    def op(self, e, reads, writes, emit, inc=True, sz=None, k=None):
        self._waits(e, reads, writes)
        px = _Proxy(self.eng[e])
        ins = emit(px)
        self.ninst += 1
        if sz is None:
            sz = px.sz
        if k is None:
            k = {"reciprocal": 2.1, "tensor_tensor_scan": 2.0, "tensor_scalar": 0.6, "tensor_copy": 0.6}.get(px.name, 1.0)
        if e == "pe":
            dur = max(62.0, sz / 2.4 + 6.0)
        elif e == "act":
            dur = 220.0 + 0.83 * sz
        elif e == "dve":
            dur = 165.0 + 1.04 * sz * k
        else:
            dur = 150.0 + 2.4 * sz
        fin = self._est(e, reads, writes, dur)
        if inc:
            ins.then_inc(self.sem[e], 1)
            self.cnt[e] += 1
            tok = (("e", e), self.sem[e], self.cnt[e])
        else:
            assert e == "pe"
            tok = (("e", e), self.sem[e], self.cnt[e] + 1)
        self.tfin[(tok[0], tok[2])] = max(fin, self.tfin.get((tok[0], tok[2]), 0.0))
        self._record(tok, reads, writes)

    def dma(self, q, out, in_, reads, writes, **kw):
        fin = self._est(q, reads, writes, 2500.0)
        self._waits(q, reads, writes)
        slot = self.dcnt[q] % self.NSLOT
        use = self.dcnt[q] // self.NSLOT
        key = ("d", q, slot)
        sem = self.dsem[q][slot]
        if use > 0 and self.waited[q].get(key, -1) < 16 * use:
            self.eng[q].wait_ge(sem, 16 * use)
            self.waited[q][key] = 16 * use
        self.eng[q].dma_start(out=out, in_=in_, **kw).then_inc(sem, 16)
        self.ninst += 1
        self.dcnt[q] += 1
        tok = (key, sem, 16 * (use + 1))
        self.tfin[(tok[0], tok[2])] = fin
        self._record(tok, reads, writes)

    def finish(self):
        for q in ("sp", "pool", "act"):
            for slot in range(self.NSLOT):
                uses = (self.dcnt[q] - slot + self.NSLOT - 1) // self.NSLOT
                if uses > 0:
                    key = ("d", q, slot)
                    if self.waited[q].get(key, -1) < 16 * uses:
                        self.eng[q].wait_ge(self.dsem[q][slot], 16 * uses)
                        self.waited[q][key] = 16 * uses


def build(seq=SEQ):
    assert seq % 128 == 0
    assert seq % 256 == 0
    NTH = seq // 256
    NPRE = NTH - 1
    NPRE_A = max(NPRE, 1)
    nc = bass.Bass("TRN2", target_bir_lowering=False)
    es = ExitStack()

    def din(name, shape):
        return nc.dram_tensor(name, list(shape), F32, kind="ExternalInput").ap()

    def dout(name, shape):
        return nc.dram_tensor(name, list(shape), F32, kind="ExternalOutput").ap()

    xpre = din("xpre", [NPRE_A * 128, D]); xp = din("xp", [(NTH + 1) * 128, D]); xs = din("xs", [NS, D])
    flag = din("flag", [1])
    ck = din("ck", [128, 128]); cv = din("cv", [128, 128])
    pvec = din("pvec", [120, 128])
    w_in_a = din("w_in_a", [D, 2 * D]); wga = din("wga", [D, 256]); wgx = din("wgx", [D, 256])
    w_out_a = din("w_out_a", [D, D]); w_kv = din("w_kv", [D, 256])
    w_in_b = din("w_in_b", [D, 2 * D]); w_out_b = din("w_out_b", [D, D])
    knorm = din("knorm", [HD]); qnorm = din("qnorm", [HD]); sinks = din("sinks", [N_HEADS])
    ident = din("ident", [128, 128]); bones = din("bones", [128, 128])
    distA = din("distA", [128, 128]); distB = din("distB", [128, 128])
    maskA = din("maskA", [128, 128]); maskB = din("maskB", [128, 128])
    kmask = din("kmask", [128, 3])

    yp = dout("yp", [NTH * 128, D]); ys = dout("ys", [NS, D])
    pk = dout("pk", [128, 128]); pv = dout("pv", [128, 128])
    pconv = dout("pconv", [3, D]); ph = dout("ph", [D])
    sk = dout("sk", [128, 128]); sv = dout("sv", [128, 128])
    sconv = dout("sconv", [3, D]); sh = dout("sh", [D])

    S = SyncMgr(nc, es)

    def sb(name, shape, dt=F32):
        t = es.enter_context(nc.sbuf_tensor(name, list(shape), dt))
        return t, Buf(name)

    def ps(name, shape, dt=F32):
        t = es.enter_context(nc.psum_tensor(name, list(shape), dt))
        return t, Buf(name)

    WinA, bWinA = sb("WinA", [128, NCH, 2 * D], BF16)
    WGA, bWGA = sb("WGA", [128, NCH, 256], BF16)
    WGX, bWGX = sb("WGX", [128, NCH, 256], BF16)
    WoutA, bWoutA = sb("WoutA", [128, NCH, D], BF16)
    Wkv, bWkv = sb("Wkv", [128, NCH, 256], BF16)
    WinB, bWinB = sb("WinB", [128, NCH, 2 * D], BF16)
    WoutB, bWoutB = sb("WoutB", [128, NCH, D], BF16)
    IDF, bIDF = sb("IDF", [128, 128], F32)
    IDB, bIDB = sb("IDB", [128, 128], BF16)
    BON, bBON = sb("BON", [128, 128], BF16)
    EAB, bEAB = sb("EAB", [128, 2, N_HEADS, 128], BF16)
    bEA = bEB = bEAB
    PVT, bPVT = sb("PVT", [128, 120], F32)
    C8, bC8 = sb("C8", [128, NCH], F32)
    CH, bCH = sb("CH", [128, NCH], F32)
    BAH, bBAH = sb("BAH", [128, NCH], F32)
    BXH, bBXH = sb("BXH", [128, NCH], F32)
    GK, bGK = sb("GK", [128, HD], F32)
    GQ8, bGQ8 = sb("GQ8", [128, HD], F32)
    MNEG, bMNEG = sb("MNEG", [128, 1], F32)
    SINKE, bSINKE = sb("SINKE", [128, N_HEADS], F32)
    CONSTS, bCONSTS = sb("CONSTS", [128, 4], F32)
    SMALL, bSMALL = sb("SMALL", [128, 64], F32)
    SMA, bSMA = sb("SMA", [128, 8], F32)
    SMB, bSMB = sb("SMB", [128, 8], F32)
    SMK, bSMK = sb("SMK", [128, 8], F32)
    SMO = [sb("SMO%d" % i, [128, 8], F32) for i in range(2)]
    KM, bKM = sb("KM", [128, 3], F32)
    CST = [sb("CST%d" % i, [128, NCH, 4], F32) for i in range(2)]
    CTMP, bCTMP = sb("CTMP", [128, 3, 128], F32)

    EPS_T, bEPS = sb("EPS_T", [128, 1], F32)
    TA = [sb("TA%d" % i, [128, D], F32) for i in range(6)]
    TB = [sb("TB%d" % i, [128, D], F32) for i in range(4)]
    BA_ = [sb("BA%d" % i, [128, D], BF16) for i in range(4)]
    BB_ = [sb("BB%d" % i, [128, D], BF16) for i in range(6)]
    for t_ in (TA[2], TB[1]):
        t_[1].parts = [Buf(t_[1].name + "_lo"), Buf(t_[1].name + "_hi")]
    X1 = [sb("X1_%d" % i, [128, D], F32) for i in range(3)]
    XBS = [sb("XB%d" % i, [128, NCH, 3 + 128], F32) for i in range(2)]
    HC, bHC = sb("HC", [128, NCH], F32)
    K2, bK2 = sb("K2", [128, 2, 2, 2, 128], BF16)
    VA, bVA = sb("VA", [128, 2, 2, 65], BF16)
    K2S, bK2S = sb("K2S", [128, 2, 2, 2, 128], BF16)
    VAS, bVAS = sb("VAS", [128, 2, 2, 65], BF16)
    KOUT, bKOUT = sb("KOUT", [128, 128], F32)
    VOUT, bVOUT = sb("VOUT", [128, 128], F32)
    KD, bKD = sb("KD", [128, 2, 2, HD], BF16)
    bWinA2 = Buf("WinA_gate")
    bK2s = [Buf("K2_0"), Buf("K2_1")]
    bVAs = [Buf("VA_0"), Buf("VA_1")]
    bK2Ss = [Buf("K2S_0"), Buf("K2S_1")]
    bVASs = [Buf("VAS_0"), Buf("VAS_1")]

    PT0, bPT0 = ps("PT0", [128, D], BF16)
    PA, bPA = ps("PA", [128, D], F32)
    PB, bPB = ps("PB", [128, D], F32)
    PC, bPC = ps("PC", [128, D], F32)
    PD, bPD = ps("PD", [128, 512], F32)
    bPC0, bPC1 = Buf("PC0"), Buf("PC1")

    op = S.op
    T = TA

    def v3(t, n=128):
        return t[:, :].rearrange("p (c t) -> p c t", c=NCH)[:, :, 0:n]

    S.dma("sp", IDF[:, :], ident, [], [bIDF])
    S.dma("sp", T[0][0][:, 0:128], bones, [], [T[0][1]])
    S.dma("sp", T[1][0][:120, 0:128], pvec, [], [T[1][1]])
    S.dma("sp", GK[:, :], knorm.partition_broadcast(128), [], [bGK])
    S.dma("sp", GQ8[:, :], qnorm.partition_broadcast(128), [], [bGQ8])
    S.dma("sp", SINKE[:, :], sinks.partition_broadcast(128), [], [bSINKE])
    S.dma("sp", T[2][0][:, 0:128], distA, [], [T[2][1]])
    S.dma("sp", T[2][0][:, 128:256], distB, [], [T[2][1]])
    S.dma("sp", T[2][0][:, 256:384], maskA, [], [T[2][1]])
    S.dma("sp", T[2][0][:, 384:512], maskB, [], [T[2][1]])
    S.dma("sp", KM[:, :], kmask, [], [bKM])
    S.dma("sp", KM[:, 2:3], flag.partition_broadcast(128), [], [bKM])

    op("dve", [bIDF], [bIDB], lambda e: e.tensor_copy(IDB[:, :], IDF[:, :]))
    op("dve", [T[0][1]], [bBON], lambda e: e.tensor_copy(BON[:, :], T[0][0][:, 0:128]))
    op("dve", [], [bCONSTS], lambda e: e.memset(CONSTS[:, 0:1], 0.5))
    op("dve", [], [bCONSTS], lambda e: e.memset(CONSTS[:, 1:2], -0.5))
    op("dve", [], [bCONSTS], lambda e: e.memset(CONSTS[:, 2:3], 0.25))
    op("dve", [], [bEPS], lambda e: e.memset(EPS_T[:, 0:1], EPS))
    op("pe", [T[1][1], bIDF], [bPD],
       lambda e: e.transpose(PD[:, 0:120], T[1][0][:120, 0:128], IDF[:120, :120]))
    op("dve", [bPD], [bPVT], lambda e: e.tensor_copy(PVT[:, :], PD[:, 0:120]))
    op("dve", [bPVT], [bBAH], lambda e: e.tensor_scalar(BAH[:, :], PVT[:, 40:48], 0.5, None, ALU.mult))
    op("dve", [bPVT], [bBXH], lambda e: e.tensor_scalar(BXH[:, :], PVT[:, 48:56], 0.5, None, ALU.mult))
    sm = SMALL
    LAM = PVT[:, 56:64]
    op("act", [bPVT], [bSMALL], lambda e: e.activation(sm[:, 0:8], LAM, AF.Abs))
    op("act", [bSMALL], [bSMALL], lambda e: e.activation(sm[:, 8:16], sm[:, 0:8], AF.Exp, scale=-1.0))
    op("dve", [bSMALL], [bSMALL], lambda e: e.tensor_scalar(sm[:, 16:24], sm[:, 8:16], 2.0, None, ALU.add))
    op("dve", [bSMALL], [bSMALL], lambda e: e.reciprocal(sm[:, 16:24], sm[:, 16:24]))
    op("dve", [bSMALL], [bSMALL], lambda e: e.tensor_tensor(sm[:, 24:32], sm[:, 8:16], sm[:, 16:24], ALU.mult))
    op("dve", [bSMALL], [bSMALL], lambda e: e.tensor_tensor(sm[:, 32:40], sm[:, 24:32], sm[:, 24:32], ALU.mult))
    op("dve", [], [bSMALL], lambda e: e.memset(sm[:, 40:48], 1.0 / 19.0))
    for kk in (17, 15, 13, 11, 9, 7, 5, 3, 1):
        op("dve", [bSMALL], [bSMALL], lambda e: e.tensor_tensor(sm[:, 40:48], sm[:, 40:48], sm[:, 32:40], ALU.mult))
        op("dve", [bSMALL], [bSMALL],
           lambda e, kk=kk: e.tensor_scalar(sm[:, 40:48], sm[:, 40:48], 1.0 / kk, None, ALU.add))
    op("dve", [bSMALL], [bSMALL], lambda e: e.tensor_tensor(sm[:, 40:48], sm[:, 40:48], sm[:, 24:32], ALU.mult))
    op("dve", [bPVT, bSMALL], [bSMALL],
       lambda e: e.tensor_scalar(sm[:, 48:56], LAM, -1.0, 0.0, ALU.mult, ALU.max))
    op("dve", [bSMALL], [bSMALL],
       lambda e: e.scalar_tensor_tensor(sm[:, 48:56], sm[:, 40:48], 2.0, sm[:, 48:56], ALU.mult, ALU.add))
    op("dve", [bSMALL], [bC8], lambda e: e.tensor_scalar(C8[:, :], sm[:, 48:56], -8.0, None, ALU.mult))
    op("dve", [bSMALL], [bCH], lambda e: e.tensor_scalar(CH[:, :], sm[:, 48:56], -4.0, None, ALU.mult))

    op("dve", [bGK, bGQ8], [bSMALL], lambda e: e.tensor_tensor(sm[:, 0:64], GK[:, :], GQ8[:, :], ALU.mult))
    op("act", [bSMALL], [bSMALL], lambda e: e.activation(sm[:, 0:64], sm[:, 0:64], AF.Abs))
    op("dve", [bSMALL], [bMNEG], lambda e: e.tensor_reduce(MNEG[:, 0:1], sm[:, 0:64], AX.X, ALU.max))
    op("dve", [bMNEG], [bMNEG], lambda e: e.tensor_scalar(MNEG[:, 0:1], MNEG[:, 0:1], -8.0, None, ALU.mult))
    op("dve", [bGQ8], [bGQ8], lambda e: e.tensor_scalar(GQ8[:, :], GQ8[:, :], 0.125, None, ALU.mult))
    op("act", [bSINKE, bMNEG], [bSINKE],
       lambda e: e.activation(SINKE[:, :], SINKE[:, :], AF.Exp, bias=MNEG[:, 0:1]))

    for (E_, bE_, dcol, mcol, Tt) in ((EAB[:, 0], bEA, 0, 256, T[3]), (EAB[:, 1], bEB, 128, 384, T[4])):
        for h0 in range(0, N_HEADS, 8):
            for h in range(h0, h0 + 8):
                slope = float(2.0 ** (-8.0 * (h + 1) / N_HEADS))
                op("act", [T[2][1]], [Tt[1]],
                   lambda e, h=h, h0=h0, slope=slope, Tt=Tt, dcol=dcol: e.activation(
                       v3(Tt[0])[:, h - h0, :], T[2][0][:, dcol:dcol + 128], AF.Exp, scale=-slope))
            op("dve", [T[2][1], Tt[1]], [bE_],
               lambda e, E_=E_, h0=h0, Tt=Tt, mcol=mcol: e.tensor_tensor(
                   E_[:, h0:h0 + 8, :], v3(Tt[0]),
                   T[2][0][:, mcol:mcol + 128].unsqueeze(1).to_broadcast([128, 8, 128]), ALU.mult))

    stg_i = [0]
    STG = [TB[0], TB[1], TB[2], TB[3], X1[0], X1[1], X1[2]]
    QS = ["sp", "act"]

    def load_weight(dst, bdst, src_ap, ncols_total, gain_col=None):
        for c0 in range(0, ncols_total, 1024):
            ww = min(ncols_total, c0 + 1024) - c0
            tt = STG[stg_i[0] % len(STG)]
            q = QS[stg_i[0] % len(QS)]
            stg_i[0] += 1
            S.dma(q, tt[0][:, 0:ww], src_ap[:, c0:c0 + ww], [], [tt[1]])
            if gain_col is None:
                op("dve", [tt[1]], [bdst],
                   lambda e, tt=tt, ww=ww, c0=c0: e.tensor_copy(dst[:, c0:c0 + ww], tt[0][:, 0:ww]))
            else:
                op("dve", [tt[1], bPVT], [bdst],
                   lambda e, tt=tt, ww=ww, c0=c0: e.tensor_scalar(
                       dst[:, c0:c0 + ww], tt[0][:, 0:ww], PVT[:, gain_col:gain_col + 1], None, ALU.mult))

    for kc in range(NCH):
        load_weight(WinA[:, kc, 0:D], bWinA, w_in_a[kc * 128:(kc + 1) * 128, 0:D], D, gain_col=96 + kc)
    for kc in range(NCH):
        load_weight(WGA[:, kc, :], bWGA, wga[kc * 128:(kc + 1) * 128, :], 256)
        load_weight(WGX[:, kc, :], bWGX, wgx[kc * 128:(kc + 1) * 128, :], 256)

    def deferred_loads():
        stg, bstg = TA[1]
        jobs = []
        for kc in range(NCH):
            jobs.append((WinA[:, kc, D:2 * D], bWinA2, w_in_a[kc * 128:(kc + 1) * 128, D:2 * D], D, 96 + kc))
        for kc in range(NCH):
            jobs.append((WoutA[:, kc, :], bWoutA, w_out_a[kc * 128:(kc + 1) * 128, :], D, None))
        for kc in range(NCH):
            jobs.append((Wkv[:, kc, :], bWkv, w_kv[kc * 128:(kc + 1) * 128, :], 256, 104 + kc))
        for kc in range(NCH):
            pass
        for kc in range(NCH):
            jobs.append((WoutB[:, kc, :], bWoutB, w_out_b[kc * 128:(kc + 1) * 128, :], D, None))
        for (dst, bdst, src, w, gcol) in jobs:
            S.dma("sp", stg[:, 0:w], src, [], [bstg])
            if gcol is None:
                op("dve", [bstg], [bdst], lambda e, dst=dst, w=w: e.tensor_copy(dst, stg[:, 0:w]))
            else:
                op("dve", [bstg, bPVT], [bdst],
                   lambda e, dst=dst, w=w, gcol=gcol: e.tensor_scalar(dst, stg[:, 0:w], PVT[:, gcol:gcol + 1], None, ALU.mult))
            yield

    def deferred_winb():
        k_ = 0
        for kc in range(NCH):
            for c0 in (0, D):
                stg, bstg = TB[2 + k_ % 2]
                k_ += 1
                S.dma("sp", stg[:, 0:D], w_in_b[kc * 128:(kc + 1) * 128, c0:c0 + D], [], [bstg])
                op("dve", [bstg, bPVT], [bWinB],
                   lambda e, kc=kc, c0=c0, stg=stg: e.tensor_scalar(WinB[:, kc, c0:c0 + D], stg[:, 0:D],
                                                                    PVT[:, 112 + kc:113 + kc], None, ALU.mult))
                yield

    op("dve", [], [XBS[0][1]], lambda e: e.memset(XBS[0][0][:, :, 0:3], 0.0))
    op("dve", [], [XBS[1][1]], lambda e: e.memset(XBS[1][0][:, :, 0:3], 0.0))
    op("dve", [], [bHC], lambda e: e.memset(HC[:, :], 0.0))
    op("dve", [], [bK2s[0], bK2s[1]], lambda e: e.memset(K2[:, :, :, :, :].rearrange("p a b c d -> p (a b c d)"), 0.0))
    op("dve", [], [bK2Ss[0], bK2Ss[1]], lambda e: e.memset(K2S[:, :, :, :, :].rearrange("p a b c d -> p (a b c d)"), 0.0))
    op("dve", [], [bVAs[1]], lambda e: e.memset(VA[:, 1, :, :], 0.0))

    n = 128

    def rms_to_T(Xsrc, bXsrc, SM_, bSM, XNb, XTb, evac_dve=False):
        XN, bXN = XNb
        XT_, bXT = XTb
        op("act", [bXsrc], [bXN, bSM],
           lambda e: e.activation(XN[:, :], Xsrc[:, :], AF.Square, accum_out=SM_[:, 0:1]))
        op("dve", [bSM], [bSM],
           lambda e: e.tensor_scalar(SM_[:, 1:2], SM_[:, 0:1], 1.0 / D, EPS, ALU.mult, ALU.add))
        op("pool", [bSM, bCONSTS], [bSM],
           lambda e: e.tensor_tensor(SM_[:, 2:3], SM_[:, 1:2], CONSTS[:, 1:2], ALU.pow))
        op("dve", [bXsrc, bSM], [bXN],
           lambda e: e.tensor_scalar(XN[:, :], Xsrc[:, :], SM_[:, 2:3], None, ALU.mult))
        yield
        for kc in range(NCH):
            op("pe", [bXN, bIDB], [bPT0],
               lambda e, kc=kc: e.transpose(PT0[:, kc * 128:(kc + 1) * 128], XN[:, kc * 128:(kc + 1) * 128], IDB[:, :]),
               inc=(kc == NCH - 1))
        if evac_dve:
            op("dve", [bPT0], [bXT], lambda e: e.tensor_copy(XT_[:, :], PT0[:, :]))
        else:
            op("act", [bPT0], [bXT], lambda e: e.activation(XT_[:, :], PT0[:, :], AF.Copy))

    def k_to_k2(Ksrc, bKsrc, K2t, bK2slot, slot):
        op("dve", [bKsrc, bGQ8], [bKD],
           lambda e: e.tensor_tensor(
               KD[:, :, :, :],
               Ksrc[:, :].rearrange("p (k d) -> p k d", k=2).unsqueeze(2).to_broadcast([n, 2, 2, HD]),
               GQ8[:, :].unsqueeze(1).unsqueeze(1).to_broadcast([n, 2, 2, HD]), ALU.mult))
        for k in range(2):
            op("pe", [bKD, bIDB], [bPT0],
               lambda e, k=k: e.transpose(PT0[:, k * 128:(k + 1) * 128],
                                          KD[:, k, :, :].rearrange("p a d -> p (a d)"), IDB[:, :]),
               inc=(k == 1))
        for par in range(2):
            op("act", [bPT0], [bK2slot],
               lambda e, par=par: e.activation(
                   K2t[par * 64:(par + 1) * 64, slot, :, par, :],
                   PT0[par * 64:(par + 1) * 64, 0:256].rearrange("p (k t) -> p k t", k=2), AF.Copy))

    BS0 = dict(T0=TA[0], SG=TA[1], XC=TA[2], TI=TA[3], LA=TA[4], A2=TA[5], XN=BA_[0], XT=BA_[1], XCb=BA_[2], HG=BA_[3],
               SM=(SMA, bSMA), X=X1[2], P0=(PA, bPA), P1=(PB, bPB), XB=XBS[0], XBn=XBS[1])
    BS1 = dict(T0=TB[0], SG=None, XC=TB[1], TI=TB[2], LA=TB[3], A2=X1[0], XN=BB_[0], XT=BB_[1], XCb=BB_[2], HG=None,
               SM=(SMB, bSMB), X=X1[1], P0=(PC, bPC), P1=(PC, bPC), XB=XBS[1], XBn=XBS[0])
    R32 = WinB[:, :, :].rearrange("p a b -> p (a b)").bitcast(F32)
    def r32(i, name):
        return (R32[:, i * D:(i + 1) * D], Buf(name))
    R_ = [r32(i, "R32_%d" % i) for i in range(6)]
    XB3 = (R32[:, 6 * D:6 * D + NCH * 131].rearrange("p (c t) -> p c t", c=NCH), Buf("XB3"))
    R_[1][1].parts = [Buf("R32_1_lo"), Buf("R32_1_hi")]
    bWinB.parts = [b for (_, b) in R_[:1]] + R_[1][1].parts + [b for (_, b) in R_[2:]] + [XB3[1], Buf("WinB_rest")]
    BS2 = dict(T0=R_[0], SG=None, XC=R_[1], TI=R_[2], LA=R_[3], A2=R_[4], XN=BB_[3], XT=BB_[4], XCb=BB_[5], HG=None,
               SM=(SMK, bSMK), X=R_[5], P0=(PB, bPB), P1=(PB, bPB), XB=XB3, XBn=XBS[0])
    BS0P = dict(BS0, P1=(PA, bPA))
    XBS3 = [XBS[0], XBS[1], XB3]
    op("dve", [], [XB3[1]], lambda e: e.memset(XB3[0][:, :, 0:3], 0.0))

    EVENTS = {}

    def phaseA(tp):
        nv = tp["nv"]
        bs = tp["bs"]
        lite, mask = tp["lite"], tp["mask"]
        Xt, bXt = tp.get("Xbuf") or bs["X"]
        XBt, bXB = tp.get("XB") or bs["XB"]
        P0, bP0 = bs["P0"]
        P1, bP1 = bs["P1"]
        SM_, bSM = bs["SM"]
        def load_x(tq):
            Xq, bXq = tq.get("Xbuf") or tq["bs"]["X"]
            if tq["nv"] < n:
                op("dve", [], [bXq], lambda e: e.memset(Xq[:, :], 0.0))
            S.dma("sp", Xq[:tq["nv"], :], tq["x_src"], [], [bXq])
            tq["preloaded"] = True
        if not tp.get("preloaded"):
            load_x(tp)
        nxt = tp.get("prefetch")
        if nxt is not None and nxt.get("Xbuf") is not None:
            load_x(nxt)
        if tp["hist_init"] is not None:
            tp["hist_init"](XBt, bXB)
        yield
        XT_, bXT = bs["XT"]
        g_ = rms_to_T(Xt, bXt, SM_, bSM, bs["XN"], bs["XT"])
        next(g_)
        if nxt is not None and nxt.get("Xbuf") is None and lite:
            load_x(nxt)
        yield
        for _ in g_:
            yield
        XT3 = v3(XT_)
        yield
        for oc in range(NCH):
            for kc in range(NCH):
                op("pe", [bXT, bWinA], [bP0],
                   lambda e, oc=oc, kc=kc: e.matmul(v3(P0)[:, oc, :], WinA[:, kc, oc * 128:(oc + 1) * 128], XT3[:, kc, :],
                                                    start=(kc == 0), stop=(kc == NCH - 1)),
                   inc=(oc == NCH - 1 and kc == NCH - 1))
            if oc % 2 == 1 and oc != NCH - 1:
                yield
        op("act", [bP0], [bXB], lambda e: e.activation(XBt[:, :, 3:3 + n], v3(P0), AF.Copy))
        XBn, bXBn = tp.get("XBn") or bs["XBn"]
        while tp.get("dep2") is not None and not EVENTS.get(("conv", tp["dep2"])):
            yield "blocked"
        if XBn is not XBt:
            op("act", [bXB], [bXBn], lambda e: e.activation(XBn[:, :, 0:3], XBt[:, :, n:n + 3], AF.Copy))
            EVENTS[("hist", tp["idx"])] = True
        yield
        while tp["dep"] is not None and not EVENTS.get(("hist", tp["dep"])):
            yield "blocked"
        if not lite:
            for oc in range(NCH):
                for kc in range(NCH):
                    op("pe", [bXT, bWinA2], [bP1],
                       lambda e, oc=oc, kc=kc: e.matmul(v3(P1)[:, oc, :], WinA[:, kc, D + oc * 128:D + (oc + 1) * 128],
                                                        XT3[:, kc, :], start=(kc == 0), stop=(kc == NCH - 1)),
                       inc=(oc == NCH - 1 and kc == NCH - 1))
                if oc % 2 == 1 and oc != NCH - 1:
                    yield
            TG, bTG = bs["T0"]
            SG, bSG = bs["SG"]
            op("act", [bP1], [bTG], lambda e: e.activation(TG[:, :], P1[:, :], AF.Tanh, scale=0.5))
            op("dve", [bTG, bP1], [bSG],
               lambda e: e.scalar_tensor_tensor(SG[:, :], TG[:, :], 1.0, P1[:, :], ALU.add, ALU.mult))
            yield
        XC, bXC = bs["XC"]
        XC3 = v3(XC)
        NDC = 5
        bXCl, bXCh = bXC.parts
        big = (not lite)
        DTMP = XBS[1][0][:, 0:NDC, 0:n]
        bDTMP = XBS[1][1]

        def wlo(col):
            return PVT[:, col:col + NDC].unsqueeze(2).to_broadcast([128, NDC, n])
        if big:
            op("dve", [bXB, bPVT], [bXCl],
               lambda e: e.tensor_tensor(XC3[:, 0:NDC, :], XBt[:, 0:NDC, 0:n], wlo(0), ALU.mult))
            op("dve", [bPVT, bXCl], [bXCl],
               lambda e: e.tensor_tensor(XC3[:, 0:NDC, :], XC3[:, 0:NDC, :], wlo(32), ALU.add))
        else:
            for c in range(NDC):
                op("dve", [bXB, bPVT], [bXCl],
                   lambda e, c=c: e.tensor_scalar(XC3[:, c, :], XBt[:, c, 0:n], PVT[:, c:c + 1], PVT[:, 32 + c:33 + c],
                                                  ALU.mult, ALU.add))
        yield

        def wbc(col):
            return PVT[:, col + NDC:col + NCH].unsqueeze(2).to_broadcast([128, NCH - NDC, n])
        op("pool", [bXB, bPVT], [bXCh],
           lambda e: e.tensor_tensor(XC3[:, NDC:NCH, :], XBt[:, NDC:NCH, 0:n], wbc(0), ALU.mult))
        op("pool", [bPVT, bXCh], [bXCh],
           lambda e: e.tensor_tensor(XC3[:, NDC:NCH, :], XC3[:, NDC:NCH, :], wbc(32), ALU.add))
        yield
        for k in range(1, 4):
            if big:
                op("dve", [bXB, bPVT], [bDTMP],
                   lambda e, k=k: e.tensor_tensor(DTMP, XBt[:, 0:NDC, k:k + n], wlo(k * 8), ALU.mult))
                op("dve", [bDTMP, bXCl], [bXCl],
                   lambda e: e.tensor_tensor(XC3[:, 0:NDC, :], XC3[:, 0:NDC, :], DTMP, ALU.add))
                yield
            else:
                for c in range(NDC):
                    op("dve", [bXB, bPVT, bXCl], [bXCl],
                       lambda e, c=c, k=k: e.scalar_tensor_tensor(
                           XC3[:, c, :], XBt[:, c, k:k + n], PVT[:, k * 8 + c:k * 8 + c + 1], XC3[:, c, :], ALU.mult, ALU.add))
                    if c == 2:
                        yield
            op("pool", [bXB, bPVT], [bCTMP],
               lambda e, k=k: e.tensor_tensor(CTMP[:, :, :], XBt[:, NDC:NCH, k:k + n], wbc(k * 8), ALU.mult))
            op("pool", [bCTMP, bXCh], [bXCh],
               lambda e: e.tensor_tensor(XC3[:, NDC:NCH, :], XC3[:, NDC:NCH, :], CTMP[:, :, :], ALU.add))
            yield
        EVENTS[("conv", tp["idx"])] = True
        so = tp["state_out"]
        if so is not None:
            CSt, bCS = so["stage"]
            op("act", [bXB], [bCS], lambda e: e.activation(CSt[:, :, 0:3], XBt[:, :, nv:nv + 3], AF.Copy))
            for j in range(3):
                S.dma("sp", so["conv"][j].rearrange("(c p) -> p c", p=128), CSt[:, :, j],
                      [bCS], [], allow_slow_non_contiguous=True)
        if XBn is XBt:
            op("act", [bXB], [bXB], lambda e: e.activation(XBt[:, :, 0:3], XBt[:, :, n:n + 3], AF.Copy))
        XCb, bXCb = bs["XCb"]
        op("act", [bXC], [bXCb], lambda e: e.activation(XCb[:, :], XC[:, :], AF.Copy))
        XCb3 = v3(XCb)
        yield
        TR, bTR = bs["T0"]
        TI, bTI = bs["TI"]
        LA, bLA = bs["LA"]
        A2, bA2 = bs["A2"]
        for (P_, bP, Wg, bWg, TT, bTT, BH, bBH) in ((P0, bP0, WGA, bWGA, TR, bTR, BAH, bBAH),
                                                   (P1, bP1, WGX, bWGX, TI, bTI, BXH, bBXH)):
            for blk in range(4):
                for oc in range(2):
                    for kc in range(2):
                        last = (blk == 3 and oc == 1 and kc == 1)
                        op("pe", [bXCb, bWg], [bP],
                           lambda e, P_=P_, Wg=Wg, blk=blk, oc=oc, kc=kc: e.matmul(
                               v3(P_)[:, blk * 2 + oc, :], Wg[:, blk * 2 + kc, oc * 128:(oc + 1) * 128],
                               XCb3[:, blk * 2 + kc, :], start=(kc == 0), stop=(kc == 1)),
                           inc=last)
                if blk == 1:
                    yield
            yield
            for c in range(NCH):
                op("act", [bP, bBH], [bTT],
                   lambda e, c=c, TT=TT, P_=P_, BH=BH: e.activation(v3(TT)[:, c, :], v3(P_)[:, c, :], AF.Tanh,
                                                                    bias=BH[:, c:c + 1], scale=0.5))
                if c == 3:
                    yield
            yield
        op("dve", [bTR, bCH], [bLA],
           lambda e: e.scalar_tensor_tensor(v3(LA), v3(TR), 1.0, CH[:, :].unsqueeze(2).to_broadcast([128, NCH, n]),
                                            ALU.add, ALU.mult))
        yield
        op("act", [bLA], [bA2], lambda e: e.activation(A2[:, :], LA[:, :], AF.Exp, scale=2.0))
        yield
        op("act", [bLA], [bLA], lambda e: e.activation(LA[:, :], LA[:, :], AF.Exp))
        yield
        op("act", [bA2, bCONSTS], [bA2],
           lambda e: e.activation(A2[:, :], A2[:, :], AF.Sqrt, bias=CONSTS[:, 2:3], scale=-0.25))
        yield
        op("dve", [bTI, bXC], [bTI],
           lambda e: e.scalar_tensor_tensor(TI[:, :], TI[:, :], 1.0, XC[:, :], ALU.add, ALU.mult))
        yield
        if mask:
            op("dve", [bA2, bTI, bKM], [bTI],
               lambda e: e.scalar_tensor_tensor(TI[:, :], A2[:, :], KM[:, 2:3], TI[:, :], ALU.mult, ALU.mult))
        else:
            op("pool", [bA2, bTI], [bTI], lambda e: e.tensor_tensor(TI[:, :], A2[:, :], TI[:, :], ALU.mult))
        yield
        while tp["dep"] is not None and not EVENTS.get(("carry", tp["dep"])):
            yield "blocked"
        Hp_fn, bHp = tp["Hprev"]
        Hv = v3(TI)
        for c in range(NCH):
            op("dve", [bLA, bTI, bHp], [bTI],
               lambda e, c=c: e.tensor_tensor_scan(Hv[:, c, :], v3(LA)[:, c, :], Hv[:, c, :],
                                                   Hp_fn(c), ALU.mult, ALU.add))
            if c % 2 == 1:
                yield
        if so is not None:
            CSt, bCS = so["stage"]
            op("act", [bTI], [bCS], lambda e: e.activation(CSt[:, :, 3:4], Hv[:, :, nv - 1:nv], AF.Copy))
            S.dma("sp", so["h"].rearrange("(c p) -> p c", p=128), CSt[:, :, 3],
                  [bCS], [], allow_slow_non_contiguous=True)
        if tp["carry"]:
            op("dve", [bTI], [bHC], lambda e: e.tensor_copy(HC[:, :], Hv[:, :, n - 1]))
        EVENTS[("carry", tp["idx"])] = True
        if lite:
            return
        X1c, bX1c = tp["X1"]
        HG, bHG = bs["HG"]
        SG, bSG = bs["SG"]
        op("dve", [bTI, bSG], [bHG],
           lambda e: e.scalar_tensor_tensor(v3(HG), Hv, 0.5, v3(SG), ALU.mult, ALU.mult))
        HG3 = v3(HG)
        yield
        for half in range(2):
            for kc in range(NCH):
                op("pe", [bHG, bWoutA], [bP0],
                   lambda e, half=half, kc=kc: e.matmul(P0[:, half * 512:(half + 1) * 512], HG3[:, kc, :],
                                                        WoutA[:, kc, half * 512:(half + 1) * 512],
                                                        start=(kc == 0), stop=(kc == NCH - 1)),
                   inc=(half == 1 and kc == NCH - 1))
            yield
        op("dve", [bXt, bP0], [bX1c], lambda e: e.tensor_tensor(X1c[:, :], Xt[:, :], P0[:, :], ALU.add))
        yield

    def phaseB(tp):
        nv = tp["nv"]
        X1c, bX1c = tp["X1"]
        K2t, bK2l, VAt, bVAl = tp["K2"], tp["bK2"], tp["VA"], tp["bVA"]
        cur, prev, kmcol = tp["cur"], tp["prev"], tp["kmcol"]
        so = tp["state_out"]
        yield from rms_to_T(X1c, bX1c, SMB, bSMB, BB_[0], BB_[1], evac_dve=True)
        X1T3 = v3(BB_[1][0])
        bX1T = BB_[1][1]
        yield
        for kc in range(NCH):
            op("pe", [bX1T, bWkv], [bPD],
               lambda e, kc=kc: e.matmul(PD[:, 0:256], X1T3[:, kc, :], Wkv[:, kc, :], start=(kc == 0), stop=(kc == NCH - 1)),
               inc=(kc == NCH - 1))
        Ksq, bKsq = TB[0]
        op("dve", [bPD], [bKsq], lambda e: e.tensor_copy(Ksq[:, 0:256], PD[:, 0:256]))
        op("dve", [bKsq], [bKsq], lambda e: e.tensor_tensor(Ksq[:, 256:384], Ksq[:, 0:128], Ksq[:, 0:128], ALU.mult))
        op("dve", [bKsq], [bSMK],
           lambda e: e.tensor_reduce(SMK[:, 0:2], Ksq[:, 256:384].rearrange("p (k d) -> p k d", k=2), AX.X, ALU.add))
        op("dve", [bSMK], [bSMK],
           lambda e: e.tensor_scalar(SMK[:, 0:2], SMK[:, 0:2], 1.0 / HD, EPS, ALU.mult, ALU.add))
        op("pool", [bSMK, bCONSTS], [bSMK],
           lambda e: e.tensor_tensor(SMK[:, 2:4], SMK[:, 0:2], CONSTS[:, 1:2].to_broadcast([n, 2]), ALU.pow))
        op("dve", [bKsq, bSMK], [bKsq],
           lambda e: e.tensor_tensor(Ksq[:, 384:512].rearrange("p (k d) -> p k d", k=2),
                                     Ksq[:, 0:128].rearrange("p (k d) -> p k d", k=2),
                                     SMK[:, 2:4].unsqueeze(2).to_broadcast([n, 2, HD]), ALU.mult))
        op("dve", [bKsq, bGK], [bKOUT],
           lambda e: e.tensor_tensor(KOUT[:, :].rearrange("p (k d) -> p k d", k=2),
                                     Ksq[:, 384:512].rearrange("p (k d) -> p k d", k=2),
                                     GK[:, :].unsqueeze(1).to_broadcast([n, 2, HD]), ALU.mult))
        yield

        def k_finish():
            k_to_k2(KOUT, bKOUT, K2t, bK2l[cur], cur)
            op("dve", [bKsq, bKM], [bVAl[cur]],
               lambda e: e.tensor_scalar(VAt[:, cur, :, 0:HD], Ksq[:, 128:256].rearrange("p (k d) -> p k d", k=2),
                                         KM[:, kmcol:kmcol + 1], None, ALU.mult))
            op("dve", [bKM], [bVAl[cur]],
               lambda e: e.tensor_copy(VAt[:, cur, :, HD:HD + 1], KM[:, kmcol:kmcol + 1].unsqueeze(1).to_broadcast([n, 2, 1])))
            if so is not None:
                op("dve", [bKsq], [bVOUT], lambda e: e.tensor_copy(VOUT[:, :], Ksq[:, 128:256]))
                S.dma("sp", so["k"], KOUT[:nv, :], [bKOUT], [])
                S.dma("sp", so["v"], VOUT[:nv, :], [bVOUT], [])
        if tp["kvonly"]:
            k_finish()
            yield
            return
        for oc in range(NCH):
            for kc in range(NCH):
                op("pe", [bX1T, bWinB], [bPC, bPC0, bPC1],
                   lambda e, oc=oc, kc=kc: e.matmul(v3(PC)[:, oc, :], WinB[:, kc, oc * 128:(oc + 1) * 128], X1T3[:, kc, :],
                                                    start=(kc == 0), stop=(kc == NCH - 1)),
                   inc=(oc == NCH - 1 and kc == NCH - 1))
            if oc % 2 == 1:
                yield
        SQ, bSQ = BB_[0]
        op("act", [bPC], [bSQ], lambda e: e.activation(SQ[:, :], PC[:, :], AF.Square))
        yield
        k_finish()
        yield
        RQ, bRQ = TB[0]
        for c0 in range(0, NCH, 4):
            op("pe", [bSQ, bBON], [bPD],
               lambda e, c0=c0: e.matmul(PD[:, :], BON[:, :], SQ[:, c0 * 128:(c0 + 4) * 128], start=True, stop=True))
            op("act", [bPD, bEPS], [bRQ],
               lambda e, c0=c0: e.activation(RQ[:, c0 * 128:(c0 + 4) * 128], PD[:, :], AF.Ln, bias=EPS_T[:, 0:1]))
        yield
        op("act", [bRQ], [bRQ], lambda e: e.activation(RQ[:, :], RQ[:, :], AF.Exp, scale=-0.5))
        yield
        QN, bQN = BB_[2]
        op("dve", [bPC, bRQ], [bQN], lambda e: e.tensor_tensor(QN[:, :], PC[:, :], RQ[:, :], ALU.mult))
        yield
        for half in range(2):
            for kc in range(NCH):
                op("pe", [bX1T, bWinB], [bPC, bPC0, bPC1],
                   lambda e, half=half, kc=kc: e.matmul(PC[:, half * 512:(half + 1) * 512], X1T3[:, kc, :],
                                                        WinB[:, kc, D + half * 512:D + (half + 1) * 512],
                                                        start=(kc == 0), stop=(kc == NCH - 1)),
                   inc=(half == 1 and kc == NCH - 1))
            yield
        SGB, bSGB = TB[1]
        op("act", [bPC], [bSGB], lambda e: e.activation(SGB[:, :], PC[:, :], AF.Tanh, scale=0.5))
        op("dve", [bSGB, bPC], [bSGB, bPC0, bPC1],
           lambda e: e.scalar_tensor_tensor(SGB[:, :], SGB[:, :], 1.0, PC[:, :], ALU.add, ALU.mult))
        yield
        OGb, bOGb = BB_[3]
        QN3 = v3(QN)
        for step in range(4):
            kvh, hf = step // 2, step % 2
            PeT, bPe = TB[2 + step % 2]
            PTt, bPTt = BB_[4 + step % 2]
            SMO_, bSMO = SMO[step % 2]
            blocks = ((prev, 0, bPC0), (cur, 1, bPC1))
            for (slot, bank, bBank) in blocks:
                for j in range(4):
                    hh = kvh * 8 + hf * 4 + j
                    oc, par = hh // 2, hh % 2
                    op("pe", [bK2l[slot], bQN], [bBank],
                       lambda e, slot=slot, bank=bank, j=j, oc=oc, par=par, kvh=kvh: e.matmul(
                           PC[:, bank * 512 + j * 128:bank * 512 + (j + 1) * 128], K2t[:, slot, kvh, par, :],
                           QN3[:, oc, :], start=True, stop=True),
                       inc=(j == 3))
            op("act", [bPC0, bPC1, bMNEG], [bPe],
               lambda e, PeT=PeT: e.activation(PeT[:, :], PC[:, :], AF.Exp, bias=MNEG[:, 0:1]))
            h0 = kvh * 8 + hf * 4
            op("pool", [bPe, bEAB], [bPTt],
               lambda e, PeT=PeT, PTt=PTt, h0=h0: e.tensor_tensor(
                   PTt[:, :].rearrange("p (b j q) -> p b j q", b=2, j=4),
                   PeT[:, :].rearrange("p (b j q) -> p b j q", b=2, j=4),
                   EAB[:, :, h0:h0 + 4, :], ALU.mult))
            yield
            for j in range(4):
                op("pe", [bPTt, bVAl[prev]], [bPD],
                   lambda e, j=j, PTt=PTt, kvh=kvh: e.matmul(PD[:, j * 65:(j + 1) * 65], PTt[:, j * 128:(j + 1) * 128],
                                                             VAt[:, prev, kvh, :], start=True, stop=False), inc=False)
                op("pe", [bPTt, bVAl[cur]], [bPD],
                   lambda e, j=j, PTt=PTt, kvh=kvh: e.matmul(PD[:, j * 65:(j + 1) * 65], PTt[:, 512 + j * 128:512 + (j + 1) * 128],
                                                             VAt[:, cur, kvh, :], start=False, stop=True), inc=(j == 3))
            yield
            O3 = PD[:, 0:260].rearrange("p (j e) -> p j e", e=65)
            h0 = kvh * 8 + hf * 4
            op("dve", [bPD, bSINKE], [bSMO],
               lambda e, O3=O3, SMO_=SMO_, h0=h0: e.tensor_tensor(
                   SMO_[:, 0:4].unsqueeze(2), O3[:, :, 64:65], SINKE[:, h0:h0 + 4].unsqueeze(2), ALU.add))
            op("dve", [bSMO], [bSMO], lambda e, SMO_=SMO_: e.reciprocal(SMO_[:, 4:8], SMO_[:, 0:4]))
            TMPO, bTMPO = TB[0]
            op("dve", [bPD, bSMO], [bTMPO],
               lambda e, O3=O3, SMO_=SMO_, TMPO=TMPO, step=step: e.tensor_tensor(
                   TMPO[:, step * 256:(step + 1) * 256].rearrange("p (j d) -> p j d", j=4), O3[:, :, 0:64],
                   SMO_[:, 4:8].unsqueeze(2).to_broadcast([n, 4, HD]), ALU.mult))
            op("dve", [bTMPO, bSGB], [bOGb],
               lambda e, TMPO=TMPO, step=step: e.scalar_tensor_tensor(
                   OGb[:, step * 256:(step + 1) * 256], TMPO[:, step * 256:(step + 1) * 256], 0.5,
                   SGB[:, step * 256:(step + 1) * 256], ALU.mult, ALU.mult))
            yield
        for kc in range(NCH):
            op("pe", [bOGb, bIDB], [bPT0],
               lambda e, kc=kc: e.transpose(PT0[:, kc * 128:(kc + 1) * 128], OGb[:, kc * 128:(kc + 1) * 128], IDB[:, :]),
               inc=(kc == NCH - 1))
        OGT, bOGT = BB_[1]
        op("act", [bPT0], [bOGT], lambda e: e.activation(OGT[:, :], PT0[:, :], AF.Copy))
        OGT3 = v3(OGT)
        yield
        for half in range(2):
            for kc in range(NCH):
                op("pe", [bOGT, bWoutB], [bPC, bPC0, bPC1],
                   lambda e, half=half, kc=kc: e.matmul(PC[:, half * 512:(half + 1) * 512], OGT3[:, kc, :],
                                                        WoutB[:, kc, half * 512:(half + 1) * 512],
                                                        start=(kc == 0), stop=(kc == NCH - 1)),
                   inc=(half == 1 and kc == NCH - 1))
            yield
        op("dve", [bX1c, bPC], [bX1c, bPC0, bPC1], lambda e: e.tensor_tensor(X1c[:, :], X1c[:, :], PC[:, :], ALU.add))
        S.dma("sp", tp["y_dst"], X1c[:nv, :], [bX1c], [])
        yield

    HPREV = ((lambda c: HC[:, c:c + 1]), bHC)
    pre_tiles = []
    for t in range(NPRE):
        pre_tiles.append(dict(
            nv=128, x_src=xpre[t * 128:(t + 1) * 128, :], y_dst=None, Hprev=HPREV, X1=None, carry=True,
            bs=(BS0P, BS1, BS2)[(t - NPRE) % 3], XB=XBS3[(t - NPRE) % 3], XBn=XBS3[(t + 1 - NPRE) % 3],
            idx=t, dep=(t - 1 if t > 0 else None), dep2=(t - 2 if t > 1 else None), Xbuf=None,
            hist_init=None, state_out=None, lite=True, mask=True, kvonly=True))
    tiles = []
    for j in range(NTH + 1):
        cur = j % 2
        prev = 1 - cur
        halo = (j == 0)
        tiles.append(dict(
            nv=128, x_src=xp[j * 128:(j + 1) * 128, :], y_dst=(None if halo else yp[(j - 1) * 128:j * 128, :]),
            Hprev=HPREV, X1=X1[j % 3], Xbuf=X1[j % 3], carry=True, bs=BS0, XBn=XBS[0], idx=NPRE + j, dep=None,
            K2=K2, bK2=bK2s, VA=VA, bVA=bVAs, cur=cur, prev=prev, kmcol=(2 if halo else 0), hist_init=None,
            state_out=({"conv": pconv, "h": ph, "k": pk, "v": pv, "stage": CST[0]} if j == NTH else None),
            lite=False, mask=halo, kvonly=halo))
    sc = (NTH + 1) % 2

    def sample_hist(XBt, bXB):
        op("dve", [bPVT], [bXB],
           lambda e: e.tensor_copy(XBt[:, :, 0:3], PVT[:, 64:88].rearrange("p (j c) -> p c j", j=3)))
    tiles.append(dict(
        nv=NS, x_src=xs[:, :], y_dst=ys[:, :],
        Hprev=((lambda c: PVT[:, 88 + c:89 + c]), bPVT), X1=X1[(NTH + 1) % 3], Xbuf=X1[(NTH + 1) % 3], carry=False,
        bs=BS0, XBn=XBS[0], idx=-1, dep=None,
        K2=K2S, bK2=bK2Ss, VA=VAS, bVA=bVASs, cur=1, prev=0, kmcol=1, hist_init=sample_hist,
        state_out={"conv": sconv, "h": sh, "k": sk[128 - NS:128, :], "v": sv[128 - NS:128, :], "stage": CST[1]},
        lite=False, mask=False, kvonly=False))

    for t in range(NPRE):
        if t + 3 < NPRE:
            pre_tiles[t]["prefetch"] = pre_tiles[t + 3]
    for j in range(len(tiles) - 1):
        tiles[j]["prefetch"] = tiles[j + 1]

    CKt, bCK = TB[3]
    S.dma("sp", CKt[:, 0:128], ck, [], [bCK])
    S.dma("sp", CKt[:, 128:256], cv, [], [bCK])
    op("dve", [bCK], [bKOUT], lambda e: e.tensor_copy(KOUT[:, :], CKt[:, 0:128]))
    k_to_k2(KOUT, bKOUT, K2S, bK2Ss[0], 0)
    op("dve", [bCK], [bVASs[0]],
       lambda e: e.tensor_copy(VAS[:, 0, :, 0:HD], CKt[:, 128:256].rearrange("p (k d) -> p k d", k=2)))
    op("dve", [], [bVASs[0]], lambda e: e.memset(VAS[:, 0, :, HD:HD + 1], 1.0))
    S.dma("sp", sk[0:128 - NS, :], ck[NS:128, :], [], [])
    S.dma("sp", sv[0:128 - NS, :], cv[NS:128, :], [], [])

    def advance(g):
        S.step_fin = 0.0
        try:
            r = next(g)
        except StopIteration:
            return None
        if r == "blocked":
            return -1.0
        return S.step_fin

    def run_streams(gens):
        gens = list(gens)
        times = [0.0 for _ in gens]
        while gens:
            j = min(range(len(gens)), key=lambda q: times[q])
            r = advance(gens[j])
            if r is None:
                gens.pop(j)
                times.pop(j)
            elif r > 0.0:
                times[j] = r

    pend = [phaseA(tp_) for tp_ in pre_tiles]
    active = [deferred_loads()]
    times = [0.0]
    nlite = 0
    NLITE = _NLITE
    while active:
        while nlite < NLITE and pend:
            active.append(pend.pop(0))
            times.append(min(times) if times else 0.0)
            nlite += 1
        j = min(range(len(active)), key=lambda q: times[q])
        r = advance(active[j])
        if r is None:
            if j != 0 or len(active) == 1 or True:
                pass
            g = active.pop(j)
            times.pop(j)
            if getattr(g, "gi_code", None) is not None and g.gi_code.co_name == "phaseA":
                nlite -= 1
        elif r < 0.0:
            times[j] = max(times) + 1.0
        elif r > 0.0:
            times[j] = r

    run_streams([phaseA(tiles[0]), deferred_winb()])
    for i in range(len(tiles)):
        gens = [phaseB(tiles[i])]
        if i + 1 < len(tiles):
            gens.append(phaseA(tiles[i + 1]))
        run_streams(gens)
    S.finish()
    es.close()
    return nc, S


_CACHE = {}


def _consts():
    ident = np.eye(128, dtype=np.float32)
    bones = np.zeros((128, 128), np.float32)
    bones[:64, :64] = 1.0 / 64
    bones[64:, 64:] = 1.0 / 64
    key = np.arange(128)[:, None]
    q = np.arange(128)[None, :]
    dA = (q + 128 - key).astype(np.float32)
    mA = ((key // 64 == 1) | (q // 64 == 0)).astype(np.float32)
    dB = np.abs(q - key).astype(np.float32)
    mB = ((key // 64 == 0) | (q // 64 == 1)).astype(np.float32)
    return ident, bones, dA * mA, dB * mB, mA, mB


def kernel(x_prompt, x_sample, cache_k, cache_v, state_conv, state_rglru,
           a_norm, a_w_in, a_conv_w, a_conv_b, a_gate_a_w, a_gate_a_b, a_gate_x_w, a_gate_x_b,
           a_lambda, a_w_out, kv_norm, w_kv, k_norm, b_norm, b_w_in, q_norm, sinks, b_w_out, _seq=None):
    f = lambda a: np.ascontiguousarray(np.asarray(a, dtype=np.float32))
    x_prompt = f(x_prompt)
    seq = x_prompt.shape[1]
    if seq not in _CACHE:
        _CACHE[seq] = build(seq)[0]
    nc = _CACHE[seq]
    ident, bones, dA, dB, mA, mB = _consts()
    km = np.ones((128, 3), np.float32)
    km[NS:, 1] = 0.0
    nth = seq // 256
    npre = nth - 1
    x_sample = f(x_sample); cache_k = f(cache_k); cache_v = f(cache_v)
    state_conv = f(state_conv); state_rglru = f(state_rglru)
    r8 = lambda v: f(v).reshape(-1, 8, 128).reshape(-1, 128)
    n_cores = 8
    BP = x_prompt.shape[0]
    in_maps = []
    for c in range(n_cores):
        b = c % BP
        half = c // BP
        zt = np.zeros((128, D), np.float32)
        if half == 0:
            xpre_c = np.zeros((max(npre, 1) * 128, D), np.float32)
            xmain_c = np.concatenate([zt, x_prompt[b, :nth * 128]], 0)
        else:
            xpre_c = x_prompt[b, :npre * 128] if npre > 0 else zt
            xmain_c = x_prompt[b, npre * 128:]
        xpre_c = np.ascontiguousarray(xpre_c)
        xmain_c = np.ascontiguousarray(xmain_c)
        pvec = np.concatenate([
            r8(f(a_conv_w)[0]),
            r8(f(a_conv_b)[0][None]), r8(f(a_gate_a_b)[0][None]), r8(f(a_gate_x_b)[0][None]), r8(f(a_lambda)[0][None]),
            r8(state_conv[0, c]),
            r8(state_rglru[0, c][None]),
            r8(f(a_norm)[0][None]), r8(f(kv_norm)[None]), r8(f(b_norm)[0][None]),
        ], axis=0)
        assert pvec.shape == (120, 128)
        in_maps.append({
            "xpre": xpre_c, "xp": xmain_c, "flag": np.array([float(half)], np.float32), "xs": x_sample[c],
            "ck": cache_k[c].reshape(128, 128), "cv": cache_v[c].reshape(128, 128),
            "pvec": np.ascontiguousarray(pvec),
            "w_in_a": f(a_w_in)[0], "wga": f(a_gate_a_w)[0].reshape(D, 256), "wgx": f(a_gate_x_w)[0].reshape(D, 256),
            "w_out_a": f(a_w_out)[0], "w_kv": f(w_kv), "w_in_b": f(b_w_in)[0], "w_out_b": f(b_w_out)[0],
            "knorm": f(k_norm), "qnorm": f(q_norm)[0], "sinks": f(sinks)[0],
            "ident": ident, "bones": bones, "distA": dA, "distB": dB, "maskA": mA, "maskB": mB, "kmask": km,
        })
    res = run_bass_kernel_spmd(nc, in_maps, core_ids=list(range(n_cores)))
    R = res.results
    g = lambda c, k: np.asarray(R[c][k], dtype=np.float32)
    y_prompt = np.stack([np.concatenate([g(b, "yp"), g(b + BP, "yp")], 0) for b in range(BP)], 0)
    y_sample = np.stack([g(c, "ys") for c in range(n_cores)], 0)
    p_k = np.stack([g(b + BP, "pk").reshape(128, 2, HD) for b in range(BP)], 0)
    p_v = np.stack([g(b + BP, "pv").reshape(128, 2, HD) for b in range(BP)], 0)
    p_conv = np.stack([g(b + BP, "pconv") for b in range(BP)], 0)[None]
    p_h = np.stack([g(b + BP, "ph") for b in range(BP)], 0)[None]
    s_k = np.stack([g(c, "sk").reshape(128, 2, HD) for c in range(n_cores)], 0)
    s_v = np.stack([g(c, "sv").reshape(128, 2, HD) for c in range(n_cores)], 0)
    s_conv = np.stack([g(c, "sconv") for c in range(n_cores)], 0)[None]
    s_h = np.stack([g(c, "sh") for c in range(n_cores)], 0)[None]
    return (y_prompt, y_sample, p_k, p_v, p_conv, p_h, s_k, s_v, s_conv, s_h)
```
